# Optimizing a Trainium2 kernel written in Bass

```python
import math
import jax, jax.numpy as jnp
from jax import lax
import numpy as np

D_MODEL = 2048
BATCH = 4
SEQ = 4096
DEPTH = 1

D_FF = 5632
ATT_HEADS = 8
ATT_HEAD_DIM = 128
ATT_V_DIM = 2 * ATT_HEAD_DIM
ATT_QK_WIDTH = ATT_HEADS * 2 * ATT_HEAD_DIM
ATT_V_WIDTH = ATT_HEADS * ATT_V_DIM
Q_BLOCK = 128
SSM_EXPAND = 2
SSM_D_INNER = SSM_EXPAND * D_MODEL
SSM_HEAD_DIM = 64
SSM_HEADS = SSM_D_INNER // SSM_HEAD_DIM
SSM_GROUPS = 8
SSM_D_STATE = 128
SSM_CONV = 4
SSM_CHUNK = 128
SSM_BC_WIDTH = SSM_GROUPS * SSM_D_STATE
SSM_CONV_DIM = SSM_D_INNER + 2 * SSM_BC_WIDTH
DT_MIN = 0.001
DT_MAX = 0.1
N_BRANCH = 2
IN_PROJ_WIDTH = 2 * ATT_QK_WIDTH + ATT_V_WIDTH + SSM_D_INNER + SSM_CONV_DIM + SSM_HEADS + N_BRANCH * D_MODEL
RMS_EPS = 1e-6

kernel_name = "hybrid_diffattn_mamba2_macaron"


def rms_norm(x, g, eps=RMS_EPS):
    xf = x.astype(jnp.float32)
    y = xf * lax.rsqrt(jnp.mean(xf * xf, axis=-1, keepdims=True) + eps)
    return (y * g.astype(jnp.float32)).astype(x.dtype)


def swiglu(x, w_gate, w_up, w_down):
    return (jax.nn.silu(x @ w_gate) * (x @ w_up)) @ w_down


def diff_attention(q, k, v, lam, subln_g, lambda_init):
    b, s = q.shape[0], q.shape[1]
    q = q * (ATT_HEAD_DIM ** -0.5)
    outs = []
    for i in range(s // Q_BLOCK):
        q0 = i * Q_BLOCK
        kv_end = q0 + Q_BLOCK
        qb = q[:, q0:kv_end]
        kb = k[:, :kv_end]
        vb = v[:, :kv_end]
        scores = jnp.einsum('bqhmd,bkhmd->bhmqk', qb, kb).astype(jnp.float32)
        q_pos = q0 + jnp.arange(Q_BLOCK)
        k_pos = jnp.arange(kv_end)
        causal = k_pos[None, :] <= q_pos[:, None]
        scores = jnp.where(causal, scores, -jnp.inf)
        probs = jax.nn.softmax(scores, axis=-1)
        diff = probs[:, :, 0] - lam * probs[:, :, 1]
        outs.append(jnp.einsum('bhqk,bkhv->bqhv', diff.astype(vb.dtype), vb))
    o = jnp.concatenate(outs, axis=1)
    o = rms_norm(o, subln_g) * (1.0 - lambda_init)
    return o.reshape(b, s, ATT_V_WIDTH)


def causal_depthwise_conv(x, w, bias):
    c = x.shape[-1]
    y = lax.conv_general_dilated(x, w[:, None, :].astype(x.dtype), window_strides=(1,),
                                 padding=[(SSM_CONV - 1, 0)],
                                 dimension_numbers=('NWC', 'WIO', 'NWC'),
                                 feature_group_count=c)
    return y + bias


def segsum_exp(a_cs):
    n = a_cs.shape[-1]
    mask = jnp.tril(jnp.ones((n, n), dtype=bool))
    diff = a_cs[..., :, None] - a_cs[..., None, :]
    return jnp.exp(jnp.where(mask, diff, -jnp.inf))


def ssd_chunked(x, dt, a, bmat, cmat, d_skip):
    b, s, h, p = x.shape
    g, n = bmat.shape[-2], bmat.shape[-1]
    e = h // g
    c, l = s // SSM_CHUNK, SSM_CHUNK
    xf = x.astype(jnp.float32)
    xdt = (xf * dt[..., None]).reshape(b, c, l, g, e, p)
    a_cs = jnp.cumsum((dt * a).reshape(b, c, l, g, e).transpose(0, 1, 3, 4, 2), axis=-1)
    bc = bmat.astype(jnp.float32).reshape(b, c, l, g, n)
    cc = cmat.astype(jnp.float32).reshape(b, c, l, g, n)
    cb = jnp.einsum('bclgn,bcsgn->bcgls', cc, bc)
    m = cb[:, :, :, None] * segsum_exp(a_cs)
    y_diag = jnp.einsum('bcgels,bcsgep->bclgep', m, xdt)
    decay_to_end = jnp.exp(a_cs[..., -1:] - a_cs).transpose(0, 1, 4, 2, 3)
    chunk_states = jnp.einsum('bclgn,bclgep->bcgepn', bc, xdt * decay_to_end[..., None])
    chunk_decay = jnp.exp(a_cs[..., -1])

    def step(state, inp):
        st_c, dec_c = inp
        return state * dec_c[..., None, None] + st_c, state

    init = jnp.zeros((b, g, e, p, n), jnp.float32)
    _, prev = lax.scan(step, init, (chunk_states.transpose(1, 0, 2, 3, 4, 5),
                                    chunk_decay.transpose(1, 0, 2, 3)))
    prev = prev.transpose(1, 0, 2, 3, 4, 5)
    decay_in = jnp.exp(a_cs).transpose(0, 1, 4, 2, 3)
    y_off = jnp.einsum('bclgn,bcgepn->bclgep', cc, prev) * decay_in[..., None]
    y = (y_diag + y_off).reshape(b, s, h, p) + xf * d_skip.astype(jnp.float32)[:, None]
    return y


def mamba2_mixer(z, xbc, dt_raw, conv_w, conv_b, dt_bias, a_log, d_skip, norm_g):
    b, s = z.shape[0], z.shape[1]
    xbc = jax.nn.silu(causal_depthwise_conv(xbc, conv_w, conv_b))
    xs, bs, cs = jnp.split(xbc, [SSM_D_INNER, SSM_D_INNER + SSM_BC_WIDTH], axis=-1)
    xs = xs.reshape(b, s, SSM_HEADS, SSM_HEAD_DIM)
    bs = bs.reshape(b, s, SSM_GROUPS, SSM_D_STATE)
    cs = cs.reshape(b, s, SSM_GROUPS, SSM_D_STATE)
    dt = jax.nn.softplus(dt_raw.astype(jnp.float32) + dt_bias.astype(jnp.float32))
    a = -jnp.exp(a_log.astype(jnp.float32))
    y = ssd_chunked(xs, dt, a, bs, cs, d_skip)
    y = y.reshape(b, s, SSM_D_INNER) * jax.nn.silu(z.astype(jnp.float32))
    y = rms_norm(y.reshape(b, s, SSM_GROUPS, SSM_D_INNER // SSM_GROUPS),
                 norm_g.reshape(SSM_GROUPS, SSM_D_INNER // SSM_GROUPS))
    return y.reshape(b, s, SSM_D_INNER).astype(z.dtype)


def setup_inputs(seed: int = 0) -> dict:
    key = jax.random.key(seed)
    ks = jax.random.split(key, 32)
    f32 = jnp.float32

    def normal(k, shape, scale):
        return jax.random.normal(k, shape, f32) * scale

    def gain(k, dim):
        return 1.0 + 0.02 * jax.random.normal(k, (DEPTH, dim), f32)

    dt0 = jnp.exp(jax.random.uniform(ks[20], (DEPTH, SSM_HEADS), f32, math.log(DT_MIN), math.log(DT_MAX)))
    return {
        "x": jax.random.normal(ks[0], (BATCH, SEQ, D_MODEL), f32),
        "ffn1_pre_g": gain(ks[1], D_MODEL),
        "ffn1_w_gate": normal(ks[2], (DEPTH, D_MODEL, D_FF), D_MODEL ** -0.5),
        "ffn1_w_up": normal(ks[3], (DEPTH, D_MODEL, D_FF), D_MODEL ** -0.5),
        "ffn1_w_down": normal(ks[4], (DEPTH, D_FF, D_MODEL), D_FF ** -0.5),
        "ffn1_post_g": gain(ks[5], D_MODEL),
        "mix_pre_g": gain(ks[6], D_MODEL),
        "w_in": normal(ks[7], (DEPTH, D_MODEL, IN_PROJ_WIDTH), D_MODEL ** -0.5),
        "att_lambda_q1": normal(ks[8], (DEPTH, ATT_HEAD_DIM), 0.1),
        "att_lambda_k1": normal(ks[9], (DEPTH, ATT_HEAD_DIM), 0.1),
        "att_lambda_q2": normal(ks[10], (DEPTH, ATT_HEAD_DIM), 0.1),
        "att_lambda_k2": normal(ks[11], (DEPTH, ATT_HEAD_DIM), 0.1),
        "att_subln_g": gain(ks[12], ATT_V_DIM),
        "ssm_conv_w": normal(ks[13], (DEPTH, SSM_CONV, SSM_CONV_DIM), SSM_CONV ** -0.5),
        "ssm_conv_b": normal(ks[14], (DEPTH, SSM_CONV_DIM), 0.01),
        "ssm_dt_bias": dt0 + jnp.log(-jnp.expm1(-dt0)),
        "ssm_a_log": jnp.log(jax.random.uniform(ks[15], (DEPTH, SSM_HEADS), f32, 1.0, 16.0)),
        "ssm_d": 1.0 + 0.01 * jax.random.normal(ks[16], (DEPTH, SSM_HEADS), f32),
        "ssm_norm_g": gain(ks[17], SSM_D_INNER),
        "w_branch_att": normal(ks[18], (DEPTH, ATT_V_WIDTH, D_MODEL), ATT_V_WIDTH ** -0.5),
        "w_branch_ssm": normal(ks[19], (DEPTH, SSM_D_INNER, D_MODEL), SSM_D_INNER ** -0.5),
        "w_out": normal(ks[21], (DEPTH, D_MODEL, D_MODEL), D_MODEL ** -0.5),
        "mix_post_g": gain(ks[22], D_MODEL),
        "ffn2_pre_g": gain(ks[23], D_MODEL),
        "ffn2_w_gate": normal(ks[24], (DEPTH, D_MODEL, D_FF), D_MODEL ** -0.5),
        "ffn2_w_up": normal(ks[25], (DEPTH, D_MODEL, D_FF), D_MODEL ** -0.5),
        "ffn2_w_down": normal(ks[26], (DEPTH, D_FF, D_MODEL), D_FF ** -0.5),
        "ffn2_post_g": gain(ks[27], D_MODEL),
    }


def reference(x, ffn1_pre_g, ffn1_w_gate, ffn1_w_up, ffn1_w_down, ffn1_post_g,
              mix_pre_g, w_in, att_lambda_q1, att_lambda_k1, att_lambda_q2, att_lambda_k2,
              att_subln_g, ssm_conv_w, ssm_conv_b, ssm_dt_bias, ssm_a_log, ssm_d, ssm_norm_g,
              w_branch_att, w_branch_ssm, w_out, mix_post_g,
              ffn2_pre_g, ffn2_w_gate, ffn2_w_up, ffn2_w_down, ffn2_post_g):
    b, s, _ = x.shape
    split_points = [ATT_QK_WIDTH, 2 * ATT_QK_WIDTH, 2 * ATT_QK_WIDTH + ATT_V_WIDTH,
                    2 * ATT_QK_WIDTH + ATT_V_WIDTH + SSM_D_INNER,
                    2 * ATT_QK_WIDTH + ATT_V_WIDTH + SSM_D_INNER + SSM_CONV_DIM,
                    2 * ATT_QK_WIDTH + ATT_V_WIDTH + SSM_D_INNER + SSM_CONV_DIM + SSM_HEADS]
    h = x
    for i in range(DEPTH):
        lambda_init = 0.8 - 0.6 * math.exp(-0.3 * i)
        f = swiglu(rms_norm(h, ffn1_pre_g[i]), ffn1_w_gate[i], ffn1_w_up[i], ffn1_w_down[i])
        h = h + 0.5 * rms_norm(f, ffn1_post_g[i])
        u = rms_norm(h, mix_pre_g[i])
        proj = u @ w_in[i]
        q, k, v, z, xbc, dt_raw, gates = jnp.split(proj, split_points, axis=-1)
        q = q.reshape(b, s, ATT_HEADS, 2, ATT_HEAD_DIM)
        k = k.reshape(b, s, ATT_HEADS, 2, ATT_HEAD_DIM)
        v = v.reshape(b, s, ATT_HEADS, ATT_V_DIM)
        lam = (jnp.exp(jnp.sum(att_lambda_q1[i].astype(jnp.float32) * att_lambda_k1[i].astype(jnp.float32)))
               - jnp.exp(jnp.sum(att_lambda_q2[i].astype(jnp.float32) * att_lambda_k2[i].astype(jnp.float32)))
               + lambda_init)
        o_att = diff_attention(q, k, v, lam, att_subln_g[i], lambda_init)
        o_ssm = mamba2_mixer(z, xbc, dt_raw, ssm_conv_w[i], ssm_conv_b[i], ssm_dt_bias[i],
                             ssm_a_log[i], ssm_d[i], ssm_norm_g[i])
        g = jax.nn.sigmoid(gates).reshape(b, s, N_BRANCH, D_MODEL)
        merged = g[:, :, 0] * (o_att @ w_branch_att[i]) + g[:, :, 1] * (o_ssm @ w_branch_ssm[i])
        h = h + rms_norm(merged @ w_out[i], mix_post_g[i])
        f = swiglu(rms_norm(h, ffn2_pre_g[i]), ffn2_w_gate[i], ffn2_w_up[i], ffn2_w_down[i])
        h = h + 0.5 * rms_norm(f, ffn2_post_g[i])
    return h
```

```python
import math
from contextlib import ExitStack
import numpy as np
import concourse.bass as bass
import concourse.mybir as mybir
from concourse.bass_utils import run_bass_kernel_spmd

F32 = mybir.dt.float32
BF16 = mybir.dt.bfloat16
F32R = mybir.dt.float32r
AF = mybir.ActivationFunctionType
ALU = mybir.AluOpType
AX = mybir.AxisListType

D = 2048
KC = 16
DFF = 5632
FC = 44
SEQ = 4096
HALF = 2048
NQK = 2048
NV = 2048
DIN = 4096
NBC = 1024
NHEAD = 64
EPS = 1e-6


class _Op:
    __slots__ = ("eng", "emit", "deps", "sem", "val", "signal", "idx")


class Prog:
    ENGS = ("pe", "act", "dve", "pool", "sp")

    def __init__(self, nc):
        self.nc = nc
        self.ops = []
        self.ks = {}
        self.dma_cnt = {}

    def add(self, eng, emit, reads=(), writes=(), dma=None):
        i = len(self.ops)
        op = _Op()
        op.eng, op.emit, op.idx, op.signal = eng, emit, i, False
        if dma is not None:
            c = self.dma_cnt.get(dma, 0) + 1
            self.dma_cnt[dma] = c
            op.sem, op.val = ("dma", dma), 16 * c
        else:
            op.sem, op.val = eng, None
        deps = set()
        ks = self.ks
        for k in reads:
            st = ks.get(k)
            if st is not None and st[0] is not None:
                deps.add(st[0])
        for k in writes:
            st = ks.get(k)
            if st is not None:
                if st[0] is not None:
                    deps.add(st[0])
                deps.update(st[1].values())
        for k in reads:
            st = ks.get(k)
            if st is None:
                st = ks[k] = [None, {}]
            st[1][op.sem] = i
        for k in writes:
            st = ks.get(k)
            if st is None:
                st = ks[k] = [None, {}]
            st[0] = i
            st[1] = {}
        deps.discard(i)
        op.deps = deps
        self.ops.append(op)
        return i

    def barrier(self):
        deps = set()
        for st in self.ks.values():
            if st[0] is not None:
                deps.add(st[0])
            deps.update(st[1].values())
        for e in self.ENGS:
            op = _Op()
            op.eng, op.emit, op.idx, op.signal = e, (lambda _e: None), len(self.ops), False
            op.sem, op.val = e, None
            op.deps = set(deps)
            self.ops.append(op)

    def finalize(self, stack):
        nc = self.nc
        ops = self.ops
        for op in ops:
            for j in op.deps:
                d = ops[j]
                if d.eng == "pe" and op.eng == "pe" and d.sem == "pe" and op.sem == "pe":
                    continue
                d.signal = True
        cnt = {e: 0 for e in self.ENGS}
        for op in ops:
            if op.sem in cnt and op.signal:
                cnt[op.sem] += 1
                op.val = cnt[op.sem]
        sems = {}
        for e in self.ENGS:
            sems[e] = stack.enter_context(nc.semaphore("s_" + e))
        for k in self.dma_cnt:
            sems[("dma", k)] = stack.enter_context(nc.semaphore("d_" + str(k)))
        per = {e: [] for e in self.ENGS}
        for op in ops:
            per[op.eng].append(op)

        def run(eng_name, e):
            waited = {}
            for op in per[eng_name]:
                need = {}
                for j in op.deps:
                    d = ops[j]
                    if d.eng == "pe" and op.eng == "pe" and d.sem == "pe" and op.sem == "pe":
                        continue
                    if need.get(d.sem, 0) < d.val:
                        need[d.sem] = d.val
                for s, v in need.items():
                    if waited.get(s, 0) < v:
                        e.wait_ge(sems[s], v)
                        waited[s] = v
                ins = op.emit(e)
                if ins is not None:
                    if op.sem == eng_name:
                        if op.signal:
                            ins.then_inc(sems[eng_name], 1)
                    else:
                        ins.then_inc(sems[op.sem], 16)

        block = stack.enter_context(nc.Block())

        @block.tensor
        def _(e):
            run("pe", e)

        @block.scalar
        def _(e):
            run("act", e)

        @block.vector
        def _(e):
            run("dve", e)

        @block.gpsimd
        def _(e):
            run("pool", e)

        @block.sync
        def _(e):
            run("sp", e)


class Ctx:
    def __init__(self, nc, stack):
        self.nc = nc
        self.stack = stack
        self.p = Prog(nc)
        self.uid = 0
        self.psum = stack.enter_context(nc.psum_tensor("psum", [128, 4096], F32))
        self.rr = {}
        self.arena = None
        self.apos = 0
        self.amark = 0

    def alloc(self, cols, dt):
        n = cols * (2 if dt == F32 else 1)
        n = (n + 15) // 16 * 16
        a = self.arena[:, self.apos:self.apos + n]
        self.apos += n
        assert self.apos <= self.arena.shape[1], ("arena overflow", self.apos)
        if dt == F32:
            a = a.bitcast(F32)
        return a[:, 0:cols]

    def phase(self):
        self.p.barrier()
        self.apos = self.amark
        self.rr = {}

    def sb(self, name, shape, dt):
        return self.stack.enter_context(self.nc.sbuf_tensor(name, shape, dt))

    def dram(self, name, shape, dt, kind="Internal"):
        return self.nc.dram_tensor(name, shape, dt, kind=kind).ap()

    def bank(self, b, n=1):
        return self.psum[:, b * 512:(b + n) * 512]

    def mm(self, out, lhsT, rhs, start, stop, reads, writes):
        self.p.add("pe", lambda e: e.matmul(out, lhsT, rhs, start=start, stop=stop),
                   reads=reads, writes=writes)

    def act(self, out, in_, func, reads, writes, bias=None, scale=None, accum_out=None):
        kw = {}
        if bias is not None:
            kw["bias"] = bias
        if scale is not None:
            kw["scale"] = scale
        if accum_out is not None:
            kw["accum_out"] = accum_out
        self.p.add("act", lambda e: e.activation(out, in_, func, **kw), reads=reads, writes=writes)

    def load(self, out, in_, reads, writes, key, eng="sp"):
        self.p.add(eng, lambda e: e.dma_start(out=out, in_=in_), reads=reads, writes=writes, dma=key)

    def store(self, out, in_, reads, writes, key, eng="sp"):
        self.p.add(eng, lambda e: e.dma_start(out=out, in_=in_), reads=reads, writes=writes, dma=key)


def norm_gen(cx, src, t0, T, gcol, uT, ukey, tag, ssb=None, rstd=None, rkey="rstd"):
    p = cx.p
    NH = T // 512
    xb = cx.xbuf
    ssb = cx.ssbank if ssb is None else ssb
    rstd = cx.rstd if rstd is None else rstd
    for ps in range(2):
        for kc in range(KC):
            b = cx.rr["xb"] = (cx.rr.get("xb", -1) + 1) % len(xb)
            xt = xb[b]
            cx.load(xt[:, 0:T], src[kc * 128:(kc + 1) * 128, t0:t0 + T], reads=[(tag, "src", kc)],
                    writes=[("xb", b)], key="xb%d" % b)
            if ps == 0:
                sb_ = cx.rr["sq"] = (cx.rr.get("sq", -1) + 1) % 2
                sq = cx.sqbuf[sb_]
                cx.act(sq[:, 0:T], xt[:, 0:T], AF.Square, reads=[("xb", b)], writes=[("sq", sb_)])
                for h in range(NH):
                    cx.mm(cx.bank(ssb + h), cx.ones[:, :], sq[:, h * 512:(h + 1) * 512],
                          start=(kc == 0), stop=(kc == KC - 1),
                          reads=[("sq", sb_), "ones"], writes=[("ps", ssb + h)])
            else:
                o = uT[:, kc, 0:T]
                g = gcol[:, kc:kc + 1]
                r = rstd[:, 0:T]
                p.add("dve", lambda e, o=o, xt=xt, g=g, r=r, T=T: e.scalar_tensor_tensor(
                    out=o, in0=xt[:, 0:T], scalar=g, in1=r, op0=ALU.mult, op1=ALU.mult),
                    reads=[("xb", b), rkey, "vecs"], writes=[(ukey, kc)])
            yield
        if ps == 0:
            rstd_from_ss(cx, T, D, ssb, rstd, rkey)
            yield


def norm_to_uT(cx, src, t0, T, gcol, uT, ukey, tag, **kw):
    for _ in norm_gen(cx, src, t0, T, gcol, uT, ukey, tag, **kw):
        pass


def rstd_from_ss(cx, T, n, ssb=None, rstd=None, rkey="rstd"):
    NH = T // 512
    ssb = cx.ssbank if ssb is None else ssb
    rstd = cx.rstd if rstd is None else rstd
    r = rstd[:, 0:T]
    ss = cx.bank(ssb, NH)
    cx.p.add("dve", lambda e: e.tensor_scalar(out=r, in0=ss, scalar1=1.0 / n, scalar2=EPS,
                                               op0=ALU.mult, op1=ALU.add),
             reads=[("ps", ssb + h) for h in range(NH)], writes=[rkey])
    cx.act(r, r, AF.Sqrt, reads=[rkey], writes=[rkey])
    cx.p.add("dve", lambda e: e.reciprocal(out=r, in_=r), reads=[rkey], writes=[rkey])


def wload(cx, dst, src, key_r, key_w, semkey):
    cx.load(dst, src, reads=key_r, writes=key_w, key=semkey, eng="pool")


def combine_gen(cx, fsrc, xsrc, dst, t0, T, gcol, tag, xtag, otag, half, xoff=0, rstd=None, rkey="rstd"):
    p = cx.p
    xb = cx.xbuf
    rstd = cx.rstd if rstd is None else rstd
    for dc in range(KC):
        b1 = cx.rr["xb"] = (cx.rr.get("xb", -1) + 1) % len(xb)
        ft = xb[b1]
        cx.load(ft[:, 0:T], fsrc[dc * 128:(dc + 1) * 128, 0:T], reads=[(tag, "f", dc)],
                writes=[("xb", b1)], key="xb%d" % b1)
        b2 = cx.rr["xb"] = (cx.rr.get("xb", -1) + 1) % len(xb)
        xt = xb[b2]
        cx.load(xt[:, 0:T], xsrc[dc * 128:(dc + 1) * 128, xoff + t0:xoff + t0 + T], reads=[(xtag, "src", dc)],
                writes=[("xb", b2)], key="xb%d" % b2)
        g = gcol[:, dc:dc + 1]
        r = rstd[:, 0:T]
        p.add("dve", lambda e, ft=ft, g=g, r=r: e.scalar_tensor_tensor(
            out=ft[:, 0:T], in0=ft[:, 0:T], scalar=g, in1=r, op0=ALU.mult, op1=ALU.mult),
            reads=[("xb", b1), rkey, "vecs"], writes=[("xb", b1)])
        if half:
            p.add("dve", lambda e, ft=ft, xt=xt: e.scalar_tensor_tensor(
                out=ft[:, 0:T], in0=ft[:, 0:T], scalar=0.5, in1=xt[:, 0:T], op0=ALU.mult, op1=ALU.add),
                reads=[("xb", b1), ("xb", b2)], writes=[("xb", b1)])
        else:
            p.add("dve", lambda e, ft=ft, xt=xt: e.tensor_tensor(
                out=ft[:, 0:T], in0=ft[:, 0:T], in1=xt[:, 0:T], op=ALU.add),
                reads=[("xb", b1), ("xb", b2)], writes=[("xb", b1)])
        cx.store(dst[dc * 128:(dc + 1) * 128, t0:t0 + T], ft[:, 0:T], reads=[("xb", b1)],
                 writes=[(otag, "src", dc)], key="xb%d" % b1)
        yield


def combine_residual(*a, **kw):
    for _ in combine_gen(*a, **kw):
        pass


def ffn(cx, src, dst, ntok, T, wg, wu, wd, gpre, gpost, tag, otag, wsc=None):
    p = cx.p
    NH = T // 512
    uT = cx.uT
    hid = cx.hid
    WB = cx.wbuf
    tiles = list(range(0, ntok, T))
    PSB = 4 if NH == 2 else 6
    norm_to_uT(cx, src, tiles[0], T, gpre, uT, "uT", tag, ssb=PSB, rstd=cx.rstdA, rkey="rstdA")
    comb = None
    for ti, t0 in enumerate(tiles):
        for j in range(FC):
            wb = cx.rr["wb"] = (cx.rr.get("wb", -1) + 1) % len(WB)
            w = WB[wb]
            reuse = wsc is not None and len(tiles) > 1
            if reuse and ti > 0:
                cx.load(w[:, 0:KC * 128], wsc[0][j], reads=[(tag, "wscg", j)], writes=[("wb", wb, 0)], key="wb%d" % wb)
                cx.load(w[:, KC * 128:2 * KC * 128], wsc[1][j], reads=[(tag, "wscu", j)], writes=[("wb", wb, 1)],
                        key="wb%d" % wb)
            else:
                wload(cx, w[:, 0:KC * 128], wg[j], [], [("wb", wb, 0)], "wb%d" % wb)
                wload(cx, w[:, KC * 128:2 * KC * 128], wu[j], [], [("wb", wb, 1)], "wb%d" % wb)
            gb = cx.rr["gub"] = (cx.rr.get("gub", -1) + 1) % 2
            gbank = gb * 2 * NH
            ubank = gbank + NH
            for which, bank0 in ((0, gbank), (1, ubank)):
                for kc in range(KC):
                    lw = w[:, which * KC * 128 + kc * 128: which * KC * 128 + (kc + 1) * 128]
                    for h in range(NH):
                        cx.mm(cx.bank(bank0 + h), lw, uT[:, kc, h * 512:(h + 1) * 512],
                              start=(kc == 0), stop=(kc == KC - 1),
                              reads=[("wb", wb, which), ("uT", kc)], writes=[("ps", bank0 + h)])
            sl = cx.rr["sil"] = (cx.rr.get("sil", -1) + 1) % 2
            st = cx.silb[sl]
            cx.act(st[:, 0:T], cx.bank(gbank, NH), AF.Silu,
                   reads=[("ps", gbank + h) for h in range(NH)], writes=[("sil", sl)])
            if reuse and ti == 0:
                cx.store(wsc[0][j], w[:, 0:KC * 128], reads=[("wb", wb, 0)], writes=[(tag, "wscg", j)],
                         key="wbs%d" % wb, eng="act")
                cx.store(wsc[1][j], w[:, KC * 128:2 * KC * 128], reads=[("wb", wb, 1)], writes=[(tag, "wscu", j)],
                         key="wbs%d" % wb, eng="act")
            ho = hid[:, j, 0:T]
            ub = cx.bank(ubank, NH)
            p.add("dve", lambda e, ho=ho, st=st, ub=ub: e.tensor_tensor(
                out=ho, in0=st[:, 0:T], in1=ub, op=ALU.mult),
                reads=[("sil", sl)] + [("ps", ubank + h) for h in range(NH)],
                writes=[("hid", j)])
            if comb is not None and j % 2 == 1:
                next(comb, None)
        if comb is not None:
            for _ in comb:
                pass
        nxt = None
        if ti + 1 < len(tiles):
            nxt = norm_gen(cx, src, tiles[ti + 1], T, gpre, uT, "uT", tag, ssb=PSB, rstd=cx.rstdA, rkey="rstdA")
        fscr = cx.fscr
        for dc in range(KC):
            wb = cx.rr["wb"] = (cx.rr.get("wb", -1) + 1) % len(WB)
            w = WB[wb]
            reuse = wsc is not None and len(tiles) > 1
            if reuse and ti > 0:
                cx.load(w[:, 0:FC * 128], wsc[2][dc], reads=[(tag, "wscd", dc)], writes=[("wb", wb, 0), ("wb", wb, 1)],
                        key="wb%d" % wb)
            else:
                wload(cx, w[:, 0:FC * 128], wd[dc], [], [("wb", wb, 0), ("wb", wb, 1)], "wb%d" % wb)
            db = cx.rr["dnb"] = (cx.rr.get("dnb", -1) + 1) % 2
            dbank = db * NH
            for fc in range(FC):
                for h in range(NH):
                    cx.mm(cx.bank(dbank + h), w[:, fc * 128:(fc + 1) * 128], hid[:, fc, h * 512:(h + 1) * 512],
                          start=(fc == 0), stop=(fc == FC - 1),
                          reads=[("wb", wb, 0), ("wb", wb, 1), ("hid", fc)], writes=[("ps", dbank + h)])
            b = cx.rr["xb"] = (cx.rr.get("xb", -1) + 1) % len(cx.xbuf)
            ft = cx.xbuf[b]
            psr = [("ps", dbank + h) for h in range(NH)]
            cx.act(ft[:, 0:T], cx.bank(dbank, NH), AF.Copy, reads=psr, writes=[("xb", b)])
            sb_ = cx.rr["sq"] = (cx.rr.get("sq", -1) + 1) % 2
            sq = cx.sqbuf[sb_]
            cx.act(sq[:, 0:T], cx.bank(dbank, NH), AF.Square, reads=psr, writes=[("sq", sb_)])
            if reuse and ti == 0:
                cx.store(wsc[2][dc], w[:, 0:FC * 128], reads=[("wb", wb, 0), ("wb", wb, 1)], writes=[(tag, "wscd", dc)],
                         key="wbs%d" % wb, eng="act")
            cx.store(fscr[dc * 128:(dc + 1) * 128, 0:T], ft[:, 0:T], reads=[("xb", b)],
                     writes=[("fs", "f", dc)], key="xb%d" % b)
            for h in range(NH):
                cx.mm(cx.bank(cx.ssbank + h), cx.ones[:, :], sq[:, h * 512:(h + 1) * 512],
                      start=(dc == 0), stop=(dc == KC - 1),
                      reads=[("sq", sb_), "ones"], writes=[("ps", cx.ssbank + h)])
            if nxt is not None and dc >= 1:
                for _ in range(3):
                    next(nxt, None)
        if nxt is not None:
            for _ in nxt:
                pass
        rstd_from_ss(cx, T, D, cx.ssbank, cx.rstd, "rstd")
        comb = combine_gen(cx, fscr, src, dst, t0, T, gpost, "fs", tag, otag, half=True)
    for _ in comb:
        pass


def layout_ffn(cx, T):
    cx.uT = cx.alloc(KC * T, BF16).rearrange("p (k t) -> p k t", k=KC)
    cx.hid = cx.alloc(FC * T, BF16).rearrange("p (k t) -> p k t", k=FC)
    cx.wbuf = [cx.alloc(FC * 128, BF16) for _ in range(3)]
    cx.xbuf = [cx.alloc(T, F32) for _ in range(4)]
    cx.sqbuf = [cx.alloc(T, BF16) for _ in range(2)]
    cx.silb = [cx.alloc(T, F32) for _ in range(2)]
    cx.rstdA = cx.alloc(T, F32)
    cx.ssbank = 8 - T // 512


def rrn(cx, name, n):
    v = cx.rr[name] = (cx.rr.get(name, -1) + 1) % n
    return v


def inproj(cx, W, ntok_ctx, ntok_own, T, side=None):
    p = cx.p
    NH = T // 512
    NT = ntok_ctx + ntok_own
    uTs = [cx.alloc(KC * T, BF16).rearrange("p (k t) -> p k t", k=KC) for _ in range(2)]
    ukeys = ["uTa", "uTb"]
    cx.wbuf = [cx.alloc(KC * 512, BF16) for _ in range(3)]
    cx.xbuf = [cx.alloc(T, F32) for _ in range(3)]
    cx.sqbuf = [cx.alloc(T, BF16) for _ in range(2)]
    cx.ssbank = 8 - NH
    gm = cx.gv("mix_pre_g")
    tiles = list(range(0, NT, T))
    norm_to_uT(cx, cx.h1, tiles[0], T, gm, uTs[0], ukeys[0], "h1")
    for ti, t0 in enumerate(tiles):
        own = t0 >= ntok_ctx
        to = t0 - ntok_ctx
        uT = uTs[ti % 2]
        uk = ukeys[ti % 2]
        nxt = None
        if ti + 1 < len(tiles):
            nxt = norm_gen(cx, cx.h1, tiles[ti + 1], T, gm, uTs[(ti + 1) % 2], ukeys[(ti + 1) % 2], "h1")

        def fm(wsrc, nchunk, epi, rows=128):
            for j in range(nchunk):
                wb = rrn(cx, "wb", 3)
                w = cx.wbuf[wb]
                wload(cx, w[:, 0:KC * rows], wsrc[j], [], [("wb", wb, 0), ("wb", wb, 1)], "wb%d" % wb)
                pb = rrn(cx, "pjb", 3) * NH
                for kc in range(KC):
                    for h in range(NH):
                        cx.mm(cx.bank(pb + h)[0:rows, :], w[:, kc * rows:(kc + 1) * rows], uT[:, kc, h * 512:(h + 1) * 512],
                              start=(kc == 0), stop=(kc == KC - 1),
                              reads=[("wb", wb, 0), ("wb", wb, 1), (uk, kc)], writes=[("ps", pb + h)])
                epi(j, cx.bank(pb, NH), [("ps", pb + h) for h in range(NH)])
                if side is not None:
                    side.step()

        def epi_bf(dst, col0, scale=None):
            def f(j, ps, psk):
                b = rrn(cx, "sq", 2)
                o = cx.sqbuf[b]
                cx.act(o[:, 0:T], ps, AF.Copy, reads=psk, writes=[("sq", b)], scale=scale)
                cx.store(dst[j * 128:(j + 1) * 128, col0:col0 + T], o[:, 0:T], reads=[("sq", b)],
                         writes=[("scr", id(dst), j)], key="sq%d" % b)
            return f

        def epi_f32(dst, col0, func=AF.Copy, rows=128, isxbc=False):
            def f(j, ps, psk):
                b = rrn(cx, "xb", 3)
                o = cx.xbuf[b]
                cx.act(o[0:rows, 0:T], ps[0:rows, :], func, reads=psk, writes=[("xb", b)])
                wk = ("scr", "xbcT", j) if isxbc else ("scr", id(dst), j)
                cx.store(dst[j * rows:(j + 1) * rows, col0:col0 + T], o[0:rows, 0:T], reads=[("xb", b)],
                         writes=[wk], key="xb%d" % b)
                if isxbc and side is not None:
                    side.avail = (t0 // T) * 48 + j + 1
            return f

        if own:
            fm(W["wq"], 16, epi_bf(cx.qT, to, scale=1.0 / math.sqrt(128.0)))
        fm(W["wk"], 16, epi_bf(cx.kT, t0))
        fm(W["wxbc"], 48, epi_f32(cx.xbcT, 4 + t0, isxbc=True))
        fm(W["wdt"], 1, epi_f32(cx.dtT, t0, rows=64), rows=64)
        if own:
            fm(W["wg"], 32, epi_f32(cx.gT, to, func=AF.Sigmoid))

        def tm(wsrc, nslab, dst, row0, dt_bf):
            for s_ in range(nslab):
                wb = rrn(cx, "wb", 3)
                w = cx.wbuf[wb]
                wload(cx, w[:, 0:KC * 512], wsrc[s_], [], [("wb", wb, 0), ("wb", wb, 1)], "wb%d" % wb)
                for tb in range(T // 128):
                    pb = rrn(cx, "pjb2", 6)
                    for kc in range(KC):
                        cx.mm(cx.bank(pb), uT[:, kc, tb * 128:(tb + 1) * 128], w[:, kc * 512:(kc + 1) * 512],
                              start=(kc == 0), stop=(kc == KC - 1),
                              reads=[("wb", wb, 0), ("wb", wb, 1), (uk, kc)], writes=[("ps", pb)])
                    if dt_bf:
                        b = rrn(cx, "sq", 2)
                        o = cx.sqbuf[b]
                        key = ("sq", b)
                        sk = "sq%d" % b
                    else:
                        b = rrn(cx, "xb", 3)
                        o = cx.xbuf[b]
                        key = ("xb", b)
                        sk = "xb%d" % b
                    cx.act(o[:, 0:512], cx.bank(pb), AF.Copy, reads=[("ps", pb)], writes=[key])
                    r0 = row0 + tb * 128
                    cx.store(dst[r0:r0 + 128, s_ * 512:(s_ + 1) * 512], o[:, 0:512], reads=[key],
                             writes=[("scr", id(dst), "tm", s_)], key=sk)
                    if side is not None:
                        side.step()
                    if nxt is not None:
                        next(nxt, None)
                        next(nxt, None)

        tm(W["wv"], 4, cx.V, t0, True)
        if own:
            tm(W["wz"], 8, cx.Z, to, False)
        if nxt is not None:
            for _ in nxt:
                pass


def attention(cx, ntok_ctx, ntok_own, side=None, side_every=3):
    p = cx.p
    nblk = [0]
    NTK = ntok_ctx + ntok_own
    NKB = NTK // 128
    NCB = ntok_ctx // 128
    NQT = ntok_own // 512
    kT = [cx.alloc(2 * NTK, BF16).rearrange("p (m t) -> p m t", m=2) for _ in range(2)]
    Vt = [cx.alloc(NKB * 256, BF16).rearrange("p (b c) -> p b c", b=NKB) for _ in range(2)]
    qT = [cx.alloc(2 * ntok_own, BF16).rearrange("p (m t) -> p m t", m=2) for _ in range(2)]
    PT = [cx.alloc(512, BF16) for _ in range(6)]
    rinv = [cx.alloc(512, F32) for _ in range(2)]
    of = [cx.alloc(512, F32) for _ in range(2)]
    tmpf = cx.alloc(512, F32)
    sqb = cx.alloc(512, BF16)
    ob = [cx.alloc(512, BF16) for _ in range(2)]
    rst = cx.alloc(512, F32)
    SBK = ((0, 1), (0, 1))
    SB = (0, 1)
    OB = ((2, 3), (4, 5))
    RB = (6, 7)
    def load_head(h):
        hb = h % 2
        for m in range(2):
            cx.load(kT[hb][:, m, :], cx.kT[(2 * h + m) * 128:(2 * h + m + 1) * 128, 0:NTK], reads=[],
                    writes=[("kT", hb)], key="kT%d" % hb)
            cx.load(qT[hb][:, m, :], cx.qT[(2 * h + m) * 128:(2 * h + m + 1) * 128, 0:ntok_own], reads=[],
                    writes=[("qT", hb)], key="qT%d" % hb)
        cx.load(Vt[hb][:, :, :], cx.V[0:NTK, h * 256:(h + 1) * 256].rearrange("(b p) c -> p b c", p=128),
                reads=[], writes=[("Vt", hb)], key="Vt%d" % hb)

    load_head(0)
    for h in range(8):
        hb = h % 2
        if h + 1 < 8:
            load_head(h + 1)
        for qt in range(NQT):
            nkb = NCB + 4 * (qt + 1)

            def geom(kb):
                ob_ = kb - NCB
                r = ob_ - 4 * qt if ob_ >= 4 * qt else -1
                return r, (128 * r if r > 0 else 0)

            def emit_S(kb):
                r, c0 = geom(kb)
                par = kb % 2
                pts = []
                for m in range(2):
                    sbk = SBK[par][m]
                    cx.mm(cx.bank(sbk)[:, c0:512], kT[hb][:, m, kb * 128:(kb + 1) * 128],
                          qT[hb][:, m, qt * 512 + c0:(qt + 1) * 512], start=True, stop=True,
                          reads=[("kT", hb), ("qT", hb)], writes=[("ps", sbk)])
                    pi = rrn(cx, "pt", len(PT))
                    pt = PT[pi]
                    pts.append((pi, pt))
                    bias = cx.flags[:, 0:1] if kb < NCB else None
                    cx.act(pt[:, c0:512], cx.bank(sbk)[:, c0:512], AF.Exp, reads=[("ps", sbk), "flags"],
                           writes=[("pt", pi)], bias=bias)
                    if r >= 0:
                        p.add("dve", lambda e, pt=pt, c0=c0: e.tensor_tensor(
                            out=pt[:, c0:c0 + 128], in0=pt[:, c0:c0 + 128], in1=cx.tri_bf[:, :], op=ALU.mult),
                            reads=[("pt", pi), "consts"], writes=[("pt", pi)])
                return pts

            def emit_PV(kb, pts):
                r, c0 = geom(kb)
                for m in range(2):
                    pi, pt = pts[m]
                    for dvc in range(2):
                        cx.mm(cx.bank(OB[m][dvc])[:, c0:512], Vt[hb][:, kb, dvc * 128:(dvc + 1) * 128], pt[:, c0:512],
                              start=(kb == 0), stop=(kb == nkb - 1),
                              reads=[("pt", pi), ("Vt", hb)], writes=[("ps", OB[m][dvc])])
                    cx.mm(cx.bank(RB[m])[:, c0:512], cx.ones[:, :], pt[:, c0:512],
                          start=(kb == 0), stop=(kb == nkb - 1),
                          reads=[("pt", pi), "ones"], writes=[("ps", RB[m])])

            nxt = emit_S(0)
            for kb in range(nkb):
                cur = nxt
                if kb + 1 < nkb:
                    nxt = emit_S(kb + 1)
                emit_PV(kb, cur)
                nblk[0] += 1
                if side is not None and nblk[0] % side_every == 0:
                    next(side, None)
            for m in range(2):
                p.add("dve", lambda e, m=m: e.reciprocal(out=rinv[m][:, :], in_=cx.bank(RB[m])),
                      reads=[("ps", RB[m])], writes=[("rinv", m)])
            p.add("dve", lambda e: e.tensor_scalar(out=rinv[1][:, :], in0=rinv[1][:, :], scalar1=cx.neglam[:, 0:1],
                                                    scalar2=None, op0=ALU.mult),
                  reads=[("rinv", 1), "neglam"], writes=[("rinv", 1)])
            for dvc in range(2):
                p.add("dve", lambda e, dvc=dvc: e.tensor_tensor(out=of[dvc][:, :], in0=cx.bank(OB[0][dvc]),
                                                                 in1=rinv[0][:, :], op=ALU.mult),
                      reads=[("ps", OB[0][dvc]), ("rinv", 0)], writes=[("of", dvc)])
                p.add("dve", lambda e, dvc=dvc: e.tensor_tensor(out=tmpf[:, :], in0=cx.bank(OB[1][dvc]),
                                                                 in1=rinv[1][:, :], op=ALU.mult),
                      reads=[("ps", OB[1][dvc]), ("rinv", 1)], writes=["tmpf"])
                p.add("dve", lambda e, dvc=dvc: e.tensor_tensor(out=of[dvc][:, :], in0=of[dvc][:, :],
                                                                 in1=tmpf[:, :], op=ALU.add),
                      reads=["tmpf", ("of", dvc)], writes=[("of", dvc)])
                cx.act(sqb[:, :], of[dvc][:, :], AF.Square, reads=[("of", dvc)], writes=["asq"])
                cx.mm(cx.bank(SB[0]), cx.ones[:, :], sqb[:, :], start=(dvc == 0), stop=(dvc == 1),
                      reads=["asq", "ones"], writes=[("ps", SB[0])])
            p.add("dve", lambda e: e.tensor_scalar(out=rst[:, :], in0=cx.bank(SB[0]), scalar1=1.0 / 256.0, scalar2=EPS,
                                                    op0=ALU.mult, op1=ALU.add),
                  reads=[("ps", SB[0])], writes=["arst"])
            cx.act(rst[:, :], rst[:, :], AF.Sqrt, reads=["arst"], writes=["arst"])
            p.add("dve", lambda e: e.reciprocal(out=rst[:, :], in_=rst[:, :]), reads=["arst"], writes=["arst"])
            for dvc in range(2):
                p.add("dve", lambda e, dvc=dvc: e.scalar_tensor_tensor(
                    out=ob[dvc][:, :], in0=of[dvc][:, :], scalar=cx.subg[:, dvc:dvc + 1], in1=rst[:, :],
                    op0=ALU.mult, op1=ALU.mult),
                    reads=[("of", dvc), "arst", "subg"], writes=[("aob", dvc)])
                j = 2 * h + dvc
                cx.store(cx.oattT[j * 128:(j + 1) * 128, qt * 512:(qt + 1) * 512], ob[dvc][:, :],
                         reads=[("aob", dvc)], writes=[("scr", "oatt", j)], key="aob%d" % dvc, eng="pool")


def bc(ap, shape, axis):
    return ap.unsqueeze(axis).to_broadcast(shape)


class ConvSide:
    def __init__(self, cx, ntok_ctx, ntok_own, T, NB=4):
        self.cx = cx
        NT = ntok_ctx + ntok_own
        self.T = T
        self.NB = NB
        self.ntok_ctx = ntok_ctx
        self.xin = [cx.alloc(T + 8, F32) for _ in range(NB)]
        self.acc = [cx.alloc(T, F32) for _ in range(NB)]
        self.outf = [cx.alloc(T, F32) for _ in range(2)]
        self.outb = [cx.alloc(T, BF16) for _ in range(2)]
        self.units = [(cc, t0) for t0 in range(0, NT, T) for cc in range(48)]
        self.avail = 0
        self.need = 0
        self.gen = self._gen()
        self.done = False

    def step(self):
        if self.done or self.need > self.avail:
            return
        try:
            next(self.gen)
        except StopIteration:
            self.done = True

    def drain(self):
        self.avail = len(self.units)
        while not self.done:
            self.step()

    def _gen(self):
        cx = self.cx
        p = cx.p
        T = self.T
        NB = self.NB
        xin, acc, outf, outb = self.xin, self.acc, self.outf, self.outb
        for u0 in range(0, len(self.units), NB):
            batch = self.units[u0:u0 + NB]
            self.need = u0 + len(batch)
            yield
            for b, (cc, t0) in enumerate(batch):
                xi = xin[b]
                cx.load(xi[:, 0:T + 3], cx.xbcT[cc * 128:(cc + 1) * 128, 4 + t0 - 3:4 + t0 + T],
                        reads=[("scr", "xbcT", cc), ("scr", "pad")], writes=[("cin", b)], key="cin%d" % b)
                if t0 == self.ntok_ctx and self.ntok_ctx > 0:
                    p.add("dve", lambda e, xi=xi: e.tensor_scalar(out=xi[:, 0:3], in0=xi[:, 0:3], scalar1=cx.flags[:, 1:2],
                                                                  scalar2=None, op0=ALU.mult),
                          reads=[("cin", b), "flags"], writes=[("cin", b)])
                a = acc[b]
                w = cx.convw
                p.add("dve", lambda e, xi=xi, a=a, cc=cc: e.tensor_scalar(
                    out=a[:, 0:T], in0=xi[:, 0:T], scalar1=w[:, cc * 4:cc * 4 + 1], scalar2=None, op0=ALU.mult),
                    reads=[("cin", b), "convw"], writes=[("cacc", b)])
                for j in range(1, 4):
                    p.add("dve", lambda e, xi=xi, a=a, cc=cc, j=j: e.scalar_tensor_tensor(
                        out=a[:, 0:T], in0=xi[:, j:j + T], scalar=w[:, cc * 4 + j:cc * 4 + j + 1], in1=a[:, 0:T],
                        op0=ALU.mult, op1=ALU.add),
                        reads=[("cin", b), ("cacc", b), "convw"], writes=[("cacc", b)])
                yield
            for b, (cc, t0) in enumerate(batch):
                a = acc[b]
                if cc < 32:
                    ob_ = rrn(cx, "cof", 2)
                    o = outf[ob_]
                    okey, dstT, row0, sk = ("cof", ob_), cx.xcT, cc * 128, "cof%d" % ob_
                else:
                    ob_ = rrn(cx, "cob", 2)
                    o = outb[ob_]
                    okey, dstT, row0, sk = ("cob", ob_), cx.bcT, (cc - 32) * 128, "cob%d" % ob_
                cx.act(o[:, 0:T], a[:, 0:T], AF.Silu, reads=[("cacc", b), "convw"], writes=[okey],
                       bias=cx.convb[:, cc:cc + 1])
                cx.store(dstT[row0:row0 + 128, t0:t0 + T], o[:, 0:T], reads=[okey],
                         writes=[("scr", "conv", cc)], key=sk, eng="act")
            yield


def ssd(cx, ntok_ctx, ntok_own):
    p = cx.p
    NT = ntok_ctx + ntok_own
    NCH = NT // 128
    NCC = ntok_ctx // 128
    A3 = lambda ap, a, b_: ap.rearrange("p (a b) -> p a b", a=a)
    xfm = [cx.alloc(4096, F32) for _ in range(2)]
    bct = [cx.alloc(2048, BF16) for _ in range(2)]
    dtr = [cx.alloc(128, F32) for _ in range(2)]
    zb = cx.alloc(4096, F32)
    xtok = cx.alloc(4096, F32)
    xw = cx.alloc(4096, BF16)
    xdt = cx.alloc(4096, BF16)
    S = cx.alloc(4096, F32)
    Sbf = cx.alloc(4096, BF16)
    y = cx.alloc(4096, F32)
    ngb = cx.alloc(4096, F32)
    yn = cx.alloc(4096, BF16)
    oTs = cx.alloc(4096, BF16)
    rhsL = [cx.alloc(1024, F32) for _ in range(2)]
    Eb = [cx.alloc(1024, BF16) for _ in range(2)]
    Mb = [cx.alloc(1024, BF16) for _ in range(2)]
    cbm = [cx.alloc(128, BF16) for _ in range(2)]
    Btok = cx.alloc(1024, BF16)
    e1 = cx.alloc(128, F32)
    dtf = cx.alloc(128, F32)
    aT = cx.alloc(128, F32)
    da = cx.alloc(128, F32)
    ct = cx.alloc(128, F32)
    dte = cx.alloc(64, F32)
    w1 = cx.alloc(64, F32)
    cdec = cx.alloc(64, F32)
    din = cx.alloc(64, F32)
    tmpg = [cx.alloc(512, F32) for _ in range(2)]
    junk = cx.alloc(512, BF16)
    ssq = cx.alloc(8, F32)
    rs8 = cx.alloc(8, F32)
    cx.load(ngb[:, :], cx.ngb_d, reads=[], writes=["ngb"], key="ngb")
    p.add("dve", lambda e: e.memset(S[:, :], 0.0), writes=["S"])

    def loads(c):
        b = c % 2
        t0 = c * 128
        cx.load(A3(xfm[b], 32, 128), cx.xcT[:, t0:t0 + 128].rearrange("(c p) t -> p c t", p=128), reads=[],
                writes=[("xfm", b)], key="xfm%d" % b)
        cx.load(A3(bct[b], 16, 128), cx.bcT[:, t0:t0 + 128].rearrange("(c p) t -> p c t", p=128), reads=[],
                writes=[("bct", b)], key="bct%d" % b)
        cx.load(dtr[b][0:64, :], cx.dtT[:, t0:t0 + 128], reads=[], writes=[("dtr", b)], key="dtr%d" % b)

    loads(0)
    for c in range(NCH):
        b = c % 2
        own = c >= NCC
        tl0 = (c - NCC) * 128
        if c + 1 < NCH:
            loads(c + 1)
        if own:
            cx.load(zb[:, :], cx.Z[tl0:tl0 + 128, :], reads=[], writes=["z"], key="z")
        xf3 = A3(xfm[b], 32, 128)
        bc3 = A3(bct[b], 16, 128)
        cx.act(e1[0:64, :], dtr[b][0:64, :], AF.Exp, reads=[("dtr", b), "ssmv"], writes=["e1"], bias=cx.ssmv[0:64, 0:1])
        cx.act(dtf[0:64, :], e1[0:64, :], AF.Ln, reads=["e1"], writes=["dtf"], bias=1.0)
        p.add("dve", lambda e: e.tensor_scalar(out=aT[0:64, :], in0=dtf[0:64, :], scalar1=cx.ssmv[0:64, 2:3], scalar2=None,
                                                op0=ALU.mult), reads=["dtf", "ssmv"], writes=["aT"])
        mb = 7
        p.add("pe", lambda e: e.transpose(cx.bank(mb)[:, 0:64], dtf[0:64, :], cx.ident_f[0:64, 0:64]),
              reads=["dtf", "consts"], writes=[("ps", mb)])
        p.add("pe", lambda e: e.transpose(cx.bank(mb)[:, 64:128], aT[0:64, :], cx.ident_f[0:64, 0:64]),
              reads=["aT", "consts"], writes=[("ps", mb)])
        p.add("dve", lambda e: e.tensor_copy(out=da[:, :], in_=cx.bank(mb)[:, 0:128]), reads=[("ps", mb)], writes=["da"])
        cx.mm(cx.bank(mb)[:, 128:192], cx.tri_f[:, :], da[:, 64:128], True, True, reads=["da", "consts"], writes=[("ps", mb)])
        cx.mm(cx.bank(mb)[:, 192:256], cx.ones_f[:, :], da[:, 64:128], True, True, reads=["da", "consts"], writes=[("ps", mb)])
        p.add("dve", lambda e: e.tensor_copy(out=ct[:, :], in_=cx.bank(mb)[:, 128:256]), reads=[("ps", mb)], writes=["ct"])
        p.add("dve", lambda e: e.tensor_tensor(out=dte[:, :], in0=ct[:, 64:128], in1=ct[:, 0:64], op=ALU.subtract),
              reads=["ct"], writes=["dte"])
        cx.act(dte[:, :], dte[:, :], AF.Exp, reads=["dte"], writes=["dte"])
        p.add("dve", lambda e: e.tensor_tensor(out=w1[:, :], in0=da[:, 0:64], in1=dte[:, :], op=ALU.mult),
              reads=["da", "dte"], writes=["w1"])
        cx.act(cdec[:, :], ct[:, 64:128], AF.Exp, reads=["ct"], writes=["cdec"])
        if own:
            cx.act(din[:, :], ct[:, 0:64], AF.Exp, reads=["ct"], writes=["din"])
        for g in range(8):
            xb_ = rrn(cx, "xtb", 4)
            for i in range(4):
                p.add("pe", lambda e, g=g, i=i, xb_=xb_, xf3=xf3: e.transpose(cx.bank(xb_)[:, i * 128:(i + 1) * 128],
                                                                      xf3[:, 4 * g + i, :], cx.ident_f[:, :]),
                      reads=[("xfm", b), "consts"], writes=[("ps", xb_)])
            if g % 2 == 0:
                cx.act(xtok[:, g * 512:(g + 1) * 512], cx.bank(xb_), AF.Copy, reads=[("ps", xb_)], writes=[("xtok", g)])
            else:
                p.add("dve", lambda e, g=g, xb_=xb_: e.tensor_copy(out=xtok[:, g * 512:(g + 1) * 512], in_=cx.bank(xb_)),
                      reads=[("ps", xb_)], writes=[("xtok", g)])
        pbf = cx.bank(4, 1).bitcast(BF16)
        for g in range(8):
            p.add("pe", lambda e, g=g, bc3=bc3: e.transpose(pbf[:, g * 128:(g + 1) * 128], bc3[:, g, :], cx.ident_bf[:, :]),
                  reads=[("bct", b), "consts"], writes=[("ps", 4)])
        p.add("dve", lambda e: e.tensor_copy(out=Btok[:, :], in_=pbf[:, 0:1024]), reads=[("ps", 4)], writes=["Btok"])
        xkeys = [("xtok", g) for g in range(8)]
        p.add("dve", lambda e: e.tensor_tensor(out=A3(xw, 64, 64), in0=A3(xtok, 64, 64), in1=bc(w1[:, :], [128, 64, 64], 2),
                                                op=ALU.mult), reads=xkeys + ["w1"], writes=["xw"])
        if own:
            p.add("dve", lambda e: e.tensor_tensor(out=A3(xdt, 64, 64), in0=A3(xtok, 64, 64),
                                                    in1=bc(da[:, 0:64], [128, 64, 64], 2), op=ALU.mult),
                  reads=xkeys + ["da"], writes=["xdt"])
            cx.act(Sbf[:, :], S[:, :], AF.Copy, reads=["S"], writes=["Sbf"])
            def s1(g):
                rb = g % 2
                rl = rhsL[rb]
                p.add("dve", lambda e, g=g, rl=rl: e.tensor_tensor(
                    out=A3(rl, 8, 128), in0=bc(da[:, 64 + 8 * g:72 + 8 * g], [128, 8, 128], 2),
                    in1=bc(cx.tri_f[:, :], [128, 8, 128], 1), op=ALU.mult),
                    reads=["da", "consts"], writes=[("rhsL", rb)])
                cbs = cx.bank(5)[:, (g % 4) * 128:(g % 4 + 1) * 128]
                cx.mm(cbs, bc3[:, g, :], bc3[:, 8 + g, :], True, True, reads=[("bct", b)], writes=[("ps", 5)])
                cb_ = cbm[rb]
                p.add("dve", lambda e, cbs=cbs, cb_=cb_: e.tensor_tensor(out=cb_[:, :], in0=cbs, in1=cx.tri_f[:, :], op=ALU.mult),
                      reads=[("ps", 5), "consts"], writes=[("cbm", rb)])

            def s1b(g):
                lb = (g % 2) * 2
                rb = g % 2
                rl = rhsL[rb]
                for hf in range(2):
                    cx.mm(cx.bank(lb + hf), cx.stri_f[:, :], rl[:, hf * 512:(hf + 1) * 512], True, True,
                          reads=[("rhsL", rb), "consts"], writes=[("ps", lb + hf)])
                E_ = Eb[rb]
                cx.act(E_[:, :], cx.bank(lb, 2), AF.Exp, reads=[("ps", lb), ("ps", lb + 1)], writes=[("E", rb)])

            def s2(g):
                rb = g % 2
                E_, M_, cb_ = Eb[rb], Mb[rb], cbm[rb]
                p.add("dve", lambda e, E_=E_, M_=M_, cb_=cb_: e.tensor_tensor(
                    out=A3(M_, 8, 128), in0=A3(E_, 8, 128), in1=bc(cb_[:, :], [128, 8, 128], 1), op=ALU.mult),
                    reads=[("E", rb), ("cbm", rb)], writes=[("M", rb)])
                M3 = A3(M_, 8, 128)
                for e_ in range(8):
                    hh = 8 * g + e_
                    cx.mm(cx.bank(6)[:, e_ * 64:(e_ + 1) * 64], M3[:, e_, :], xdt[:, hh * 64:(hh + 1) * 64], True, True,
                          reads=[("M", rb), "xdt"], writes=[("ps", 6)])
                cx.mm(cx.bank(7), bc3[:, 8 + g, :], Sbf[:, g * 512:(g + 1) * 512], True, True,
                      reads=[("bct", b), "Sbf"], writes=[("ps", 7)])
                tg = tmpg[rb]
                p.add("dve", lambda e, g=g, tg=tg: e.tensor_tensor(
                    out=A3(tg, 8, 64), in0=A3(xtok[:, g * 512:(g + 1) * 512], 8, 64),
                    in1=bc(cx.dskb[:, 8 * g:8 * g + 8], [128, 8, 64], 2), op=ALU.mult),
                    reads=[("xtok", g), "dskb"], writes=[("tmpg", rb)])
                yg = y[:, g * 512:(g + 1) * 512]
                p.add("dve", lambda e, g=g, yg=yg: e.tensor_tensor(
                    out=A3(yg, 8, 64), in0=A3(cx.bank(7), 8, 64), in1=bc(din[:, 8 * g:8 * g + 8], [128, 8, 64], 2),
                    op=ALU.mult), reads=[("ps", 7), "din"], writes=[("y", g)])
                p.add("dve", lambda e, yg=yg: e.tensor_tensor(out=yg, in0=yg, in1=cx.bank(6), op=ALU.add),
                      reads=[("ps", 6), ("y", g)], writes=[("y", g)])
                p.add("dve", lambda e, yg=yg, tg=tg: e.tensor_tensor(out=yg, in0=yg, in1=tg[:, :], op=ALU.add),
                      reads=[("tmpg", rb), ("y", g)], writes=[("y", g)])

            s1(0)
            s1b(0)
            for g in range(8):
                if g + 1 < 8:
                    s1(g + 1)
                s2(g)
                if g + 1 < 8:
                    s1b(g + 1)
        if c < NCH - 1:
            for g in range(8):
                sb_ = 4 if False else (2 + g % 2) if not own else 7
                sb_ = rrn(cx, "stb", 2) + 2 if not own else 7
                cx.mm(cx.bank(sb_), Btok[:, g * 128:(g + 1) * 128], xw[:, g * 512:(g + 1) * 512], True, True,
                      reads=["Btok", "xw"], writes=[("ps", sb_)])
                Sg = S[:, g * 512:(g + 1) * 512]
                p.add("dve", lambda e, g=g, Sg=Sg: e.tensor_tensor(
                    out=A3(Sg, 8, 64), in0=A3(Sg, 8, 64), in1=bc(cdec[:, 8 * g:8 * g + 8], [128, 8, 64], 2), op=ALU.mult),
                    reads=["S", "cdec", "Sbf"], writes=["S"])
                p.add("dve", lambda e, Sg=Sg, sb_=sb_: e.tensor_tensor(out=Sg, in0=Sg, in1=cx.bank(sb_), op=ALU.add),
                      reads=["S", ("ps", sb_)], writes=["S"])
            if c == NCC - 1:
                p.add("dve", lambda e: e.tensor_scalar(out=S[:, :], in0=S[:, :], scalar1=cx.flags[:, 1:2], scalar2=None,
                                                        op0=ALU.mult), reads=["S", "flags"], writes=["S"])
        if c in cx.dbg_chunks:
            cx.dbg("da%d" % c, da[:, :], ["da"])
            cx.dbg("xtok%d" % c, xtok[:, :], [("xtok", g) for g in range(8)])
            cx.dbg("y%d" % c, y[:, :], [("y", g) for g in range(8)])
        if own:
            cx.act(zb[:, :], zb[:, :], AF.Silu, reads=["z"], writes=["z"])
            ykeys = [("y", g) for g in range(8)]
            p.add("dve", lambda e: e.tensor_tensor(out=y[:, :], in0=y[:, :], in1=zb[:, :], op=ALU.mult),
                  reads=ykeys + ["z"], writes=ykeys)
            p.add("dve", lambda e: e.memset(ssq[:, :], 0.0), writes=["ssq"])
            for g in range(8):
                cx.act(junk[:, :], y[:, g * 512:(g + 1) * 512], AF.Square, reads=[("y", g), "ssq"], writes=["junk", "ssq"],
                       accum_out=ssq[:, g:g + 1])
            p.add("dve", lambda e: e.tensor_scalar(out=rs8[:, :], in0=ssq[:, :], scalar1=1.0 / 512.0, scalar2=EPS,
                                                    op0=ALU.mult, op1=ALU.add), reads=["ssq"], writes=["rs8"])
            cx.act(rs8[:, :], rs8[:, :], AF.Sqrt, reads=["rs8"], writes=["rs8"])
            p.add("dve", lambda e: e.reciprocal(out=rs8[:, :], in_=rs8[:, :]), reads=["rs8"], writes=["rs8"])
            for g in range(8):
                p.add("dve", lambda e, g=g: e.scalar_tensor_tensor(
                    out=yn[:, g * 512:(g + 1) * 512], in0=y[:, g * 512:(g + 1) * 512], scalar=rs8[:, g:g + 1],
                    in1=ngb[:, g * 512:(g + 1) * 512], op0=ALU.mult, op1=ALU.mult),
                    reads=[("y", g), "rs8", "ngb"], writes=["yn"])
            for q4 in range(4):
                tb_ = rrn(cx, "xtb", 4)
                pv = cx.bank(tb_).bitcast(BF16)
                for i in range(8):
                    cc = q4 * 8 + i
                    p.add("pe", lambda e, pv=pv, i=i, cc=cc: e.transpose(pv[:, i * 128:(i + 1) * 128],
                                                                         yn[:, cc * 128:(cc + 1) * 128], cx.ident_bf[:, :]),
                          reads=["yn", "consts"], writes=[("ps", tb_)])
                if q4 % 2 == 0:
                    cx.act(oTs[:, q4 * 1024:(q4 + 1) * 1024], pv[:, 0:1024], AF.Copy, reads=[("ps", tb_)], writes=["oTs"])
                else:
                    p.add("dve", lambda e, pv=pv, q4=q4: e.tensor_copy(out=oTs[:, q4 * 1024:(q4 + 1) * 1024], in_=pv[:, 0:1024]),
                          reads=[("ps", tb_)], writes=["oTs"])
            cx.store(cx.ossmT[:, tl0:tl0 + 128].rearrange("(c p) t -> p c t", p=128), A3(oTs, 32, 128), reads=["oTs"],
                     writes=[("scr", "ossm")], key="oTs", eng="pool")


def merge(cx, W, ntok_ctx, ntok_own, T=1024):
    p = cx.p
    NH = T // 512
    rT = cx.alloc(48 * T, BF16).rearrange("p (k t) -> p k t", k=48)
    mT = cx.alloc(KC * T, BF16).rearrange("p (k t) -> p k t", k=KC)
    cx.wbuf = [cx.alloc(48 * 128, BF16) for _ in range(2)]
    cx.xbuf = [cx.alloc(T, F32) for _ in range(3)]
    cx.sqbuf = [cx.alloc(T, BF16) for _ in range(2)]
    gb = [cx.alloc(2 * T, F32) for _ in range(2)]
    cx.ssbank = 8 - NH
    for t0 in range(0, ntok_own, T):
        cx.load(rT[:, 0:16, :], cx.oattT[:, t0:t0 + T].rearrange("(c p) t -> p c t", p=128), reads=[],
                writes=["rT"], key="rTa")
        cx.load(rT[:, 16:48, :], cx.ossmT[:, t0:t0 + T].rearrange("(c p) t -> p c t", p=128), reads=[],
                writes=["rT"], key="rTb")
        for dc in range(KC):
            wb = rrn(cx, "wb", 2)
            w = cx.wbuf[wb]
            wload(cx, w[:, 0:2048], W["wba"][dc], [], [("wb", wb, 0)], "wb%d" % wb)
            wload(cx, w[:, 2048:6144], W["wbs"][dc], [], [("wb", wb, 1)], "wb%d" % wb)
            g_ = rrn(cx, "gb", 2)
            gt = gb[g_]
            cx.load(gt[:, 0:T], cx.gT[dc * 128:(dc + 1) * 128, t0:t0 + T], reads=[], writes=[("gb", g_)], key="gb%d" % g_)
            cx.load(gt[:, T:2 * T], cx.gT[D + dc * 128:D + (dc + 1) * 128, t0:t0 + T], reads=[], writes=[("gb", g_)],
                    key="gb%d" % g_)
            pa = rrn(cx, "mpb", 2) * 2 * NH
            psa = [("ps", pa + h) for h in range(NH)]
            pss = [("ps", pa + NH + h) for h in range(NH)]
            for kc in range(16):
                for h in range(NH):
                    cx.mm(cx.bank(pa + h), w[:, kc * 128:(kc + 1) * 128], rT[:, kc, h * 512:(h + 1) * 512], kc == 0, kc == 15,
                          reads=[("wb", wb, 0), "rT"], writes=[("ps", pa + h)])
            for kc in range(32):
                for h in range(NH):
                    cx.mm(cx.bank(pa + NH + h), w[:, 2048 + kc * 128:2048 + (kc + 1) * 128], rT[:, 16 + kc, h * 512:(h + 1) * 512],
                          kc == 0, kc == 31, reads=[("wb", wb, 1), "rT"], writes=[("ps", pa + NH + h)])
            p.add("dve", lambda e, gt=gt, pa=pa: e.tensor_tensor(out=gt[:, 0:T], in0=gt[:, 0:T], in1=cx.bank(pa, NH), op=ALU.mult),
                  reads=[("gb", g_)] + psa, writes=[("gb", g_)])
            p.add("dve", lambda e, gt=gt, pa=pa: e.tensor_tensor(out=gt[:, T:2 * T], in0=gt[:, T:2 * T],
                                                                  in1=cx.bank(pa + NH, NH), op=ALU.mult),
                  reads=[("gb", g_)] + pss, writes=[("gb", g_)])
            p.add("dve", lambda e, gt=gt, dc=dc: e.tensor_tensor(out=mT[:, dc, :], in0=gt[:, 0:T], in1=gt[:, T:2 * T], op=ALU.add),
                  reads=[("gb", g_)], writes=[("mT", dc)])
        for dc in range(KC):
            wb = rrn(cx, "wb", 2)
            w = cx.wbuf[wb]
            wload(cx, w[:, 0:2048], W["wo"][dc], [], [("wb", wb, 0), ("wb", wb, 1)], "wb%d" % wb)
            pa = rrn(cx, "mob", 2) * NH
            pk = [("ps", pa + h) for h in range(NH)]
            for kc in range(16):
                for h in range(NH):
                    cx.mm(cx.bank(pa + h), w[:, kc * 128:(kc + 1) * 128], mT[:, kc, h * 512:(h + 1) * 512], kc == 0, kc == 15,
                          reads=[("wb", wb, 0), ("wb", wb, 1), ("mT", kc)], writes=[("ps", pa + h)])
            b = rrn(cx, "xb", 3)
            ft = cx.xbuf[b]
            cx.act(ft[:, 0:T], cx.bank(pa, NH), AF.Copy, reads=pk, writes=[("xb", b)])
            sb_ = rrn(cx, "sq", 2)
            sq = cx.sqbuf[sb_]
            cx.act(sq[:, 0:T], cx.bank(pa, NH), AF.Square, reads=pk, writes=[("sq", sb_)])
            cx.store(cx.fscr[dc * 128:(dc + 1) * 128, 0:T], ft[:, 0:T], reads=[("xb", b)], writes=[("fs", "f", dc)],
                     key="xb%d" % b)
            for h in range(NH):
                cx.mm(cx.bank(cx.ssbank + h), cx.ones[:, :], sq[:, h * 512:(h + 1) * 512], dc == 0, dc == KC - 1,
                      reads=[("sq", sb_), "ones"], writes=[("ps", cx.ssbank + h)])
        rstd_from_ss(cx, T, D)
        combine_residual(cx, cx.fscr, cx.h1, cx.h2, t0, T, cx.gv("mix_post_g"), "fs", "h1o", "h2", half=False,
                         xoff=ntok_ctx)


VEC_NAMES = ["ffn1_pre_g", "ffn1_post_g", "mix_pre_g", "mix_post_g", "ffn2_pre_g", "ffn2_post_g"]
NVEC = len(VEC_NAMES) * KC
W_SHAPES = {
    "w1g": [FC, 128, KC * 128], "w1u": [FC, 128, KC * 128], "w1d": [KC, 128, FC * 128],
    "w2g": [FC, 128, KC * 128], "w2u": [FC, 128, KC * 128], "w2d": [KC, 128, FC * 128],
    "wq": [16, 128, 2048], "wk": [16, 128, 2048], "wxbc": [48, 128, 2048], "wdt": [1, 128, KC * 64],
    "wg": [32, 128, 2048], "wv": [4, 128, KC * 512], "wz": [8, 128, KC * 512],
    "wba": [16, 128, 2048], "wbs": [16, 128, 4096], "wo": [16, 128, 2048],
}
SM_CONVW = 0
SM_CONVB = 192
SM_SSMV = 240
SM_LAM = 243
SM_SUBG = 247
SM_FLAGS = 249
SM_DSK = 251
NSM = 315
ALL_STAGES = ("ffn1", "inproj", "attn", "conv", "ssd", "merge", "ffn2")


def build(T=1024, stages=ALL_STAGES, ntok_ctx=HALF, ntok_own=HALF, debug=(), ext_in=(), dbg_chunks=()):
    nc = bass.Bass("TRN2", target_bir_lowering=False)
    NT = ntok_ctx + ntok_own

    def din(name, shape, dt=F32):
        return nc.dram_tensor(name, shape, dt, kind="ExternalInput").ap()

    def scr(name, shape, dt=F32):
        kind = "ExternalOutput" if name in debug else ("ExternalInput" if name in ext_in else "Internal")
        return nc.dram_tensor(name, shape, dt, kind=kind).ap()

    xT = din("xT", [D, NT])
    vecs_d = din("vecs", [128, NVEC])
    sm_d = din("smalls", [128, NSM])
    consts_d = din("consts", [128, 4 * 128])
    ngb_d = din("ngb", [128, DIN])
    W = {k: din(k, v) for k, v in W_SHAPES.items()}
    outT = nc.dram_tensor("outT", [D, ntok_own], F32, kind="ExternalOutput").ap()
    with ExitStack() as stack:
        cx = Ctx(nc, stack)
        p = cx.p
        cx.ngb_d = ngb_d
        cx.dbg_chunks = dbg_chunks

        def dbg(name, ap, keys):
            t = nc.dram_tensor("dbg_" + name, list(ap.shape), ap.dtype, kind="ExternalOutput").ap()
            cx.store(t, ap, reads=keys, writes=[("dbg", name)], key="dbg_" + name)
        cx.dbg = dbg
        cx.h1 = scr("h1", [D, NT])
        cx.h2 = scr("h2", [D, ntok_own])
        cx.fscr = scr("fscr", [D, T])
        cx.qT = scr("qT", [NQK, ntok_own], BF16)
        cx.kT = scr("kT", [NQK, NT], BF16)
        cx.V = scr("V", [NT, NV], BF16)
        cx.Z = scr("Z", [ntok_own, DIN])
        cx.xbcT = scr("xbcT", [6144, 4 + NT])
        cx.dtT = scr("dtT", [64, NT])
        cx.gT = scr("gT", [2 * D, ntok_own])
        cx.xcT = scr("xcT", [DIN, NT])
        cx.bcT = scr("bcT", [2 * NBC, NT], BF16)
        cx.oattT = scr("oattT", [NV, ntok_own], BF16)
        cx.ossmT = scr("ossmT", [DIN, ntok_own], BF16)
        wsc = (scr("wsc_g", [FC, 128, KC * 128], BF16), scr("wsc_u", [FC, 128, KC * 128], BF16),
               scr("wsc_d", [KC, 128, FC * 128], BF16))
        cx.vecs = cx.sb("vecsb", [128, NVEC], F32)
        sm = cx.sb("smalls_sb", [128, NSM], F32)
        cf = cx.sb("consts_sb", [128, 4 * 128], F32)
        cb16 = cx.sb("consts_bf", [128, 3 * 128], BF16)
        cx.rstd = cx.sb("rstd", [128, 1024], F32)
        cx.neglam = cx.sb("neglam", [128, 4], F32)
        cx.subg = cx.sb("subg", [128, 2], F32)
        zero = cx.sb("zero", [128, 48 * 4], F32)

        cx.arena = cx.sb("arena", [128, 98 * 1024], BF16)
        cx.ident_f, cx.ones_f, cx.tri_f, cx.stri_f = (cf[:, i * 128:(i + 1) * 128] for i in range(4))
        cx.ident_bf, cx.ones, cx.tri_bf = (cb16[:, i * 128:(i + 1) * 128] for i in range(3))
        cx.convw = sm[:, SM_CONVW:SM_CONVW + 192]
        cx.convb = sm[:, SM_CONVB:SM_CONVB + 48]
        cx.ssmv = sm[:, SM_SSMV:SM_SSMV + 3]
        cx.flags = sm[:, SM_FLAGS:SM_FLAGS + 2]
        cx.dskb = sm[:, SM_DSK:SM_DSK + 64]
        cx.negb = cx.sb("negb", [128, 48], F32)
        p.add("dve", lambda e: e.tensor_scalar(out=cx.negb[:, :], in0=sm[:, SM_CONVB:SM_CONVB + 48], scalar1=-1.0,
                                                scalar2=None, op0=ALU.mult), reads=["convw"], writes=["negb"])
        cx.gv = lambda name: cx.vecs[:, VEC_NAMES.index(name) * KC:(VEC_NAMES.index(name) + 1) * KC]
        cx.load(cx.vecs[:, :], vecs_d, reads=[], writes=["vecs"], key="vecs")
        cx.load(sm[:, :], sm_d, reads=[], writes=["convw", "ssmv", "flags", "dskb", "smraw"], key="sm")
        cx.load(cf[:, :], consts_d, reads=[], writes=["constsf"], key="cf")
        p.add("dve", lambda e: e.tensor_copy(out=cb16[:, :], in_=cf[:, 0:384]), reads=["constsf"], writes=["consts", "ones"])
        p.add("dve", lambda e: e.memset(zero[:, :], 0.0), writes=["zero"])
        cx.store(cx.xbcT[:, 0:4].rearrange("(c p) t -> p c t", p=128), zero[:, :].rearrange("p (c t) -> p c t", c=48),
                 reads=["zero"], writes=[("scr", "pad")], key="zero")
        cx.act(sm[0:64, SM_SSMV + 2:SM_SSMV + 3], sm[0:64, SM_SSMV + 1:SM_SSMV + 2], AF.Exp, reads=["ssmv"], writes=["ssmv"])
        p.add("dve", lambda e: e.tensor_scalar(out=sm[0:64, SM_SSMV + 2:SM_SSMV + 3], in0=sm[0:64, SM_SSMV + 2:SM_SSMV + 3],
                                                scalar1=-1.0, scalar2=None, op0=ALU.mult), reads=["ssmv"], writes=["ssmv"])
        lam = sm[:, SM_LAM:SM_LAM + 4]
        nl = cx.neglam
        p.add("dve", lambda e: e.tensor_tensor(out=nl[:, 0:1], in0=lam[:, 0:1], in1=lam[:, 1:2], op=ALU.mult),
              reads=["smraw"], writes=["nl"])
        p.add("dve", lambda e: e.tensor_tensor(out=nl[:, 1:2], in0=lam[:, 2:3], in1=lam[:, 3:4], op=ALU.mult),
              reads=["smraw", "nl"], writes=["nl"])
        cx.mm(cx.bank(0)[:, 0:2], cx.ones_f, nl[:, 0:2], True, True, reads=["nl", "constsf"], writes=[("ps", 0)])
        cx.act(nl[:, 2:4], cx.bank(0)[:, 0:2], AF.Exp, reads=[("ps", 0)], writes=["nl2"])
        p.add("dve", lambda e: e.scalar_tensor_tensor(out=nl[:, 0:1], in0=nl[:, 3:4], scalar=-0.2, in1=nl[:, 2:3],
                                                       op0=ALU.add, op1=ALU.subtract), reads=["nl2", "nl"], writes=["neglam"])
        p.add("dve", lambda e: e.tensor_scalar(out=cx.subg[:, :], in0=sm[:, SM_SUBG:SM_SUBG + 2], scalar1=0.8, scalar2=None,
                                                op0=ALU.mult), reads=["smraw"], writes=["subg"])
        cx.amark = 0

        if "ffn1" in stages:
            cx.phase()
            layout_ffn(cx, T)
            ffn(cx, xT, cx.h1, NT, T, W["w1g"], W["w1u"], W["w1d"], cx.gv("ffn1_pre_g"), cx.gv("ffn1_post_g"), "x", "h1",
                wsc=wsc)
        if "inproj" in stages or "conv" in stages:
            cx.phase()
            side = ConvSide(cx, ntok_ctx, ntok_own, T) if "conv" in stages else None
            if "inproj" in stages:
                inproj(cx, W, ntok_ctx, ntok_own, T, side=side)
            if side is not None:
                side.drain()
        if "attn" in stages:
            cx.phase()
            attention(cx, ntok_ctx, ntok_own)
        if "ssd" in stages:
            cx.phase()
            ssd(cx, ntok_ctx, ntok_own)
        if "merge" in stages:
            cx.phase()
            merge(cx, W, ntok_ctx, ntok_own)
        if "ffn2" in stages:
            cx.phase()
            layout_ffn(cx, T)
            ffn(cx, cx.h2, outT, ntok_own, T, W["w2g"], W["w2u"], W["w2d"], cx.gv("ffn2_pre_g"), cx.gv("ffn2_post_g"),
                "h2", "out", wsc=wsc)
        cx.phase()
        p.finalize(stack)
    return nc


def tile_w(W):
    K, N = W.shape
    return np.ascontiguousarray(
        W.reshape(K // 128, 128, N // 128, 128).transpose(2, 1, 0, 3).reshape(N // 128, 128, K))


def tile_w_rows(W, rows):
    K, N = W.shape
    return np.ascontiguousarray(
        W.reshape(K // 128, 128, N // rows, rows).transpose(2, 1, 0, 3).reshape(N // rows, 128, (K // 128) * rows))


def col_vec(v):
    return np.ascontiguousarray(v.reshape(-1, 128).T)


def host_weights(inp):
    w_in = inp["w_in"][0]
    o = 0
    sl = {}
    for name, n in (("q", 2048), ("k", 2048), ("v", 2048), ("z", 4096), ("xbc", 6144), ("dt", 64), ("g", 4096)):
        sl[name] = w_in[:, o:o + n]
        o += n
    Wd = {
        "w1g": tile_w(inp["ffn1_w_gate"][0]), "w1u": tile_w(inp["ffn1_w_up"][0]), "w1d": tile_w(inp["ffn1_w_down"][0]),
        "w2g": tile_w(inp["ffn2_w_gate"][0]), "w2u": tile_w(inp["ffn2_w_up"][0]), "w2d": tile_w(inp["ffn2_w_down"][0]),
        "wq": tile_w(sl["q"]), "wk": tile_w(sl["k"]), "wxbc": tile_w(sl["xbc"]), "wdt": tile_w_rows(sl["dt"], 64),
        "wg": tile_w(sl["g"]), "wv": tile_w_rows(sl["v"], 512), "wz": tile_w_rows(sl["z"], 512),
        "wba": tile_w(inp["w_branch_att"][0]), "wbs": tile_w(inp["w_branch_ssm"][0]), "wo": tile_w(inp["w_out"][0]),
    }
    vecs = np.concatenate([col_vec(inp[n][0]) for n in VEC_NAMES], axis=1).astype(np.float32)
    sm = np.zeros((128, NSM), np.float32)
    cw = inp["ssm_conv_w"][0]
    sm[:, SM_CONVW:SM_CONVW + 192] = cw.reshape(4, 48, 128).transpose(2, 1, 0).reshape(128, 192)
    sm[:, SM_CONVB:SM_CONVB + 48] = col_vec(inp["ssm_conv_b"][0])
    sm[0:64, SM_SSMV] = inp["ssm_dt_bias"][0]
    sm[0:64, SM_SSMV + 1] = inp["ssm_a_log"][0]
    for i, n in enumerate(("att_lambda_q1", "att_lambda_k1", "att_lambda_q2", "att_lambda_k2")):
        sm[:, SM_LAM + i] = inp[n][0]
    sm[:, SM_SUBG:SM_SUBG + 2] = col_vec(inp["att_subln_g"][0])
    sm[:, SM_DSK:SM_DSK + 64] = inp["ssm_d"][0][None, :]
    ngb = np.ascontiguousarray(np.broadcast_to(inp["ssm_norm_g"][0][None, :], (128, DIN))).astype(np.float32)
    idx = np.arange(128)
    consts = np.concatenate([
        np.eye(128), np.ones((128, 128)),
        (idx[:, None] <= idx[None, :]).astype(np.float64),
        (idx[:, None] > idx[None, :]).astype(np.float64),
    ], axis=1).astype(np.float32)
    return Wd, vecs, sm, ngb, consts


def core_maps(inp, ntok_ctx=HALF, ntok_own=HALF, cores=range(8)):
    Wd, vecs, sm, ngb, consts = host_weights(inp)
    maps = []
    for c in cores:
        b, r = c // 2, c % 2
        x = inp["x"][b]
        xc = x[0:ntok_ctx]
        xo = x[r * ntok_own + (ntok_ctx if False else 0):][:ntok_own] if r == 0 else x[ntok_ctx:ntok_ctx + ntok_own]
        xT = np.ascontiguousarray(np.concatenate([xc, xo], axis=0).T)
        smc = sm.copy()
        smc[:, SM_FLAGS] = 0.0 if r == 1 else -30000.0
        smc[:, SM_FLAGS + 1] = 1.0 if r == 1 else 0.0
        m = {"xT": xT, "vecs": vecs, "smalls": smc, "consts": consts, "ngb": ngb}
        m.update(Wd)
        maps.append(m)
    return maps


_NC_CACHE = {}


def kernel(**inputs):
    inp = {k: np.asarray(v) for k, v in inputs.items()}
    if "nc" not in _NC_CACHE:
        _NC_CACHE["nc"] = build()
    nc = _NC_CACHE["nc"]
    maps = core_maps(inp)
    res = run_bass_kernel_spmd(nc, maps, core_ids=list(range(8)))
    out = np.empty((4, SEQ, D), np.float32)
    for c in range(8):
        b, r = c // 2, c % 2
        out[b, r * HALF:(r + 1) * HALF, :] = res.results[c]["outT"].T
    return out
```

```python
import math
from contextlib import ExitStack
import numpy as np
import concourse.bass as bass
import concourse.mybir as mybir
from concourse.bass_utils import run_bass_kernel_spmd

F32 = mybir.dt.float32
BF16 = mybir.dt.bfloat16
F32R = mybir.dt.float32r
AF = mybir.ActivationFunctionType
ALU = mybir.AluOpType
AX = mybir.AxisListType

D = 2048
KC = 16
DFF = 5632
FC = 44
SEQ = 4096
HALF = 2048
NQK = 2048
NV = 2048
DIN = 4096
NBC = 1024
NHEAD = 64
EPS = 1e-6


class _Op:
    __slots__ = ("eng", "emit", "deps", "sem", "val", "signal", "idx")


class Prog:
    ENGS = ("pe", "act", "dve", "pool", "sp")

    def __init__(self, nc):
        self.nc = nc
        self.ops = []
        self.ks = {}
        self.dma_cnt = {}

    def add(self, eng, emit, reads=(), writes=(), dma=None):
        i = len(self.ops)
        op = _Op()
        op.eng, op.emit, op.idx, op.signal = eng, emit, i, False
        if dma is not None:
            c = self.dma_cnt.get(dma, 0) + 1
            self.dma_cnt[dma] = c
            op.sem, op.val = ("dma", dma), 16 * c
        else:
            op.sem, op.val = eng, None
        deps = set()
        ks = self.ks
        for k in reads:
            st = ks.get(k)
            if st is not None and st[0] is not None:
                deps.add(st[0])
        for k in writes:
            st = ks.get(k)
            if st is not None:
                if st[0] is not None:
                    deps.add(st[0])
                deps.update(st[1].values())
        for k in reads:
            st = ks.get(k)
            if st is None:
                st = ks[k] = [None, {}]
            st[1][op.sem] = i
        for k in writes:
            st = ks.get(k)
            if st is None:
                st = ks[k] = [None, {}]
            st[0] = i
            st[1] = {}
        deps.discard(i)
        op.deps = deps
        self.ops.append(op)
        return i

    def barrier(self):
        deps = set()
        for st in self.ks.values():
            if st[0] is not None:
                deps.add(st[0])
            deps.update(st[1].values())
        for e in self.ENGS:
            op = _Op()
            op.eng, op.emit, op.idx, op.signal = e, (lambda _e: None), len(self.ops), False
            op.sem, op.val = e, None
            op.deps = set(deps)
            self.ops.append(op)

    def finalize(self, stack):
        nc = self.nc
        ops = self.ops
        for op in ops:
            for j in op.deps:
                d = ops[j]
                if d.eng == "pe" and op.eng == "pe" and d.sem == "pe" and op.sem == "pe":
                    continue
                d.signal = True
        cnt = {e: 0 for e in self.ENGS}
        for op in ops:
            if op.sem in cnt and op.signal:
                cnt[op.sem] += 1
                op.val = cnt[op.sem]
        sems = {}
        for e in self.ENGS:
            sems[e] = stack.enter_context(nc.semaphore("s_" + e))
        for k in self.dma_cnt:
            sems[("dma", k)] = stack.enter_context(nc.semaphore("d_" + str(k)))
        per = {e: [] for e in self.ENGS}
        for op in ops:
            per[op.eng].append(op)

        def run(eng_name, e):
            waited = {}
            for op in per[eng_name]:
                need = {}
                for j in op.deps:
                    d = ops[j]
                    if d.eng == "pe" and op.eng == "pe" and d.sem == "pe" and op.sem == "pe":
                        continue
                    if need.get(d.sem, 0) < d.val:
                        need[d.sem] = d.val
                for s, v in need.items():
                    if waited.get(s, 0) < v:
                        e.wait_ge(sems[s], v)
                        waited[s] = v
                ins = op.emit(e)
                if ins is not None:
                    if op.sem == eng_name:
                        if op.signal:
                            ins.then_inc(sems[eng_name], 1)
                    else:
                        ins.then_inc(sems[op.sem], 16)

        block = stack.enter_context(nc.Block())

        @block.tensor
        def _(e):
            run("pe", e)

        @block.scalar
        def _(e):
            run("act", e)

        @block.vector
        def _(e):
            run("dve", e)

        @block.gpsimd
        def _(e):
            run("pool", e)

        @block.sync
        def _(e):
            run("sp", e)


class Ctx:
    def __init__(self, nc, stack):
        self.nc = nc
        self.stack = stack
        self.p = Prog(nc)
        self.uid = 0
        self.psum = stack.enter_context(nc.psum_tensor("psum", [128, 4096], F32))
        self.rr = {}
        self.arena = None
        self.apos = 0
        self.amark = 0

    def alloc(self, cols, dt):
        n = cols * (2 if dt == F32 else 1)
        n = (n + 15) // 16 * 16
        a = self.arena[:, self.apos:self.apos + n]
        self.apos += n
        assert self.apos <= self.arena.shape[1], ("arena overflow", self.apos)
        if dt == F32:
            a = a.bitcast(F32)
        return a[:, 0:cols]

    def phase(self):
        self.p.barrier()
        self.apos = self.amark
        self.rr = {}

    def sb(self, name, shape, dt):
        return self.stack.enter_context(self.nc.sbuf_tensor(name, shape, dt))

    def dram(self, name, shape, dt, kind="Internal"):
        return self.nc.dram_tensor(name, shape, dt, kind=kind).ap()

    def bank(self, b, n=1):
        return self.psum[:, b * 512:(b + n) * 512]

    def mm(self, out, lhsT, rhs, start, stop, reads, writes):
        self.p.add("pe", lambda e: e.matmul(out, lhsT, rhs, start=start, stop=stop),
                   reads=reads, writes=writes)

    def act(self, out, in_, func, reads, writes, bias=None, scale=None, accum_out=None):
        kw = {}
        if bias is not None:
            kw["bias"] = bias
        if scale is not None:
            kw["scale"] = scale
        if accum_out is not None:
            kw["accum_out"] = accum_out
        self.p.add("act", lambda e: e.activation(out, in_, func, **kw), reads=reads, writes=writes)

    def load(self, out, in_, reads, writes, key, eng="sp"):
        self.p.add(eng, lambda e: e.dma_start(out=out, in_=in_), reads=reads, writes=writes, dma=key)

    def store(self, out, in_, reads, writes, key, eng="sp"):
        self.p.add(eng, lambda e: e.dma_start(out=out, in_=in_), reads=reads, writes=writes, dma=key)


def norm_gen(cx, src, t0, T, gcol, uT, ukey, tag, ssb=None, rstd=None, rkey="rstd"):
    p = cx.p
    NH = T // 512
    xb = cx.xbuf
    ssb = cx.ssbank if ssb is None else ssb
    rstd = cx.rstd if rstd is None else rstd
    for ps in range(2):
        for kc in range(KC):
            b = cx.rr["xb"] = (cx.rr.get("xb", -1) + 1) % len(xb)
            xt = xb[b]
            cx.load(xt[:, 0:T], src[kc * 128:(kc + 1) * 128, t0:t0 + T], reads=[(tag, "src", kc)],
                    writes=[("xb", b)], key="xb%d" % b)
            if ps == 0:
                sb_ = cx.rr["sq"] = (cx.rr.get("sq", -1) + 1) % 2
                sq = cx.sqbuf[sb_]
                cx.act(sq[:, 0:T], xt[:, 0:T], AF.Square, reads=[("xb", b)], writes=[("sq", sb_)])
                for h in range(NH):
                    cx.mm(cx.bank(ssb + h), cx.ones[:, :], sq[:, h * 512:(h + 1) * 512],
                          start=(kc == 0), stop=(kc == KC - 1),
                          reads=[("sq", sb_), "ones"], writes=[("ps", ssb + h)])
            else:
                o = uT[:, kc, 0:T]
                g = gcol[:, kc:kc + 1]
                r = rstd[:, 0:T]
                p.add("dve", lambda e, o=o, xt=xt, g=g, r=r, T=T: e.scalar_tensor_tensor(
                    out=o, in0=xt[:, 0:T], scalar=g, in1=r, op0=ALU.mult, op1=ALU.mult),
                    reads=[("xb", b), rkey, "vecs"], writes=[(ukey, kc)])
            yield
        if ps == 0:
            rstd_from_ss(cx, T, D, ssb, rstd, rkey)
            yield


def norm_to_uT(cx, src, t0, T, gcol, uT, ukey, tag, **kw):
    for _ in norm_gen(cx, src, t0, T, gcol, uT, ukey, tag, **kw):
        pass


def rstd_from_ss(cx, T, n, ssb=None, rstd=None, rkey="rstd"):
    NH = T // 512
    ssb = cx.ssbank if ssb is None else ssb
    rstd = cx.rstd if rstd is None else rstd
    r = rstd[:, 0:T]
    ss = cx.bank(ssb, NH)
    cx.p.add("dve", lambda e: e.tensor_scalar(out=r, in0=ss, scalar1=1.0 / n, scalar2=EPS,
                                               op0=ALU.mult, op1=ALU.add),
             reads=[("ps", ssb + h) for h in range(NH)], writes=[rkey])
    cx.act(r, r, AF.Sqrt, reads=[rkey], writes=[rkey])
    cx.p.add("dve", lambda e: e.reciprocal(out=r, in_=r), reads=[rkey], writes=[rkey])


def wload(cx, dst, src, key_r, key_w, semkey):
    cx.load(dst, src, reads=key_r, writes=key_w, key=semkey, eng="pool")


def combine_gen(cx, fsrc, xsrc, dst, t0, T, gcol, tag, xtag, otag, half, xoff=0, rstd=None, rkey="rstd"):
    p = cx.p
    xb = cx.xbuf
    rstd = cx.rstd if rstd is None else rstd
    for dc in range(KC):
        b1 = cx.rr["xb"] = (cx.rr.get("xb", -1) + 1) % len(xb)
        ft = xb[b1]
        cx.load(ft[:, 0:T], fsrc[dc * 128:(dc + 1) * 128, 0:T], reads=[(tag, "f", dc)],
                writes=[("xb", b1)], key="xb%d" % b1)
        b2 = cx.rr["xb"] = (cx.rr.get("xb", -1) + 1) % len(xb)
        xt = xb[b2]
        cx.load(xt[:, 0:T], xsrc[dc * 128:(dc + 1) * 128, xoff + t0:xoff + t0 + T], reads=[(xtag, "src", dc)],
                writes=[("xb", b2)], key="xb%d" % b2)
        g = gcol[:, dc:dc + 1]
        r = rstd[:, 0:T]
        p.add("dve", lambda e, ft=ft, g=g, r=r: e.scalar_tensor_tensor(
            out=ft[:, 0:T], in0=ft[:, 0:T], scalar=g, in1=r, op0=ALU.mult, op1=ALU.mult),
            reads=[("xb", b1), rkey, "vecs"], writes=[("xb", b1)])
        if half:
            p.add("dve", lambda e, ft=ft, xt=xt: e.scalar_tensor_tensor(
                out=ft[:, 0:T], in0=ft[:, 0:T], scalar=0.5, in1=xt[:, 0:T], op0=ALU.mult, op1=ALU.add),
                reads=[("xb", b1), ("xb", b2)], writes=[("xb", b1)])
        else:
            p.add("dve", lambda e, ft=ft, xt=xt: e.tensor_tensor(
                out=ft[:, 0:T], in0=ft[:, 0:T], in1=xt[:, 0:T], op=ALU.add),
                reads=[("xb", b1), ("xb", b2)], writes=[("xb", b1)])
        cx.store(dst[dc * 128:(dc + 1) * 128, t0:t0 + T], ft[:, 0:T], reads=[("xb", b1)],
                 writes=[(otag, "src", dc)], key="xb%d" % b1)
        yield


def combine_residual(*a, **kw):
    for _ in combine_gen(*a, **kw):
        pass


def ffn(cx, src, dst, ntok, T, wg, wu, wd, gpre, gpost, tag, otag, wsc=None):
    p = cx.p
    NH = T // 512
    uT = cx.uT
    hid = cx.hid
    WB = cx.wbuf
    tiles = list(range(0, ntok, T))
    PSB = 4 if NH == 2 else 6
    norm_to_uT(cx, src, tiles[0], T, gpre, uT, "uT", tag, ssb=PSB, rstd=cx.rstdA, rkey="rstdA")
    comb = None
    for ti, t0 in enumerate(tiles):
        for j in range(FC):
            wb = cx.rr["wb"] = (cx.rr.get("wb", -1) + 1) % len(WB)
            w = WB[wb]
            reuse = wsc is not None and len(tiles) > 1
            if reuse and ti > 0:
                cx.load(w[:, 0:KC * 128], wsc[0][j], reads=[(tag, "wscg", j)], writes=[("wb", wb, 0)], key="wb%d" % wb, eng="pool")
                cx.load(w[:, KC * 128:2 * KC * 128], wsc[1][j], reads=[(tag, "wscu", j)], writes=[("wb", wb, 1)],
                        key="wb%d" % wb, eng="pool")
            else:
                wload(cx, w[:, 0:KC * 128], wg[j], [], [("wb", wb, 0)], "wb%d" % wb)
                wload(cx, w[:, KC * 128:2 * KC * 128], wu[j], [], [("wb", wb, 1)], "wb%d" % wb)
            gb = cx.rr["gub"] = (cx.rr.get("gub", -1) + 1) % 2
            gbank = gb * 2 * NH
            ubank = gbank + NH
            for which, bank0 in ((0, gbank), (1, ubank)):
                for kc in range(KC):
                    lw = w[:, which * KC * 128 + kc * 128: which * KC * 128 + (kc + 1) * 128]
                    for h in range(NH):
                        cx.mm(cx.bank(bank0 + h), lw, uT[:, kc, h * 512:(h + 1) * 512],
                              start=(kc == 0), stop=(kc == KC - 1),
                              reads=[("wb", wb, which), ("uT", kc)], writes=[("ps", bank0 + h)])
            sl = cx.rr["sil"] = (cx.rr.get("sil", -1) + 1) % 2
            st = cx.silb[sl]
            cx.act(st[:, 0:T], cx.bank(gbank, NH), AF.Silu,
                   reads=[("ps", gbank + h) for h in range(NH)], writes=[("sil", sl)])
            if reuse and ti == 0:
                cx.store(wsc[0][j], w[:, 0:KC * 128], reads=[("wb", wb, 0)], writes=[(tag, "wscg", j)],
                         key="wbs%d" % wb, eng="act")
                cx.store(wsc[1][j], w[:, KC * 128:2 * KC * 128], reads=[("wb", wb, 1)], writes=[(tag, "wscu", j)],
                         key="wbs%d" % wb, eng="act")
            ho = hid[:, j, 0:T]
            ub = cx.bank(ubank, NH)
            p.add("dve", lambda e, ho=ho, st=st, ub=ub: e.tensor_tensor(
                out=ho, in0=st[:, 0:T], in1=ub, op=ALU.mult),
                reads=[("sil", sl)] + [("ps", ubank + h) for h in range(NH)],
                writes=[("hid", j)])
            if comb is not None and j % 2 == 1:
                next(comb, None)
        if comb is not None:
            for _ in comb:
                pass
        nxt = None
        if ti + 1 < len(tiles):
            nxt = norm_gen(cx, src, tiles[ti + 1], T, gpre, uT, "uT", tag, ssb=PSB, rstd=cx.rstdA, rkey="rstdA")
        fscr = cx.fscr
        for dc in range(KC):
            wb = cx.rr["wb"] = (cx.rr.get("wb", -1) + 1) % len(WB)
            w = WB[wb]
            reuse = wsc is not None and len(tiles) > 1
            if reuse and ti > 0:
                cx.load(w[:, 0:FC * 128], wsc[2][dc], reads=[(tag, "wscd", dc)], writes=[("wb", wb, 0), ("wb", wb, 1)],
                        key="wb%d" % wb, eng="pool")
            else:
                wload(cx, w[:, 0:FC * 128], wd[dc], [], [("wb", wb, 0), ("wb", wb, 1)], "wb%d" % wb)
            db = cx.rr["dnb"] = (cx.rr.get("dnb", -1) + 1) % 2
            dbank = db * NH
            for fc in range(FC):
                for h in range(NH):
                    cx.mm(cx.bank(dbank + h), w[:, fc * 128:(fc + 1) * 128], hid[:, fc, h * 512:(h + 1) * 512],
                          start=(fc == 0), stop=(fc == FC - 1),
                          reads=[("wb", wb, 0), ("wb", wb, 1), ("hid", fc)], writes=[("ps", dbank + h)])
            b = cx.rr["xb"] = (cx.rr.get("xb", -1) + 1) % len(cx.xbuf)
            ft = cx.xbuf[b]
            psr = [("ps", dbank + h) for h in range(NH)]
            cx.act(ft[:, 0:T], cx.bank(dbank, NH), AF.Copy, reads=psr, writes=[("xb", b)])
            sb_ = cx.rr["sq"] = (cx.rr.get("sq", -1) + 1) % 2
            sq = cx.sqbuf[sb_]
            cx.act(sq[:, 0:T], cx.bank(dbank, NH), AF.Square, reads=psr, writes=[("sq", sb_)])
            if reuse and ti == 0:
                cx.store(wsc[2][dc], w[:, 0:FC * 128], reads=[("wb", wb, 0), ("wb", wb, 1)], writes=[(tag, "wscd", dc)],
                         key="wbs%d" % wb, eng="act")
            cx.store(fscr[dc * 128:(dc + 1) * 128, 0:T], ft[:, 0:T], reads=[("xb", b)],
                     writes=[("fs", "f", dc)], key="xb%d" % b)
            for h in range(NH):
                cx.mm(cx.bank(cx.ssbank + h), cx.ones[:, :], sq[:, h * 512:(h + 1) * 512],
                      start=(dc == 0), stop=(dc == KC - 1),
                      reads=[("sq", sb_), "ones"], writes=[("ps", cx.ssbank + h)])
            if nxt is not None and dc >= 1:
                for _ in range(3):
                    next(nxt, None)
        if nxt is not None:
            for _ in nxt:
                pass
        rstd_from_ss(cx, T, D, cx.ssbank, cx.rstd, "rstd")
        comb = combine_gen(cx, fscr, src, dst, t0, T, gpost, "fs", tag, otag, half=True)
    for _ in comb:
        pass


def layout_ffn(cx, T):
    cx.uT = cx.alloc(KC * T, BF16).rearrange("p (k t) -> p k t", k=KC)
    cx.hid = cx.alloc(FC * T, BF16).rearrange("p (k t) -> p k t", k=FC)
    cx.wbuf = [cx.alloc(FC * 128, BF16) for _ in range(3)]
    cx.xbuf = [cx.alloc(T, F32) for _ in range(4)]
    cx.sqbuf = [cx.alloc(T, BF16) for _ in range(2)]
    cx.silb = [cx.alloc(T, F32) for _ in range(2)]
    cx.rstdA = cx.alloc(T, F32)
    cx.ssbank = 8 - T // 512


def rrn(cx, name, n):
    v = cx.rr[name] = (cx.rr.get(name, -1) + 1) % n
    return v


def inproj(cx, W, ntok_ctx, ntok_own, T, side=None):
    p = cx.p
    NH = T // 512
    NT = ntok_ctx + ntok_own
    uTs = [cx.alloc(KC * T, BF16).rearrange("p (k t) -> p k t", k=KC) for _ in range(2)]
    ukeys = ["uTa", "uTb"]
    cx.wbuf = [cx.alloc(KC * 512, BF16) for _ in range(3)]
    cx.xbuf = [cx.alloc(T, F32) for _ in range(3)]
    cx.sqbuf = [cx.alloc(T, BF16) for _ in range(2)]
    cx.ssbank = 8 - NH
    gm = cx.gv("mix_pre_g")
    tiles = list(range(0, NT, T))
    norm_to_uT(cx, cx.h1, tiles[0], T, gm, uTs[0], ukeys[0], "h1")
    for ti, t0 in enumerate(tiles):
        own = t0 >= ntok_ctx
        to = t0 - ntok_ctx
        uT = uTs[ti % 2]
        uk = ukeys[ti % 2]
        nxt = None
        if ti + 1 < len(tiles):
            nxt = norm_gen(cx, cx.h1, tiles[ti + 1], T, gm, uTs[(ti + 1) % 2], ukeys[(ti + 1) % 2], "h1")

        def fm(wsrc, nchunk, epi, rows=128):
            for j in range(nchunk):
                wb = rrn(cx, "wb", 3)
                w = cx.wbuf[wb]
                wload(cx, w[:, 0:KC * rows], wsrc[j], [], [("wb", wb, 0), ("wb", wb, 1)], "wb%d" % wb)
                pb = rrn(cx, "pjb", 3) * NH
                for kc in range(KC):
                    for h in range(NH):
                        cx.mm(cx.bank(pb + h)[0:rows, :], w[:, kc * rows:(kc + 1) * rows], uT[:, kc, h * 512:(h + 1) * 512],
                              start=(kc == 0), stop=(kc == KC - 1),
                              reads=[("wb", wb, 0), ("wb", wb, 1), (uk, kc)], writes=[("ps", pb + h)])
                epi(j, cx.bank(pb, NH), [("ps", pb + h) for h in range(NH)])
                if side is not None:
                    side.step()

        def epi_bf(dst, col0, scale=None):
            def f(j, ps, psk):
                b = rrn(cx, "sq", 2)
                o = cx.sqbuf[b]
                cx.act(o[:, 0:T], ps, AF.Copy, reads=psk, writes=[("sq", b)], scale=scale)
                cx.store(dst[j * 128:(j + 1) * 128, col0:col0 + T], o[:, 0:T], reads=[("sq", b)],
                         writes=[("scr", id(dst), j)], key="sq%d" % b)
            return f

        def epi_f32(dst, col0, func=AF.Copy, rows=128, isxbc=False):
            def f(j, ps, psk):
                b = rrn(cx, "xb", 3)
                o = cx.xbuf[b]
                cx.act(o[0:rows, 0:T], ps[0:rows, :], func, reads=psk, writes=[("xb", b)])
                wk = ("scr", "xbcT", j) if isxbc else ("scr", id(dst), j)
                cx.store(dst[j * rows:(j + 1) * rows, col0:col0 + T], o[0:rows, 0:T], reads=[("xb", b)],
                         writes=[wk], key="xb%d" % b)
                if isxbc and side is not None:
                    side.avail = (t0 // T) * 48 + j + 1
            return f

        if own:
            fm(W["wq"], 16, epi_bf(cx.qT, to, scale=1.0 / math.sqrt(128.0)))
        fm(W["wk"], 16, epi_bf(cx.kT, t0))
        fm(W["wxbc"], 48, epi_f32(cx.xbcT, 4 + t0, isxbc=True))
        fm(W["wdt"], 1, epi_f32(cx.dtT, t0, rows=64), rows=64)
        if own:
            fm(W["wg"], 32, epi_f32(cx.gT, to, func=AF.Sigmoid))

        def tm(wsrc, nslab, dst, row0, dt_bf):
            for s_ in range(nslab):
                wb = rrn(cx, "wb", 3)
                w = cx.wbuf[wb]
                wload(cx, w[:, 0:KC * 512], wsrc[s_], [], [("wb", wb, 0), ("wb", wb, 1)], "wb%d" % wb)
                for tb in range(T // 128):
                    pb = rrn(cx, "pjb2", 6)
                    for kc in range(KC):
                        cx.mm(cx.bank(pb), uT[:, kc, tb * 128:(tb + 1) * 128], w[:, kc * 512:(kc + 1) * 512],
                              start=(kc == 0), stop=(kc == KC - 1),
                              reads=[("wb", wb, 0), ("wb", wb, 1), (uk, kc)], writes=[("ps", pb)])
                    if dt_bf:
                        b = rrn(cx, "sq", 2)
                        o = cx.sqbuf[b]
                        key = ("sq", b)
                        sk = "sq%d" % b
                    else:
                        b = rrn(cx, "xb", 3)
                        o = cx.xbuf[b]
                        key = ("xb", b)
                        sk = "xb%d" % b
                    cx.act(o[:, 0:512], cx.bank(pb), AF.Copy, reads=[("ps", pb)], writes=[key])
                    r0 = row0 + tb * 128
                    cx.store(dst[r0:r0 + 128, s_ * 512:(s_ + 1) * 512], o[:, 0:512], reads=[key],
                             writes=[("scr", id(dst), "tm", s_)], key=sk)
                    if side is not None:
                        side.step()
                    if nxt is not None:
                        next(nxt, None)
                        next(nxt, None)

        tm(W["wv"], 4, cx.V, t0, True)
        if own:
            tm(W["wz"], 8, cx.Z, to, False)
        if nxt is not None:
            for _ in nxt:
                pass


def attention(cx, ntok_ctx, ntok_own, side=None, side_every=3):
    p = cx.p
    nblk = [0]
    NTK = ntok_ctx + ntok_own
    NKB = NTK // 128
    NCB = ntok_ctx // 128
    NQT = ntok_own // 512
    kT = [cx.alloc(2 * NTK, BF16).rearrange("p (m t) -> p m t", m=2) for _ in range(2)]
    Vt = [cx.alloc(NKB * 256, BF16).rearrange("p (b c) -> p b c", b=NKB) for _ in range(2)]
    qT = [cx.alloc(2 * ntok_own, BF16).rearrange("p (m t) -> p m t", m=2) for _ in range(2)]
    PT = [cx.alloc(512, BF16) for _ in range(6)]
    rinv = [cx.alloc(512, F32) for _ in range(2)]
    of = [cx.alloc(512, F32) for _ in range(2)]
    tmpf = cx.alloc(512, F32)
    sqb = cx.alloc(512, BF16)
    ob = [cx.alloc(512, BF16) for _ in range(2)]
    rst = cx.alloc(512, F32)
    SBK = ((0, 1), (0, 1))
    SB = (0, 1)
    OB = ((2, 3), (4, 5))
    RB = (6, 7)
    def load_head(h):
        hb = h % 2
        for m in range(2):
            cx.load(kT[hb][:, m, :], cx.kT[(2 * h + m) * 128:(2 * h + m + 1) * 128, 0:NTK], reads=[],
                    writes=[("kT", hb)], key="kT%d" % hb)
            cx.load(qT[hb][:, m, :], cx.qT[(2 * h + m) * 128:(2 * h + m + 1) * 128, 0:ntok_own], reads=[],
                    writes=[("qT", hb)], key="qT%d" % hb)
        cx.load(Vt[hb][:, :, :], cx.V[0:NTK, h * 256:(h + 1) * 256].rearrange("(b p) c -> p b c", p=128),
                reads=[], writes=[("Vt", hb)], key="Vt%d" % hb)

    load_head(0)
    for h in range(8):
        hb = h % 2
        if h + 1 < 8:
            load_head(h + 1)
        for qt in range(NQT):
            nkb = NCB + 4 * (qt + 1)

            def geom(kb):
                ob_ = kb - NCB
                r = ob_ - 4 * qt if ob_ >= 4 * qt else -1
                return r, (128 * r if r > 0 else 0)

            def emit_S(kb):
                r, c0 = geom(kb)
                par = kb % 2
                pts = []
                for m in range(2):
                    sbk = SBK[par][m]
                    cx.mm(cx.bank(sbk)[:, c0:512], kT[hb][:, m, kb * 128:(kb + 1) * 128],
                          qT[hb][:, m, qt * 512 + c0:(qt + 1) * 512], start=True, stop=True,
                          reads=[("kT", hb), ("qT", hb)], writes=[("ps", sbk)])
                    pi = rrn(cx, "pt", len(PT))
                    pt = PT[pi]
                    pts.append((pi, pt))
                    bias = cx.flags[:, 0:1] if kb < NCB else None
                    cx.act(pt[:, c0:512], cx.bank(sbk)[:, c0:512], AF.Exp, reads=[("ps", sbk), "flags"],
                           writes=[("pt", pi)], bias=bias)
                    if r >= 0:
                        p.add("dve", lambda e, pt=pt, c0=c0: e.tensor_tensor(
                            out=pt[:, c0:c0 + 128], in0=pt[:, c0:c0 + 128], in1=cx.tri_bf[:, :], op=ALU.mult),
                            reads=[("pt", pi), "consts"], writes=[("pt", pi)])
                return pts

            def emit_PV(kb, pts):
                r, c0 = geom(kb)
                for m in range(2):
                    pi, pt = pts[m]
                    for dvc in range(2):
                        cx.mm(cx.bank(OB[m][dvc])[:, c0:512], Vt[hb][:, kb, dvc * 128:(dvc + 1) * 128], pt[:, c0:512],
                              start=(kb == 0), stop=(kb == nkb - 1),
                              reads=[("pt", pi), ("Vt", hb)], writes=[("ps", OB[m][dvc])])
                    cx.mm(cx.bank(RB[m])[:, c0:512], cx.ones[:, :], pt[:, c0:512],
                          start=(kb == 0), stop=(kb == nkb - 1),
                          reads=[("pt", pi), "ones"], writes=[("ps", RB[m])])

            nxt = emit_S(0)
            for kb in range(nkb):
                cur = nxt
                if kb + 1 < nkb:
                    nxt = emit_S(kb + 1)
                emit_PV(kb, cur)
                nblk[0] += 1
                if side is not None and nblk[0] % side_every == 0:
                    next(side, None)
            for m in range(2):
                p.add("dve", lambda e, m=m: e.reciprocal(out=rinv[m][:, :], in_=cx.bank(RB[m])),
                      reads=[("ps", RB[m])], writes=[("rinv", m)])
            p.add("dve", lambda e: e.tensor_scalar(out=rinv[1][:, :], in0=rinv[1][:, :], scalar1=cx.neglam[:, 0:1],
                                                    scalar2=None, op0=ALU.mult),
                  reads=[("rinv", 1), "neglam"], writes=[("rinv", 1)])
            for dvc in range(2):
                p.add("dve", lambda e, dvc=dvc: e.tensor_tensor(out=of[dvc][:, :], in0=cx.bank(OB[0][dvc]),
                                                                 in1=rinv[0][:, :], op=ALU.mult),
                      reads=[("ps", OB[0][dvc]), ("rinv", 0)], writes=[("of", dvc)])
                p.add("dve", lambda e, dvc=dvc: e.tensor_tensor(out=tmpf[:, :], in0=cx.bank(OB[1][dvc]),
                                                                 in1=rinv[1][:, :], op=ALU.mult),
                      reads=[("ps", OB[1][dvc]), ("rinv", 1)], writes=["tmpf"])
                p.add("dve", lambda e, dvc=dvc: e.tensor_tensor(out=of[dvc][:, :], in0=of[dvc][:, :],
                                                                 in1=tmpf[:, :], op=ALU.add),
                      reads=["tmpf", ("of", dvc)], writes=[("of", dvc)])
                cx.act(sqb[:, :], of[dvc][:, :], AF.Square, reads=[("of", dvc)], writes=["asq"])
                cx.mm(cx.bank(SB[0]), cx.ones[:, :], sqb[:, :], start=(dvc == 0), stop=(dvc == 1),
                      reads=["asq", "ones"], writes=[("ps", SB[0])])
            p.add("dve", lambda e: e.tensor_scalar(out=rst[:, :], in0=cx.bank(SB[0]), scalar1=1.0 / 256.0, scalar2=EPS,
                                                    op0=ALU.mult, op1=ALU.add),
                  reads=[("ps", SB[0])], writes=["arst"])
            cx.act(rst[:, :], rst[:, :], AF.Sqrt, reads=["arst"], writes=["arst"])
            p.add("dve", lambda e: e.reciprocal(out=rst[:, :], in_=rst[:, :]), reads=["arst"], writes=["arst"])
            for dvc in range(2):
                p.add("dve", lambda e, dvc=dvc: e.scalar_tensor_tensor(
                    out=ob[dvc][:, :], in0=of[dvc][:, :], scalar=cx.subg[:, dvc:dvc + 1], in1=rst[:, :],
                    op0=ALU.mult, op1=ALU.mult),
                    reads=[("of", dvc), "arst", "subg"], writes=[("aob", dvc)])
                j = 2 * h + dvc
                cx.store(cx.oattT[j * 128:(j + 1) * 128, qt * 512:(qt + 1) * 512], ob[dvc][:, :],
                         reads=[("aob", dvc)], writes=[("scr", "oatt", j)], key="aob%d" % dvc, eng="pool")


def bc(ap, shape, axis):
    return ap.unsqueeze(axis).to_broadcast(shape)


class ConvSide:
    def __init__(self, cx, ntok_ctx, ntok_own, T, NB=4):
        self.cx = cx
        NT = ntok_ctx + ntok_own
        self.T = T
        self.NB = NB
        self.ntok_ctx = ntok_ctx
        self.xin = [cx.alloc(T + 8, F32) for _ in range(NB)]
        self.acc = [cx.alloc(T, F32) for _ in range(NB)]
        self.outf = [cx.alloc(T, F32) for _ in range(2)]
        self.outb = [cx.alloc(T, BF16) for _ in range(2)]
        self.units = [(cc, t0) for t0 in range(0, NT, T) for cc in range(48)]
        self.avail = 0
        self.need = 0
        self.gen = self._gen()
        self.done = False

    def step(self):
        if self.done or self.need > self.avail:
            return
        try:
            next(self.gen)
        except StopIteration:
            self.done = True

    def drain(self):
        self.avail = len(self.units)
        while not self.done:
            self.step()

    def _gen(self):
        cx = self.cx
        p = cx.p
        T = self.T
        NB = self.NB
        xin, acc, outf, outb = self.xin, self.acc, self.outf, self.outb
        for u0 in range(0, len(self.units), NB):
            batch = self.units[u0:u0 + NB]
            self.need = u0 + len(batch)
            yield
            for b, (cc, t0) in enumerate(batch):
                xi = xin[b]
                cx.load(xi[:, 0:T + 3], cx.xbcT[cc * 128:(cc + 1) * 128, 4 + t0 - 3:4 + t0 + T],
                        reads=[("scr", "xbcT", cc), ("scr", "pad")], writes=[("cin", b)], key="cin%d" % b)
                if t0 == self.ntok_ctx and self.ntok_ctx > 0:
                    p.add("dve", lambda e, xi=xi: e.tensor_scalar(out=xi[:, 0:3], in0=xi[:, 0:3], scalar1=cx.flags[:, 1:2],
                                                                  scalar2=None, op0=ALU.mult),
                          reads=[("cin", b), "flags"], writes=[("cin", b)])
                a = acc[b]
                w = cx.convw
                p.add("dve", lambda e, xi=xi, a=a, cc=cc: e.tensor_scalar(
                    out=a[:, 0:T], in0=xi[:, 0:T], scalar1=w[:, cc * 4:cc * 4 + 1], scalar2=None, op0=ALU.mult),
                    reads=[("cin", b), "convw"], writes=[("cacc", b)])
                for j in range(1, 4):
                    p.add("dve", lambda e, xi=xi, a=a, cc=cc, j=j: e.scalar_tensor_tensor(
                        out=a[:, 0:T], in0=xi[:, j:j + T], scalar=w[:, cc * 4 + j:cc * 4 + j + 1], in1=a[:, 0:T],
                        op0=ALU.mult, op1=ALU.add),
                        reads=[("cin", b), ("cacc", b), "convw"], writes=[("cacc", b)])
                yield
            for b, (cc, t0) in enumerate(batch):
                a = acc[b]
                if cc < 32:
                    ob_ = rrn(cx, "cof", 2)
                    o = outf[ob_]
                    okey, dstT, row0, sk = ("cof", ob_), cx.xcT, cc * 128, "cof%d" % ob_
                else:
                    ob_ = rrn(cx, "cob", 2)
                    o = outb[ob_]
                    okey, dstT, row0, sk = ("cob", ob_), cx.bcT, (cc - 32) * 128, "cob%d" % ob_
                cx.act(o[:, 0:T], a[:, 0:T], AF.Silu, reads=[("cacc", b), "convw"], writes=[okey],
                       bias=cx.convb[:, cc:cc + 1])
                cx.store(dstT[row0:row0 + 128, t0:t0 + T], o[:, 0:T], reads=[okey],
                         writes=[("scr", "conv", cc)], key=sk, eng="act")
            yield


def ssd(cx, ntok_ctx, ntok_own):
    p = cx.p
    NT = ntok_ctx + ntok_own
    NCH = NT // 128
    NCC = ntok_ctx // 128
    A3 = lambda ap, a, b_: ap.rearrange("p (a b) -> p a b", a=a)
    xfm = [cx.alloc(4096, F32) for _ in range(2)]
    bct = [cx.alloc(2048, BF16) for _ in range(2)]
    dtr = [cx.alloc(128, F32) for _ in range(2)]
    zb = cx.alloc(4096, F32)
    xtok = cx.alloc(4096, F32)
    xw = cx.alloc(4096, BF16)
    xdt = cx.alloc(4096, BF16)
    S = cx.alloc(4096, F32)
    Sbf = cx.alloc(4096, BF16)
    y = cx.alloc(4096, F32)
    ngb = cx.alloc(4096, F32)
    yn = cx.alloc(4096, BF16)
    oTs = cx.alloc(4096, BF16)
    rhsL = [cx.alloc(1024, F32) for _ in range(2)]
    Eb = [cx.alloc(1024, BF16) for _ in range(2)]
    Mb = [cx.alloc(1024, BF16) for _ in range(2)]
    cbm = [cx.alloc(128, BF16) for _ in range(2)]
    Btok = cx.alloc(1024, BF16)
    e1 = cx.alloc(128, F32)
    dtf = cx.alloc(128, F32)
    aT = cx.alloc(128, F32)
    da = cx.alloc(128, F32)
    ct = cx.alloc(128, F32)
    dte = cx.alloc(64, F32)
    w1 = cx.alloc(64, F32)
    cdec = cx.alloc(64, F32)
    din = cx.alloc(64, F32)
    tmpg = [cx.alloc(512, F32) for _ in range(2)]
    junk = cx.alloc(512, BF16)
    ssq = cx.alloc(8, F32)
    rs8 = cx.alloc(8, F32)
    cx.load(ngb[:, :], cx.ngb_d, reads=[], writes=["ngb"], key="ngb")
    p.add("dve", lambda e: e.memset(S[:, :], 0.0), writes=["S"])

    def loads(c):
        b = c % 2
        t0 = c * 128
        cx.load(A3(xfm[b], 32, 128), cx.xcT[:, t0:t0 + 128].rearrange("(c p) t -> p c t", p=128), reads=[],
                writes=[("xfm", b)], key="xfm%d" % b)
        cx.load(A3(bct[b], 16, 128), cx.bcT[:, t0:t0 + 128].rearrange("(c p) t -> p c t", p=128), reads=[],
                writes=[("bct", b)], key="bct%d" % b)
        cx.load(dtr[b][0:64, :], cx.dtT[:, t0:t0 + 128], reads=[], writes=[("dtr", b)], key="dtr%d" % b)

    loads(0)
    for c in range(NCH):
        b = c % 2
        own = c >= NCC
        tl0 = (c - NCC) * 128
        if c + 1 < NCH:
            loads(c + 1)
        if own:
            cx.load(zb[:, :], cx.Z[tl0:tl0 + 128, :], reads=[], writes=["z"], key="z")
        xf3 = A3(xfm[b], 32, 128)
        bc3 = A3(bct[b], 16, 128)
        cx.act(e1[0:64, :], dtr[b][0:64, :], AF.Exp, reads=[("dtr", b), "ssmv"], writes=["e1"], bias=cx.ssmv[0:64, 0:1])
        cx.act(dtf[0:64, :], e1[0:64, :], AF.Ln, reads=["e1"], writes=["dtf"], bias=1.0)
        p.add("dve", lambda e: e.tensor_scalar(out=aT[0:64, :], in0=dtf[0:64, :], scalar1=cx.ssmv[0:64, 2:3], scalar2=None,
                                                op0=ALU.mult), reads=["dtf", "ssmv"], writes=["aT"])
        mb = 7
        p.add("pe", lambda e: e.transpose(cx.bank(mb)[:, 0:64], dtf[0:64, :], cx.ident_f[0:64, 0:64]),
              reads=["dtf", "consts"], writes=[("ps", mb)])
        p.add("pe", lambda e: e.transpose(cx.bank(mb)[:, 64:128], aT[0:64, :], cx.ident_f[0:64, 0:64]),
              reads=["aT", "consts"], writes=[("ps", mb)])
        p.add("dve", lambda e: e.tensor_copy(out=da[:, :], in_=cx.bank(mb)[:, 0:128]), reads=[("ps", mb)], writes=["da"])
        cx.mm(cx.bank(mb)[:, 128:192], cx.tri_f[:, :], da[:, 64:128], True, True, reads=["da", "consts"], writes=[("ps", mb)])
        cx.mm(cx.bank(mb)[:, 192:256], cx.ones_f[:, :], da[:, 64:128], True, True, reads=["da", "consts"], writes=[("ps", mb)])
        p.add("dve", lambda e: e.tensor_copy(out=ct[:, :], in_=cx.bank(mb)[:, 128:256]), reads=[("ps", mb)], writes=["ct"])
        p.add("dve", lambda e: e.tensor_tensor(out=dte[:, :], in0=ct[:, 64:128], in1=ct[:, 0:64], op=ALU.subtract),
              reads=["ct"], writes=["dte"])
        cx.act(dte[:, :], dte[:, :], AF.Exp, reads=["dte"], writes=["dte"])
        p.add("dve", lambda e: e.tensor_tensor(out=w1[:, :], in0=da[:, 0:64], in1=dte[:, :], op=ALU.mult),
              reads=["da", "dte"], writes=["w1"])
        cx.act(cdec[:, :], ct[:, 64:128], AF.Exp, reads=["ct"], writes=["cdec"])
        if own:
            cx.act(din[:, :], ct[:, 0:64], AF.Exp, reads=["ct"], writes=["din"])
        for g in range(8):
            xb_ = rrn(cx, "xtb", 4)
            for i in range(4):
                p.add("pe", lambda e, g=g, i=i, xb_=xb_, xf3=xf3: e.transpose(cx.bank(xb_)[:, i * 128:(i + 1) * 128],
                                                                      xf3[:, 4 * g + i, :], cx.ident_f[:, :]),
                      reads=[("xfm", b), "consts"], writes=[("ps", xb_)])
            if g % 2 == 0:
                cx.act(xtok[:, g * 512:(g + 1) * 512], cx.bank(xb_), AF.Copy, reads=[("ps", xb_)], writes=[("xtok", g)])
            else:
                p.add("dve", lambda e, g=g, xb_=xb_: e.tensor_copy(out=xtok[:, g * 512:(g + 1) * 512], in_=cx.bank(xb_)),
                      reads=[("ps", xb_)], writes=[("xtok", g)])
        pbf = cx.bank(4, 1).bitcast(BF16)
        for g in range(8):
            p.add("pe", lambda e, g=g, bc3=bc3: e.transpose(pbf[:, g * 128:(g + 1) * 128], bc3[:, g, :], cx.ident_bf[:, :]),
                  reads=[("bct", b), "consts"], writes=[("ps", 4)])
        p.add("dve", lambda e: e.tensor_copy(out=Btok[:, :], in_=pbf[:, 0:1024]), reads=[("ps", 4)], writes=["Btok"])
        xkeys = [("xtok", g) for g in range(8)]
        p.add("dve", lambda e: e.tensor_tensor(out=A3(xw, 64, 64), in0=A3(xtok, 64, 64), in1=bc(w1[:, :], [128, 64, 64], 2),
                                                op=ALU.mult), reads=xkeys + ["w1"], writes=["xw"])
        if own:
            p.add("dve", lambda e: e.tensor_tensor(out=A3(xdt, 64, 64), in0=A3(xtok, 64, 64),
                                                    in1=bc(da[:, 0:64], [128, 64, 64], 2), op=ALU.mult),
                  reads=xkeys + ["da"], writes=["xdt"])
            cx.act(Sbf[:, :], S[:, :], AF.Copy, reads=["S"], writes=["Sbf"])
            def s1(g):
                rb = g % 2
                rl = rhsL[rb]
                p.add("dve", lambda e, g=g, rl=rl: e.tensor_tensor(
                    out=A3(rl, 8, 128), in0=bc(da[:, 64 + 8 * g:72 + 8 * g], [128, 8, 128], 2),
                    in1=bc(cx.tri_f[:, :], [128, 8, 128], 1), op=ALU.mult),
                    reads=["da", "consts"], writes=[("rhsL", rb)])
                cbs = cx.bank(5)[:, (g % 4) * 128:(g % 4 + 1) * 128]
                cx.mm(cbs, bc3[:, g, :], bc3[:, 8 + g, :], True, True, reads=[("bct", b)], writes=[("ps", 5)])
                cb_ = cbm[rb]
                p.add("dve", lambda e, cbs=cbs, cb_=cb_: e.tensor_tensor(out=cb_[:, :], in0=cbs, in1=cx.tri_f[:, :], op=ALU.mult),
                      reads=[("ps", 5), "consts"], writes=[("cbm", rb)])

            def s1b(g):
                lb = (g % 2) * 2
                rb = g % 2
                rl = rhsL[rb]
                for hf in range(2):
                    cx.mm(cx.bank(lb + hf), cx.stri_f[:, :], rl[:, hf * 512:(hf + 1) * 512], True, True,
                          reads=[("rhsL", rb), "consts"], writes=[("ps", lb + hf)])
                E_ = Eb[rb]
                cx.act(E_[:, :], cx.bank(lb, 2), AF.Exp, reads=[("ps", lb), ("ps", lb + 1)], writes=[("E", rb)])

            def s2(g):
                rb = g % 2
                E_, M_, cb_ = Eb[rb], Mb[rb], cbm[rb]
                p.add("dve", lambda e, E_=E_, M_=M_, cb_=cb_: e.tensor_tensor(
                    out=A3(M_, 8, 128), in0=A3(E_, 8, 128), in1=bc(cb_[:, :], [128, 8, 128], 1), op=ALU.mult),
                    reads=[("E", rb), ("cbm", rb)], writes=[("M", rb)])
                M3 = A3(M_, 8, 128)
                for e_ in range(8):
                    hh = 8 * g + e_
                    cx.mm(cx.bank(6)[:, e_ * 64:(e_ + 1) * 64], M3[:, e_, :], xdt[:, hh * 64:(hh + 1) * 64], True, True,
                          reads=[("M", rb), "xdt"], writes=[("ps", 6)])
                cx.mm(cx.bank(7), bc3[:, 8 + g, :], Sbf[:, g * 512:(g + 1) * 512], True, True,
                      reads=[("bct", b), "Sbf"], writes=[("ps", 7)])
                tg = tmpg[rb]
                p.add("dve", lambda e, g=g, tg=tg: e.tensor_tensor(
                    out=A3(tg, 8, 64), in0=A3(xtok[:, g * 512:(g + 1) * 512], 8, 64),
                    in1=bc(cx.dskb[:, 8 * g:8 * g + 8], [128, 8, 64], 2), op=ALU.mult),
                    reads=[("xtok", g), "dskb"], writes=[("tmpg", rb)])
                yg = y[:, g * 512:(g + 1) * 512]
                p.add("dve", lambda e, g=g, yg=yg: e.tensor_tensor(
                    out=A3(yg, 8, 64), in0=A3(cx.bank(7), 8, 64), in1=bc(din[:, 8 * g:8 * g + 8], [128, 8, 64], 2),
                    op=ALU.mult), reads=[("ps", 7), "din"], writes=[("y", g)])
                p.add("dve", lambda e, yg=yg: e.tensor_tensor(out=yg, in0=yg, in1=cx.bank(6), op=ALU.add),
                      reads=[("ps", 6), ("y", g)], writes=[("y", g)])
                p.add("dve", lambda e, yg=yg, tg=tg: e.tensor_tensor(out=yg, in0=yg, in1=tg[:, :], op=ALU.add),
                      reads=[("tmpg", rb), ("y", g)], writes=[("y", g)])

            s1(0)
            s1b(0)
            for g in range(8):
                if g + 1 < 8:
                    s1(g + 1)
                s2(g)
                if g + 1 < 8:
                    s1b(g + 1)
        if c < NCH - 1:
            for g in range(8):
                sb_ = 4 if False else (2 + g % 2) if not own else 7
                sb_ = rrn(cx, "stb", 2) + 2 if not own else 7
                cx.mm(cx.bank(sb_), Btok[:, g * 128:(g + 1) * 128], xw[:, g * 512:(g + 1) * 512], True, True,
                      reads=["Btok", "xw"], writes=[("ps", sb_)])
                Sg = S[:, g * 512:(g + 1) * 512]
                p.add("dve", lambda e, g=g, Sg=Sg: e.tensor_tensor(
                    out=A3(Sg, 8, 64), in0=A3(Sg, 8, 64), in1=bc(cdec[:, 8 * g:8 * g + 8], [128, 8, 64], 2), op=ALU.mult),
                    reads=["S", "cdec", "Sbf"], writes=["S"])
                p.add("dve", lambda e, Sg=Sg, sb_=sb_: e.tensor_tensor(out=Sg, in0=Sg, in1=cx.bank(sb_), op=ALU.add),
                      reads=["S", ("ps", sb_)], writes=["S"])
            if c == NCC - 1:
                p.add("dve", lambda e: e.tensor_scalar(out=S[:, :], in0=S[:, :], scalar1=cx.flags[:, 1:2], scalar2=None,
                                                        op0=ALU.mult), reads=["S", "flags"], writes=["S"])
        if c in cx.dbg_chunks:
            cx.dbg("da%d" % c, da[:, :], ["da"])
            cx.dbg("xtok%d" % c, xtok[:, :], [("xtok", g) for g in range(8)])
            cx.dbg("y%d" % c, y[:, :], [("y", g) for g in range(8)])
        if own:
            cx.act(zb[:, :], zb[:, :], AF.Silu, reads=["z"], writes=["z"])
            ykeys = [("y", g) for g in range(8)]
            p.add("dve", lambda e: e.tensor_tensor(out=y[:, :], in0=y[:, :], in1=zb[:, :], op=ALU.mult),
                  reads=ykeys + ["z"], writes=ykeys)
            p.add("dve", lambda e: e.memset(ssq[:, :], 0.0), writes=["ssq"])
            for g in range(8):
                cx.act(junk[:, :], y[:, g * 512:(g + 1) * 512], AF.Square, reads=[("y", g), "ssq"], writes=["junk", "ssq"],
                       accum_out=ssq[:, g:g + 1])
            p.add("dve", lambda e: e.tensor_scalar(out=rs8[:, :], in0=ssq[:, :], scalar1=1.0 / 512.0, scalar2=EPS,
                                                    op0=ALU.mult, op1=ALU.add), reads=["ssq"], writes=["rs8"])
            cx.act(rs8[:, :], rs8[:, :], AF.Sqrt, reads=["rs8"], writes=["rs8"])
            p.add("dve", lambda e: e.reciprocal(out=rs8[:, :], in_=rs8[:, :]), reads=["rs8"], writes=["rs8"])
            for g in range(8):
                p.add("dve", lambda e, g=g: e.scalar_tensor_tensor(
                    out=yn[:, g * 512:(g + 1) * 512], in0=y[:, g * 512:(g + 1) * 512], scalar=rs8[:, g:g + 1],
                    in1=ngb[:, g * 512:(g + 1) * 512], op0=ALU.mult, op1=ALU.mult),
                    reads=[("y", g), "rs8", "ngb"], writes=["yn"])
            for q4 in range(4):
                tb_ = rrn(cx, "xtb", 4)
                pv = cx.bank(tb_).bitcast(BF16)
                for i in range(8):
                    cc = q4 * 8 + i
                    p.add("pe", lambda e, pv=pv, i=i, cc=cc: e.transpose(pv[:, i * 128:(i + 1) * 128],
                                                                         yn[:, cc * 128:(cc + 1) * 128], cx.ident_bf[:, :]),
                          reads=["yn", "consts"], writes=[("ps", tb_)])
                if q4 % 2 == 0:
                    cx.act(oTs[:, q4 * 1024:(q4 + 1) * 1024], pv[:, 0:1024], AF.Copy, reads=[("ps", tb_)], writes=["oTs"])
                else:
                    p.add("dve", lambda e, pv=pv, q4=q4: e.tensor_copy(out=oTs[:, q4 * 1024:(q4 + 1) * 1024], in_=pv[:, 0:1024]),
                          reads=[("ps", tb_)], writes=["oTs"])
            cx.store(cx.ossmT[:, tl0:tl0 + 128].rearrange("(c p) t -> p c t", p=128), A3(oTs, 32, 128), reads=["oTs"],
                     writes=[("scr", "ossm")], key="oTs", eng="pool")


def merge(cx, W, ntok_ctx, ntok_own, T=1024):
    p = cx.p
    NH = T // 512
    rT = cx.alloc(48 * T, BF16).rearrange("p (k t) -> p k t", k=48)
    mT = cx.alloc(KC * T, BF16).rearrange("p (k t) -> p k t", k=KC)
    cx.wbuf = [cx.alloc(48 * 128, BF16) for _ in range(2)]
    cx.xbuf = [cx.alloc(T, F32) for _ in range(3)]
    cx.sqbuf = [cx.alloc(T, BF16) for _ in range(2)]
    gb = [cx.alloc(2 * T, F32) for _ in range(2)]
    cx.ssbank = 8 - NH
    for t0 in range(0, ntok_own, T):
        cx.load(rT[:, 0:16, :], cx.oattT[:, t0:t0 + T].rearrange("(c p) t -> p c t", p=128), reads=[],
                writes=["rT"], key="rTa")
        cx.load(rT[:, 16:48, :], cx.ossmT[:, t0:t0 + T].rearrange("(c p) t -> p c t", p=128), reads=[],
                writes=["rT"], key="rTb")
        for dc in range(KC):
            wb = rrn(cx, "wb", 2)
            w = cx.wbuf[wb]
            wload(cx, w[:, 0:2048], W["wba"][dc], [], [("wb", wb, 0)], "wb%d" % wb)
            wload(cx, w[:, 2048:6144], W["wbs"][dc], [], [("wb", wb, 1)], "wb%d" % wb)
            g_ = rrn(cx, "gb", 2)
            gt = gb[g_]
            cx.load(gt[:, 0:T], cx.gT[dc * 128:(dc + 1) * 128, t0:t0 + T], reads=[], writes=[("gb", g_)], key="gb%d" % g_)
            cx.load(gt[:, T:2 * T], cx.gT[D + dc * 128:D + (dc + 1) * 128, t0:t0 + T], reads=[], writes=[("gb", g_)],
                    key="gb%d" % g_)
            pa = rrn(cx, "mpb", 2) * 2 * NH
            psa = [("ps", pa + h) for h in range(NH)]
            pss = [("ps", pa + NH + h) for h in range(NH)]
            for kc in range(16):
                for h in range(NH):
                    cx.mm(cx.bank(pa + h), w[:, kc * 128:(kc + 1) * 128], rT[:, kc, h * 512:(h + 1) * 512], kc == 0, kc == 15,
                          reads=[("wb", wb, 0), "rT"], writes=[("ps", pa + h)])
            for kc in range(32):
                for h in range(NH):
                    cx.mm(cx.bank(pa + NH + h), w[:, 2048 + kc * 128:2048 + (kc + 1) * 128], rT[:, 16 + kc, h * 512:(h + 1) * 512],
                          kc == 0, kc == 31, reads=[("wb", wb, 1), "rT"], writes=[("ps", pa + NH + h)])
            p.add("dve", lambda e, gt=gt, pa=pa: e.tensor_tensor(out=gt[:, 0:T], in0=gt[:, 0:T], in1=cx.bank(pa, NH), op=ALU.mult),
                  reads=[("gb", g_)] + psa, writes=[("gb", g_)])
            p.add("dve", lambda e, gt=gt, pa=pa: e.tensor_tensor(out=gt[:, T:2 * T], in0=gt[:, T:2 * T],
                                                                  in1=cx.bank(pa + NH, NH), op=ALU.mult),
                  reads=[("gb", g_)] + pss, writes=[("gb", g_)])
            p.add("dve", lambda e, gt=gt, dc=dc: e.tensor_tensor(out=mT[:, dc, :], in0=gt[:, 0:T], in1=gt[:, T:2 * T], op=ALU.add),
                  reads=[("gb", g_)], writes=[("mT", dc)])
        for dc in range(KC):
            wb = rrn(cx, "wb", 2)
            w = cx.wbuf[wb]
            wload(cx, w[:, 0:2048], W["wo"][dc], [], [("wb", wb, 0), ("wb", wb, 1)], "wb%d" % wb)
            pa = rrn(cx, "mob", 2) * NH
            pk = [("ps", pa + h) for h in range(NH)]
            for kc in range(16):
                for h in range(NH):
                    cx.mm(cx.bank(pa + h), w[:, kc * 128:(kc + 1) * 128], mT[:, kc, h * 512:(h + 1) * 512], kc == 0, kc == 15,
                          reads=[("wb", wb, 0), ("wb", wb, 1), ("mT", kc)], writes=[("ps", pa + h)])
            b = rrn(cx, "xb", 3)
            ft = cx.xbuf[b]
            cx.act(ft[:, 0:T], cx.bank(pa, NH), AF.Copy, reads=pk, writes=[("xb", b)])
            sb_ = rrn(cx, "sq", 2)
            sq = cx.sqbuf[sb_]
            cx.act(sq[:, 0:T], cx.bank(pa, NH), AF.Square, reads=pk, writes=[("sq", sb_)])
            cx.store(cx.fscr[dc * 128:(dc + 1) * 128, 0:T], ft[:, 0:T], reads=[("xb", b)], writes=[("fs", "f", dc)],
                     key="xb%d" % b)
            for h in range(NH):
                cx.mm(cx.bank(cx.ssbank + h), cx.ones[:, :], sq[:, h * 512:(h + 1) * 512], dc == 0, dc == KC - 1,
                      reads=[("sq", sb_), "ones"], writes=[("ps", cx.ssbank + h)])
        rstd_from_ss(cx, T, D)
        combine_residual(cx, cx.fscr, cx.h1, cx.h2, t0, T, cx.gv("mix_post_g"), "fs", "h1o", "h2", half=False,
                         xoff=ntok_ctx)


VEC_NAMES = ["ffn1_pre_g", "ffn1_post_g", "mix_pre_g", "mix_post_g", "ffn2_pre_g", "ffn2_post_g"]
NVEC = len(VEC_NAMES) * KC
W_SHAPES = {
    "w1g": [FC, 128, KC * 128], "w1u": [FC, 128, KC * 128], "w1d": [KC, 128, FC * 128],
    "w2g": [FC, 128, KC * 128], "w2u": [FC, 128, KC * 128], "w2d": [KC, 128, FC * 128],
    "wq": [16, 128, 2048], "wk": [16, 128, 2048], "wxbc": [48, 128, 2048], "wdt": [1, 128, KC * 64],
    "wg": [32, 128, 2048], "wv": [4, 128, KC * 512], "wz": [8, 128, KC * 512],
    "wba": [16, 128, 2048], "wbs": [16, 128, 4096], "wo": [16, 128, 2048],
}
SM_CONVW = 0
SM_CONVB = 192
SM_SSMV = 240
SM_LAM = 243
SM_SUBG = 247
SM_FLAGS = 249
SM_DSK = 251
NSM = 315
ALL_STAGES = ("ffn1", "inproj", "attn", "conv", "ssd", "merge", "ffn2")


def build(T=1024, stages=ALL_STAGES, ntok_ctx=HALF, ntok_own=HALF, debug=(), ext_in=(), dbg_chunks=()):
    nc = bass.Bass("TRN2", target_bir_lowering=False)
    NT = ntok_ctx + ntok_own

    def din(name, shape, dt=F32):
        return nc.dram_tensor(name, shape, dt, kind="ExternalInput").ap()

    def scr(name, shape, dt=F32):
        kind = "ExternalOutput" if name in debug else ("ExternalInput" if name in ext_in else "Internal")
        return nc.dram_tensor(name, shape, dt, kind=kind).ap()

    xT = din("xT", [D, NT])
    vecs_d = din("vecs", [128, NVEC])
    sm_d = din("smalls", [128, NSM])
    consts_d = din("consts", [128, 4 * 128])
    ngb_d = din("ngb", [128, DIN])
    W = {k: din(k, v) for k, v in W_SHAPES.items()}
    outT = nc.dram_tensor("outT", [D, ntok_own], F32, kind="ExternalOutput").ap()
    with ExitStack() as stack:
        cx = Ctx(nc, stack)
        p = cx.p
        cx.ngb_d = ngb_d
        cx.dbg_chunks = dbg_chunks

        def dbg(name, ap, keys):
            t = nc.dram_tensor("dbg_" + name, list(ap.shape), ap.dtype, kind="ExternalOutput").ap()
            cx.store(t, ap, reads=keys, writes=[("dbg", name)], key="dbg_" + name)
        cx.dbg = dbg
        cx.h1 = scr("h1", [D, NT])
        cx.h2 = scr("h2", [D, ntok_own])
        cx.fscr = scr("fscr", [D, T])
        cx.qT = scr("qT", [NQK, ntok_own], BF16)
        cx.kT = scr("kT", [NQK, NT], BF16)
        cx.V = scr("V", [NT, NV], BF16)
        cx.Z = scr("Z", [ntok_own, DIN])
        cx.xbcT = scr("xbcT", [6144, 4 + NT])
        cx.dtT = scr("dtT", [64, NT])
        cx.gT = scr("gT", [2 * D, ntok_own])
        cx.xcT = scr("xcT", [DIN, NT])
        cx.bcT = scr("bcT", [2 * NBC, NT], BF16)
        cx.oattT = scr("oattT", [NV, ntok_own], BF16)
        cx.ossmT = scr("ossmT", [DIN, ntok_own], BF16)
        wsc = (scr("wsc_g", [FC, 128, KC * 128], BF16), scr("wsc_u", [FC, 128, KC * 128], BF16),
               scr("wsc_d", [KC, 128, FC * 128], BF16))
        cx.vecs = cx.sb("vecsb", [128, NVEC], F32)
        sm = cx.sb("smalls_sb", [128, NSM], F32)
        cf = cx.sb("consts_sb", [128, 4 * 128], F32)
        cb16 = cx.sb("consts_bf", [128, 3 * 128], BF16)
        cx.rstd = cx.sb("rstd", [128, 1024], F32)
        cx.neglam = cx.sb("neglam", [128, 4], F32)
        cx.subg = cx.sb("subg", [128, 2], F32)
        zero = cx.sb("zero", [128, 48 * 4], F32)

        cx.arena = cx.sb("arena", [128, 98 * 1024], BF16)
        cx.ident_f, cx.ones_f, cx.tri_f, cx.stri_f = (cf[:, i * 128:(i + 1) * 128] for i in range(4))
        cx.ident_bf, cx.ones, cx.tri_bf = (cb16[:, i * 128:(i + 1) * 128] for i in range(3))
        cx.convw = sm[:, SM_CONVW:SM_CONVW + 192]
        cx.convb = sm[:, SM_CONVB:SM_CONVB + 48]
        cx.ssmv = sm[:, SM_SSMV:SM_SSMV + 3]
        cx.flags = sm[:, SM_FLAGS:SM_FLAGS + 2]
        cx.dskb = sm[:, SM_DSK:SM_DSK + 64]
        cx.negb = cx.sb("negb", [128, 48], F32)
        p.add("dve", lambda e: e.tensor_scalar(out=cx.negb[:, :], in0=sm[:, SM_CONVB:SM_CONVB + 48], scalar1=-1.0,
                                                scalar2=None, op0=ALU.mult), reads=["convw"], writes=["negb"])
        cx.gv = lambda name: cx.vecs[:, VEC_NAMES.index(name) * KC:(VEC_NAMES.index(name) + 1) * KC]
        cx.load(cx.vecs[:, :], vecs_d, reads=[], writes=["vecs"], key="vecs")
        cx.load(sm[:, :], sm_d, reads=[], writes=["convw", "ssmv", "flags", "dskb", "smraw"], key="sm")
        cx.load(cf[:, :], consts_d, reads=[], writes=["constsf"], key="cf")
        p.add("dve", lambda e: e.tensor_copy(out=cb16[:, :], in_=cf[:, 0:384]), reads=["constsf"], writes=["consts", "ones"])
        p.add("dve", lambda e: e.memset(zero[:, :], 0.0), writes=["zero"])
        cx.store(cx.xbcT[:, 0:4].rearrange("(c p) t -> p c t", p=128), zero[:, :].rearrange("p (c t) -> p c t", c=48),
                 reads=["zero"], writes=[("scr", "pad")], key="zero")
        cx.act(sm[0:64, SM_SSMV + 2:SM_SSMV + 3], sm[0:64, SM_SSMV + 1:SM_SSMV + 2], AF.Exp, reads=["ssmv"], writes=["ssmv"])
        p.add("dve", lambda e: e.tensor_scalar(out=sm[0:64, SM_SSMV + 2:SM_SSMV + 3], in0=sm[0:64, SM_SSMV + 2:SM_SSMV + 3],
                                                scalar1=-1.0, scalar2=None, op0=ALU.mult), reads=["ssmv"], writes=["ssmv"])
        lam = sm[:, SM_LAM:SM_LAM + 4]
        nl = cx.neglam
        p.add("dve", lambda e: e.tensor_tensor(out=nl[:, 0:1], in0=lam[:, 0:1], in1=lam[:, 1:2], op=ALU.mult),
              reads=["smraw"], writes=["nl"])
        p.add("dve", lambda e: e.tensor_tensor(out=nl[:, 1:2], in0=lam[:, 2:3], in1=lam[:, 3:4], op=ALU.mult),
              reads=["smraw", "nl"], writes=["nl"])
        cx.mm(cx.bank(0)[:, 0:2], cx.ones_f, nl[:, 0:2], True, True, reads=["nl", "constsf"], writes=[("ps", 0)])
        cx.act(nl[:, 2:4], cx.bank(0)[:, 0:2], AF.Exp, reads=[("ps", 0)], writes=["nl2"])
        p.add("dve", lambda e: e.scalar_tensor_tensor(out=nl[:, 0:1], in0=nl[:, 3:4], scalar=-0.2, in1=nl[:, 2:3],
                                                       op0=ALU.add, op1=ALU.subtract), reads=["nl2", "nl"], writes=["neglam"])
        p.add("dve", lambda e: e.tensor_scalar(out=cx.subg[:, :], in0=sm[:, SM_SUBG:SM_SUBG + 2], scalar1=0.8, scalar2=None,
                                                op0=ALU.mult), reads=["smraw"], writes=["subg"])
        cx.amark = 0

        if "ffn1" in stages:
            cx.phase()
            layout_ffn(cx, T)
            ffn(cx, xT, cx.h1, NT, T, W["w1g"], W["w1u"], W["w1d"], cx.gv("ffn1_pre_g"), cx.gv("ffn1_post_g"), "x", "h1",
                wsc=wsc)
        if "inproj" in stages or "conv" in stages:
            cx.phase()
            side = ConvSide(cx, ntok_ctx, ntok_own, T) if "conv" in stages else None
            if "inproj" in stages:
                inproj(cx, W, ntok_ctx, ntok_own, T, side=side)
            if side is not None:
                side.drain()
        if "attn" in stages:
            cx.phase()
            attention(cx, ntok_ctx, ntok_own)
        if "ssd" in stages:
            cx.phase()
            ssd(cx, ntok_ctx, ntok_own)
        if "merge" in stages:
            cx.phase()
            merge(cx, W, ntok_ctx, ntok_own)
        if "ffn2" in stages:
            cx.phase()
            layout_ffn(cx, T)
            ffn(cx, cx.h2, outT, ntok_own, T, W["w2g"], W["w2u"], W["w2d"], cx.gv("ffn2_pre_g"), cx.gv("ffn2_post_g"),
                "h2", "out", wsc=wsc)
        cx.phase()
        p.finalize(stack)
    return nc


def tile_w(W):
    K, N = W.shape
    return np.ascontiguousarray(
        W.reshape(K // 128, 128, N // 128, 128).transpose(2, 1, 0, 3).reshape(N // 128, 128, K))


def tile_w_rows(W, rows):
    K, N = W.shape
    return np.ascontiguousarray(
        W.reshape(K // 128, 128, N // rows, rows).transpose(2, 1, 0, 3).reshape(N // rows, 128, (K // 128) * rows))


def col_vec(v):
    return np.ascontiguousarray(v.reshape(-1, 128).T)


def host_weights(inp):
    w_in = inp["w_in"][0]
    o = 0
    sl = {}
    for name, n in (("q", 2048), ("k", 2048), ("v", 2048), ("z", 4096), ("xbc", 6144), ("dt", 64), ("g", 4096)):
        sl[name] = w_in[:, o:o + n]
        o += n
    Wd = {
        "w1g": tile_w(inp["ffn1_w_gate"][0]), "w1u": tile_w(inp["ffn1_w_up"][0]), "w1d": tile_w(inp["ffn1_w_down"][0]),
        "w2g": tile_w(inp["ffn2_w_gate"][0]), "w2u": tile_w(inp["ffn2_w_up"][0]), "w2d": tile_w(inp["ffn2_w_down"][0]),
        "wq": tile_w(sl["q"]), "wk": tile_w(sl["k"]), "wxbc": tile_w(sl["xbc"]), "wdt": tile_w_rows(sl["dt"], 64),
        "wg": tile_w(sl["g"]), "wv": tile_w_rows(sl["v"], 512), "wz": tile_w_rows(sl["z"], 512),
        "wba": tile_w(inp["w_branch_att"][0]), "wbs": tile_w(inp["w_branch_ssm"][0]), "wo": tile_w(inp["w_out"][0]),
    }
    vecs = np.concatenate([col_vec(inp[n][0]) for n in VEC_NAMES], axis=1).astype(np.float32)
    sm = np.zeros((128, NSM), np.float32)
    cw = inp["ssm_conv_w"][0]
    sm[:, SM_CONVW:SM_CONVW + 192] = cw.reshape(4, 48, 128).transpose(2, 1, 0).reshape(128, 192)
    sm[:, SM_CONVB:SM_CONVB + 48] = col_vec(inp["ssm_conv_b"][0])
    sm[0:64, SM_SSMV] = inp["ssm_dt_bias"][0]
    sm[0:64, SM_SSMV + 1] = inp["ssm_a_log"][0]
    for i, n in enumerate(("att_lambda_q1", "att_lambda_k1", "att_lambda_q2", "att_lambda_k2")):
        sm[:, SM_LAM + i] = inp[n][0]
    sm[:, SM_SUBG:SM_SUBG + 2] = col_vec(inp["att_subln_g"][0])
    sm[:, SM_DSK:SM_DSK + 64] = inp["ssm_d"][0][None, :]
    ngb = np.ascontiguousarray(np.broadcast_to(inp["ssm_norm_g"][0][None, :], (128, DIN))).astype(np.float32)
    idx = np.arange(128)
    consts = np.concatenate([
        np.eye(128), np.ones((128, 128)),
        (idx[:, None] <= idx[None, :]).astype(np.float64),
        (idx[:, None] > idx[None, :]).astype(np.float64),
    ], axis=1).astype(np.float32)
    return Wd, vecs, sm, ngb, consts


def core_maps(inp, ntok_ctx=HALF, ntok_own=HALF, cores=range(8)):
    Wd, vecs, sm, ngb, consts = host_weights(inp)
    maps = []
    for c in cores:
        b, r = c // 2, c % 2
        x = inp["x"][b]
        xc = x[0:ntok_ctx]
        xo = x[r * ntok_own + (ntok_ctx if False else 0):][:ntok_own] if r == 0 else x[ntok_ctx:ntok_ctx + ntok_own]
        xT = np.ascontiguousarray(np.concatenate([xc, xo], axis=0).T)
        smc = sm.copy()
        smc[:, SM_FLAGS] = 0.0 if r == 1 else -30000.0
        smc[:, SM_FLAGS + 1] = 1.0 if r == 1 else 0.0
        m = {"xT": xT, "vecs": vecs, "smalls": smc, "consts": consts, "ngb": ngb}
        m.update(Wd)
        maps.append(m)
    return maps


_NC_CACHE = {}


def kernel(**inputs):
    inp = {k: np.asarray(v) for k, v in inputs.items()}
    if "nc" not in _NC_CACHE:
        _NC_CACHE["nc"] = build()
    nc = _NC_CACHE["nc"]
    maps = core_maps(inp)
    res = run_bass_kernel_spmd(nc, maps, core_ids=list(range(8)))
    out = np.empty((4, SEQ, D), np.float32)
    for c in range(8):
        b, r = c // 2, c % 2
        out[b, r * HALF:(r + 1) * HALF, :] = res.results[c]["outT"].T
    return out
```

```python
import math
from contextlib import ExitStack
import numpy as np
import concourse.bass as bass
import concourse.mybir as mybir
from concourse.bass_utils import run_bass_kernel_spmd

F32 = mybir.dt.float32
BF16 = mybir.dt.bfloat16
F32R = mybir.dt.float32r
AF = mybir.ActivationFunctionType
ALU = mybir.AluOpType
AX = mybir.AxisListType

D = 2048
KC = 16
DFF = 5632
FC = 44
SEQ = 4096
HALF = 2048
NQK = 2048
NV = 2048
DIN = 4096
NBC = 1024
NHEAD = 64
EPS = 1e-6


class _Op:
    __slots__ = ("eng", "emit", "deps", "sem", "val", "signal", "idx")


class Prog:
    ENGS = ("pe", "act", "dve", "pool", "sp")

    def __init__(self, nc):
        self.nc = nc
        self.ops = []
        self.ks = {}
        self.dma_cnt = {}

    def add(self, eng, emit, reads=(), writes=(), dma=None):
        i = len(self.ops)
        op = _Op()
        op.eng, op.emit, op.idx, op.signal = eng, emit, i, False
        if dma is not None:
            c = self.dma_cnt.get(dma, 0) + 1
            self.dma_cnt[dma] = c
            op.sem, op.val = ("dma", dma), 16 * c
        else:
            op.sem, op.val = eng, None
        deps = set()
        ks = self.ks
        for k in reads:
            st = ks.get(k)
            if st is not None and st[0] is not None:
                deps.add(st[0])
        for k in writes:
            st = ks.get(k)
            if st is not None:
                if st[0] is not None:
                    deps.add(st[0])
                deps.update(st[1].values())
        for k in reads:
            st = ks.get(k)
            if st is None:
                st = ks[k] = [None, {}]
            st[1][op.sem] = i
        for k in writes:
            st = ks.get(k)
            if st is None:
                st = ks[k] = [None, {}]
            st[0] = i
            st[1] = {}
        deps.discard(i)
        op.deps = deps
        self.ops.append(op)
        return i

    def barrier(self):
        deps = set()
        for st in self.ks.values():
            if st[0] is not None:
                deps.add(st[0])
            deps.update(st[1].values())
        for e in self.ENGS:
            op = _Op()
            op.eng, op.emit, op.idx, op.signal = e, (lambda _e: None), len(self.ops), False
            op.sem, op.val = e, None
            op.deps = set(deps)
            self.ops.append(op)

    def finalize(self, stack):
        nc = self.nc
        ops = self.ops
        for op in ops:
            for j in op.deps:
                d = ops[j]
                if d.eng == "pe" and op.eng == "pe" and d.sem == "pe" and op.sem == "pe":
                    continue
                d.signal = True
        cnt = {e: 0 for e in self.ENGS}
        for op in ops:
            if op.sem in cnt and op.signal:
                cnt[op.sem] += 1
                op.val = cnt[op.sem]
        sems = {}
        for e in self.ENGS:
            sems[e] = stack.enter_context(nc.semaphore("s_" + e))
        for k in self.dma_cnt:
            sems[("dma", k)] = stack.enter_context(nc.semaphore("d_" + str(k)))
        per = {e: [] for e in self.ENGS}
        for op in ops:
            per[op.eng].append(op)

        def run(eng_name, e):
            waited = {}
            for op in per[eng_name]:
                need = {}
                for j in op.deps:
                    d = ops[j]
                    if d.eng == "pe" and op.eng == "pe" and d.sem == "pe" and op.sem == "pe":
                        continue
                    if need.get(d.sem, 0) < d.val:
                        need[d.sem] = d.val
                for s, v in need.items():
                    if waited.get(s, 0) < v:
                        e.wait_ge(sems[s], v)
                        waited[s] = v
                ins = op.emit(e)
                if ins is not None:
                    if op.sem == eng_name:
                        if op.signal:
                            ins.then_inc(sems[eng_name], 1)
                    else:
                        ins.then_inc(sems[op.sem], 16)

        block = stack.enter_context(nc.Block())

        @block.tensor
        def _(e):
            run("pe", e)

        @block.scalar
        def _(e):
            run("act", e)

        @block.vector
        def _(e):
            run("dve", e)

        @block.gpsimd
        def _(e):
            run("pool", e)

        @block.sync
        def _(e):
            run("sp", e)


class Ctx:
    def __init__(self, nc, stack):
        self.nc = nc
        self.stack = stack
        self.p = Prog(nc)
        self.uid = 0
        self.psum = stack.enter_context(nc.psum_tensor("psum", [128, 4096], F32))
        self.rr = {}
        self.arena = None
        self.apos = 0
        self.amark = 0

    def alloc(self, cols, dt):
        n = cols * (2 if dt == F32 else 1)
        n = (n + 15) // 16 * 16
        a = self.arena[:, self.apos:self.apos + n]
        self.apos += n
        assert self.apos <= self.arena.shape[1], ("arena overflow", self.apos)
        if dt == F32:
            a = a.bitcast(F32)
        return a[:, 0:cols]

    def phase(self):
        self.p.barrier()
        self.apos = self.amark
        self.rr = {}

    def sb(self, name, shape, dt):
        return self.stack.enter_context(self.nc.sbuf_tensor(name, shape, dt))

    def dram(self, name, shape, dt, kind="Internal"):
        return self.nc.dram_tensor(name, shape, dt, kind=kind).ap()

    def bank(self, b, n=1):
        return self.psum[:, b * 512:(b + n) * 512]

    def mm(self, out, lhsT, rhs, start, stop, reads, writes):
        self.p.add("pe", lambda e: e.matmul(out, lhsT, rhs, start=start, stop=stop),
                   reads=reads, writes=writes)

    def act(self, out, in_, func, reads, writes, bias=None, scale=None, accum_out=None):
        kw = {}
        if bias is not None:
            kw["bias"] = bias
        if scale is not None:
            kw["scale"] = scale
        if accum_out is not None:
            kw["accum_out"] = accum_out
        self.p.add("act", lambda e: e.activation(out, in_, func, **kw), reads=reads, writes=writes)

    def load(self, out, in_, reads, writes, key, eng="sp"):
        self.p.add(eng, lambda e: e.dma_start(out=out, in_=in_), reads=reads, writes=writes, dma=key)

    def store(self, out, in_, reads, writes, key, eng="sp"):
        self.p.add(eng, lambda e: e.dma_start(out=out, in_=in_), reads=reads, writes=writes, dma=key)


def norm_gen(cx, src, t0, T, gcol, uT, ukey, tag, ssb=None, rstd=None, rkey="rstd"):
    p = cx.p
    NH = T // 512
    xb = cx.xbuf
    ssb = cx.ssbank if ssb is None else ssb
    rstd = cx.rstd if rstd is None else rstd
    for ps in range(2):
        for kc in range(KC):
            b = cx.rr["xb"] = (cx.rr.get("xb", -1) + 1) % len(xb)
            xt = xb[b]
            cx.load(xt[:, 0:T], src[kc * 128:(kc + 1) * 128, t0:t0 + T], reads=[(tag, "src", kc)],
                    writes=[("xb", b)], key="xb%d" % b)
            if ps == 0:
                sb_ = cx.rr["sq"] = (cx.rr.get("sq", -1) + 1) % 2
                sq = cx.sqbuf[sb_]
                cx.act(sq[:, 0:T], xt[:, 0:T], AF.Square, reads=[("xb", b)], writes=[("sq", sb_)])
                for h in range(NH):
                    cx.mm(cx.bank(ssb + h), cx.ones[:, :], sq[:, h * 512:(h + 1) * 512],
                          start=(kc == 0), stop=(kc == KC - 1),
                          reads=[("sq", sb_), "ones"], writes=[("ps", ssb + h)])
            else:
                o = uT[:, kc, 0:T]
                g = gcol[:, kc:kc + 1]
                r = rstd[:, 0:T]
                p.add("dve", lambda e, o=o, xt=xt, g=g, r=r, T=T: e.scalar_tensor_tensor(
                    out=o, in0=xt[:, 0:T], scalar=g, in1=r, op0=ALU.mult, op1=ALU.mult),
                    reads=[("xb", b), rkey, "vecs"], writes=[(ukey, kc)])
            yield
        if ps == 0:
            rstd_from_ss(cx, T, D, ssb, rstd, rkey)
            yield


def norm_to_uT(cx, src, t0, T, gcol, uT, ukey, tag, **kw):
    for _ in norm_gen(cx, src, t0, T, gcol, uT, ukey, tag, **kw):
        pass


def rstd_from_ss(cx, T, n, ssb=None, rstd=None, rkey="rstd"):
    NH = T // 512
    ssb = cx.ssbank if ssb is None else ssb
    rstd = cx.rstd if rstd is None else rstd
    r = rstd[:, 0:T]
    ss = cx.bank(ssb, NH)
    cx.p.add("dve", lambda e: e.tensor_scalar(out=r, in0=ss, scalar1=1.0 / n, scalar2=EPS,
                                               op0=ALU.mult, op1=ALU.add),
             reads=[("ps", ssb + h) for h in range(NH)], writes=[rkey])
    cx.act(r, r, AF.Sqrt, reads=[rkey], writes=[rkey])
    cx.p.add("dve", lambda e: e.reciprocal(out=r, in_=r), reads=[rkey], writes=[rkey])


def wload(cx, dst, src, key_r, key_w, semkey):
    cx.load(dst, src, reads=key_r, writes=key_w, key=semkey, eng="pool")


def combine_gen(cx, fsrc, xsrc, dst, t0, T, gcol, tag, xtag, otag, half, xoff=0, rstd=None, rkey="rstd"):
    p = cx.p
    xb = cx.xbuf
    rstd = cx.rstd if rstd is None else rstd
    for dc in range(KC):
        b1 = cx.rr["xb"] = (cx.rr.get("xb", -1) + 1) % len(xb)
        ft = xb[b1]
        cx.load(ft[:, 0:T], fsrc[dc * 128:(dc + 1) * 128, 0:T], reads=[(tag, "f", dc)],
                writes=[("xb", b1)], key="xb%d" % b1)
        b2 = cx.rr["xb"] = (cx.rr.get("xb", -1) + 1) % len(xb)
        xt = xb[b2]
        cx.load(xt[:, 0:T], xsrc[dc * 128:(dc + 1) * 128, xoff + t0:xoff + t0 + T], reads=[(xtag, "src", dc)],
                writes=[("xb", b2)], key="xb%d" % b2)
        g = gcol[:, dc:dc + 1]
        r = rstd[:, 0:T]
        p.add("dve", lambda e, ft=ft, g=g, r=r: e.scalar_tensor_tensor(
            out=ft[:, 0:T], in0=ft[:, 0:T], scalar=g, in1=r, op0=ALU.mult, op1=ALU.mult),
            reads=[("xb", b1), rkey, "vecs"], writes=[("xb", b1)])
        if half:
            p.add("dve", lambda e, ft=ft, xt=xt: e.scalar_tensor_tensor(
                out=ft[:, 0:T], in0=ft[:, 0:T], scalar=0.5, in1=xt[:, 0:T], op0=ALU.mult, op1=ALU.add),
                reads=[("xb", b1), ("xb", b2)], writes=[("xb", b1)])
        else:
            p.add("dve", lambda e, ft=ft, xt=xt: e.tensor_tensor(
                out=ft[:, 0:T], in0=ft[:, 0:T], in1=xt[:, 0:T], op=ALU.add),
                reads=[("xb", b1), ("xb", b2)], writes=[("xb", b1)])
        cx.store(dst[dc * 128:(dc + 1) * 128, t0:t0 + T], ft[:, 0:T], reads=[("xb", b1)],
                 writes=[(otag, "src", dc)], key="xb%d" % b1)
        yield


def combine_residual(*a, **kw):
    for _ in combine_gen(*a, **kw):
        pass


def ffn(cx, src, dst, ntok, T, wg, wu, wd, gpre, gpost, tag, otag, wsc=None):
    p = cx.p
    NH = T // 512
    uT = cx.uT
    hid = cx.hid
    WB = cx.wbuf
    tiles = list(range(0, ntok, T))
    PSB = 4 if NH == 2 else 6
    norm_to_uT(cx, src, tiles[0], T, gpre, uT, "uT", tag, ssb=PSB, rstd=cx.rstdA, rkey="rstdA")
    comb = None
    for ti, t0 in enumerate(tiles):
        for j in range(FC):
            wb = cx.rr["wb"] = (cx.rr.get("wb", -1) + 1) % len(WB)
            w = WB[wb]
            reuse = wsc is not None and len(tiles) > 1
            if reuse and ti > 0:
                cx.load(w[:, 0:KC * 128], wsc[0][j], reads=[(tag, "wscg", j)], writes=[("wb", wb, 0)], key="wb%d" % wb, eng="pool")
                cx.load(w[:, KC * 128:2 * KC * 128], wsc[1][j], reads=[(tag, "wscu", j)], writes=[("wb", wb, 1)],
                        key="wb%d" % wb, eng="pool")
            else:
                wload(cx, w[:, 0:KC * 128], wg[j], [], [("wb", wb, 0)], "wb%d" % wb)
                wload(cx, w[:, KC * 128:2 * KC * 128], wu[j], [], [("wb", wb, 1)], "wb%d" % wb)
            gb = cx.rr["gub"] = (cx.rr.get("gub", -1) + 1) % 2
            gbank = gb * 2 * NH
            ubank = gbank + NH
            for which, bank0 in ((0, gbank), (1, ubank)):
                for kc in range(KC):
                    lw = w[:, which * KC * 128 + kc * 128: which * KC * 128 + (kc + 1) * 128]
                    for h in range(NH):
                        cx.mm(cx.bank(bank0 + h), lw, uT[:, kc, h * 512:(h + 1) * 512],
                              start=(kc == 0), stop=(kc == KC - 1),
                              reads=[("wb", wb, which), ("uT", kc)], writes=[("ps", bank0 + h)])
            sl = cx.rr["sil"] = (cx.rr.get("sil", -1) + 1) % 2
            st = cx.silb[sl]
            cx.act(st[:, 0:T], cx.bank(gbank, NH), AF.Silu,
                   reads=[("ps", gbank + h) for h in range(NH)], writes=[("sil", sl)])
            if reuse and ti == 0:
                cx.store(wsc[0][j], w[:, 0:KC * 128], reads=[("wb", wb, 0)], writes=[(tag, "wscg", j)],
                         key="wbs%d" % wb, eng="act")
                cx.store(wsc[1][j], w[:, KC * 128:2 * KC * 128], reads=[("wb", wb, 1)], writes=[(tag, "wscu", j)],
                         key="wbs%d" % wb, eng="act")
            ho = hid[:, j, 0:T]
            ub = cx.bank(ubank, NH)
            p.add("dve", lambda e, ho=ho, st=st, ub=ub: e.tensor_tensor(
                out=ho, in0=st[:, 0:T], in1=ub, op=ALU.mult),
                reads=[("sil", sl)] + [("ps", ubank + h) for h in range(NH)],
                writes=[("hid", j)])
            if comb is not None and j % 2 == 1:
                next(comb, None)
        if comb is not None:
            for _ in comb:
                pass
        nxt = None
        if ti + 1 < len(tiles):
            nxt = norm_gen(cx, src, tiles[ti + 1], T, gpre, uT, "uT", tag, ssb=PSB, rstd=cx.rstdA, rkey="rstdA")
        fscr = cx.fscr
        for dc in range(KC):
            wb = cx.rr["wb"] = (cx.rr.get("wb", -1) + 1) % len(WB)
            w = WB[wb]
            reuse = wsc is not None and len(tiles) > 1
            if reuse and ti > 0:
                cx.load(w[:, 0:FC * 128], wsc[2][dc], reads=[(tag, "wscd", dc)], writes=[("wb", wb, 0), ("wb", wb, 1)],
                        key="wb%d" % wb, eng="pool")
            else:
                wload(cx, w[:, 0:FC * 128], wd[dc], [], [("wb", wb, 0), ("wb", wb, 1)], "wb%d" % wb)
            db = cx.rr["dnb"] = (cx.rr.get("dnb", -1) + 1) % 2
            dbank = db * NH
            for fc in range(FC):
                for h in range(NH):
                    cx.mm(cx.bank(dbank + h), w[:, fc * 128:(fc + 1) * 128], hid[:, fc, h * 512:(h + 1) * 512],
                          start=(fc == 0), stop=(fc == FC - 1),
                          reads=[("wb", wb, 0), ("wb", wb, 1), ("hid", fc)], writes=[("ps", dbank + h)])
            b = cx.rr["xb"] = (cx.rr.get("xb", -1) + 1) % len(cx.xbuf)
            ft = cx.xbuf[b]
            psr = [("ps", dbank + h) for h in range(NH)]
            cx.act(ft[:, 0:T], cx.bank(dbank, NH), AF.Copy, reads=psr, writes=[("xb", b)])
            sb_ = cx.rr["sq"] = (cx.rr.get("sq", -1) + 1) % 2
            sq = cx.sqbuf[sb_]
            cx.act(sq[:, 0:T], cx.bank(dbank, NH), AF.Square, reads=psr, writes=[("sq", sb_)])
            if reuse and ti == 0:
                cx.store(wsc[2][dc], w[:, 0:FC * 128], reads=[("wb", wb, 0), ("wb", wb, 1)], writes=[(tag, "wscd", dc)],
                         key="wbs%d" % wb, eng="act")
            cx.store(fscr[dc * 128:(dc + 1) * 128, 0:T], ft[:, 0:T], reads=[("xb", b)],
                     writes=[("fs", "f", dc)], key="xb%d" % b)
            for h in range(NH):
                cx.mm(cx.bank(cx.ssbank + h), cx.ones[:, :], sq[:, h * 512:(h + 1) * 512],
                      start=(dc == 0), stop=(dc == KC - 1),
                      reads=[("sq", sb_), "ones"], writes=[("ps", cx.ssbank + h)])
            if nxt is not None and dc >= 1:
                for _ in range(3):
                    next(nxt, None)
        if nxt is not None:
            for _ in nxt:
                pass
        rstd_from_ss(cx, T, D, cx.ssbank, cx.rstd, "rstd")
        comb = combine_gen(cx, fscr, src, dst, t0, T, gpost, "fs", tag, otag, half=True)
    for _ in comb:
        pass


def layout_ffn(cx, T):
    cx.uT = cx.alloc(KC * T, BF16).rearrange("p (k t) -> p k t", k=KC)
    cx.hid = cx.alloc(FC * T, BF16).rearrange("p (k t) -> p k t", k=FC)
    cx.wbuf = [cx.alloc(FC * 128, BF16) for _ in range(3)]
    cx.xbuf = [cx.alloc(T, F32) for _ in range(4)]
    cx.sqbuf = [cx.alloc(T, BF16) for _ in range(2)]
    cx.silb = [cx.alloc(T, F32) for _ in range(2)]
    cx.rstdA = cx.alloc(T, F32)
    cx.ssbank = 8 - T // 512


def rrn(cx, name, n):
    v = cx.rr[name] = (cx.rr.get(name, -1) + 1) % n
    return v


def inproj(cx, W, ntok_ctx, ntok_own, T, side=None):
    p = cx.p
    NH = T // 512
    NT = ntok_ctx + ntok_own
    uTs = [cx.alloc(KC * T, BF16).rearrange("p (k t) -> p k t", k=KC) for _ in range(2)]
    ukeys = ["uTa", "uTb"]
    cx.wbuf = [cx.alloc(KC * 512, BF16) for _ in range(3)]
    cx.xbuf = [cx.alloc(T, F32) for _ in range(3)]
    cx.sqbuf = [cx.alloc(T, BF16) for _ in range(2)]
    cx.ssbank = 8 - NH
    gm = cx.gv("mix_pre_g")
    tiles = list(range(0, NT, T))
    norm_to_uT(cx, cx.h1, tiles[0], T, gm, uTs[0], ukeys[0], "h1")
    for ti, t0 in enumerate(tiles):
        own = t0 >= ntok_ctx
        to = t0 - ntok_ctx
        uT = uTs[ti % 2]
        uk = ukeys[ti % 2]
        nxt = None
        if ti + 1 < len(tiles):
            nxt = norm_gen(cx, cx.h1, tiles[ti + 1], T, gm, uTs[(ti + 1) % 2], ukeys[(ti + 1) % 2], "h1")

        def fm(wsrc, nchunk, epi, rows=128):
            for j in range(nchunk):
                wb = rrn(cx, "wb", 3)
                w = cx.wbuf[wb]
                wload(cx, w[:, 0:KC * rows], wsrc[j], [], [("wb", wb, 0), ("wb", wb, 1)], "wb%d" % wb)
                pb = rrn(cx, "pjb", 3) * NH
                for kc in range(KC):
                    for h in range(NH):
                        cx.mm(cx.bank(pb + h)[0:rows, :], w[:, kc * rows:(kc + 1) * rows], uT[:, kc, h * 512:(h + 1) * 512],
                              start=(kc == 0), stop=(kc == KC - 1),
                              reads=[("wb", wb, 0), ("wb", wb, 1), (uk, kc)], writes=[("ps", pb + h)])
                epi(j, cx.bank(pb, NH), [("ps", pb + h) for h in range(NH)])
                if side is not None:
                    side.step()

        def epi_bf(dst, col0, scale=None):
            def f(j, ps, psk):
                b = rrn(cx, "sq", 2)
                o = cx.sqbuf[b]
                cx.act(o[:, 0:T], ps, AF.Copy, reads=psk, writes=[("sq", b)], scale=scale)
                cx.store(dst[j * 128:(j + 1) * 128, col0:col0 + T], o[:, 0:T], reads=[("sq", b)],
                         writes=[("scr", id(dst), j)], key="sq%d" % b)
            return f

        def epi_f32(dst, col0, func=AF.Copy, rows=128, isxbc=False):
            def f(j, ps, psk):
                b = rrn(cx, "xb", 3)
                o = cx.xbuf[b]
                cx.act(o[0:rows, 0:T], ps[0:rows, :], func, reads=psk, writes=[("xb", b)])
                wk = ("scr", "xbcT", j) if isxbc else ("scr", id(dst), j)
                cx.store(dst[j * rows:(j + 1) * rows, col0:col0 + T], o[0:rows, 0:T], reads=[("xb", b)],
                         writes=[wk], key="xb%d" % b)
                if isxbc and side is not None:
                    side.avail = (t0 // T) * 48 + j + 1
            return f

        if own:
            fm(W["wq"], 16, epi_bf(cx.qT, to, scale=1.0 / math.sqrt(128.0)))
        fm(W["wk"], 16, epi_bf(cx.kT, t0))
        fm(W["wxbc"], 48, epi_f32(cx.xbcT, 4 + t0, isxbc=True))
        fm(W["wdt"], 1, epi_f32(cx.dtT, t0, rows=64), rows=64)
        if own:
            fm(W["wg"], 32, epi_f32(cx.gT, to, func=AF.Sigmoid))

        def tm(wsrc, nslab, dst, row0, dt_bf):
            for s_ in range(nslab):
                wb = rrn(cx, "wb", 3)
                w = cx.wbuf[wb]
                wload(cx, w[:, 0:KC * 512], wsrc[s_], [], [("wb", wb, 0), ("wb", wb, 1)], "wb%d" % wb)
                for tb in range(T // 128):
                    pb = rrn(cx, "pjb2", 6)
                    for kc in range(KC):
                        cx.mm(cx.bank(pb), uT[:, kc, tb * 128:(tb + 1) * 128], w[:, kc * 512:(kc + 1) * 512],
                              start=(kc == 0), stop=(kc == KC - 1),
                              reads=[("wb", wb, 0), ("wb", wb, 1), (uk, kc)], writes=[("ps", pb)])
                    if dt_bf:
                        b = rrn(cx, "sq", 2)
                        o = cx.sqbuf[b]
                        key = ("sq", b)
                        sk = "sq%d" % b
                    else:
                        b = rrn(cx, "xb", 3)
                        o = cx.xbuf[b]
                        key = ("xb", b)
                        sk = "xb%d" % b
                    cx.act(o[:, 0:512], cx.bank(pb), AF.Copy, reads=[("ps", pb)], writes=[key])
                    r0 = row0 + tb * 128
                    cx.store(dst[r0:r0 + 128, s_ * 512:(s_ + 1) * 512], o[:, 0:512], reads=[key],
                             writes=[("scr", id(dst), "tm", s_)], key=sk)
                    if side is not None:
                        side.step()
                    if nxt is not None:
                        next(nxt, None)
                        next(nxt, None)

        tm(W["wv"], 4, cx.V, t0, True)
        if own:
            tm(W["wz"], 8, cx.Z, to, False)
        if nxt is not None:
            for _ in nxt:
                pass


def attention(cx, ntok_ctx, ntok_own, side=None, side_every=3):
    p = cx.p
    nblk = [0]
    NTK = ntok_ctx + ntok_own
    NKB = NTK // 128
    NCB = ntok_ctx // 128
    NQT = ntok_own // 512
    kT = [cx.alloc(2 * NTK, BF16).rearrange("p (m t) -> p m t", m=2) for _ in range(2)]
    Vt = [cx.alloc(NKB * 256, BF16).rearrange("p (b c) -> p b c", b=NKB) for _ in range(2)]
    qT = [cx.alloc(2 * ntok_own, BF16).rearrange("p (m t) -> p m t", m=2) for _ in range(2)]
    PT = [cx.alloc(512, BF16) for _ in range(6)]
    rinv = [cx.alloc(512, F32) for _ in range(2)]
    of = [cx.alloc(512, F32) for _ in range(2)]
    tmpf = cx.alloc(512, F32)
    sqb = cx.alloc(512, BF16)
    ob = [cx.alloc(512, BF16) for _ in range(2)]
    rst = cx.alloc(512, F32)
    SBK = ((0, 1), (0, 1))
    SB = (0, 1)
    OB = ((2, 3), (4, 5))
    RB = (6, 7)
    def load_head(h):
        hb = h % 2
        for m in range(2):
            cx.load(kT[hb][:, m, :], cx.kT[(2 * h + m) * 128:(2 * h + m + 1) * 128, 0:NTK], reads=[],
                    writes=[("kT", hb)], key="kT%d" % hb)
            cx.load(qT[hb][:, m, :], cx.qT[(2 * h + m) * 128:(2 * h + m + 1) * 128, 0:ntok_own], reads=[],
                    writes=[("qT", hb)], key="qT%d" % hb)
        cx.load(Vt[hb][:, :, :], cx.V[0:NTK, h * 256:(h + 1) * 256].rearrange("(b p) c -> p b c", p=128),
                reads=[], writes=[("Vt", hb)], key="Vt%d" % hb)

    load_head(0)
    for h in range(8):
        hb = h % 2
        if h + 1 < 8:
            load_head(h + 1)
        for qt in range(NQT):
            nkb = NCB + 4 * (qt + 1)

            def geom(kb):
                ob_ = kb - NCB
                r = ob_ - 4 * qt if ob_ >= 4 * qt else -1
                return r, (128 * r if r > 0 else 0)

            def emit_S(kb):
                r, c0 = geom(kb)
                par = kb % 2
                pts = []
                for m in range(2):
                    sbk = SBK[par][m]
                    cx.mm(cx.bank(sbk)[:, c0:512], kT[hb][:, m, kb * 128:(kb + 1) * 128],
                          qT[hb][:, m, qt * 512 + c0:(qt + 1) * 512], start=True, stop=True,
                          reads=[("kT", hb), ("qT", hb)], writes=[("ps", sbk)])
                    pi = rrn(cx, "pt", len(PT))
                    pt = PT[pi]
                    pts.append((pi, pt))
                    bias = cx.flags[:, 0:1] if kb < NCB else None
                    cx.act(pt[:, c0:512], cx.bank(sbk)[:, c0:512], AF.Exp, reads=[("ps", sbk), "flags"],
                           writes=[("pt", pi)], bias=bias)
                    if r >= 0:
                        p.add("dve", lambda e, pt=pt, c0=c0: e.tensor_tensor(
                            out=pt[:, c0:c0 + 128], in0=pt[:, c0:c0 + 128], in1=cx.tri_bf[:, :], op=ALU.mult),
                            reads=[("pt", pi), "consts"], writes=[("pt", pi)])
                return pts

            def emit_PV(kb, pts):
                r, c0 = geom(kb)
                for m in range(2):
                    pi, pt = pts[m]
                    for dvc in range(2):
                        cx.mm(cx.bank(OB[m][dvc])[:, c0:512], Vt[hb][:, kb, dvc * 128:(dvc + 1) * 128], pt[:, c0:512],
                              start=(kb == 0), stop=(kb == nkb - 1),
                              reads=[("pt", pi), ("Vt", hb)], writes=[("ps", OB[m][dvc])])
                    cx.mm(cx.bank(RB[m])[:, c0:512], cx.ones[:, :], pt[:, c0:512],
                          start=(kb == 0), stop=(kb == nkb - 1),
                          reads=[("pt", pi), "ones"], writes=[("ps", RB[m])])

            nxt = emit_S(0)
            for kb in range(nkb):
                cur = nxt
                if kb + 1 < nkb:
                    nxt = emit_S(kb + 1)
                emit_PV(kb, cur)
                nblk[0] += 1
                if side is not None and nblk[0] % side_every == 0:
                    next(side, None)
            for m in range(2):
                p.add("dve", lambda e, m=m: e.reciprocal(out=rinv[m][:, :], in_=cx.bank(RB[m])),
                      reads=[("ps", RB[m])], writes=[("rinv", m)])
            p.add("dve", lambda e: e.tensor_scalar(out=rinv[1][:, :], in0=rinv[1][:, :], scalar1=cx.neglam[:, 0:1],
                                                    scalar2=None, op0=ALU.mult),
                  reads=[("rinv", 1), "neglam"], writes=[("rinv", 1)])
            for dvc in range(2):
                p.add("dve", lambda e, dvc=dvc: e.tensor_tensor(out=of[dvc][:, :], in0=cx.bank(OB[0][dvc]),
                                                                 in1=rinv[0][:, :], op=ALU.mult),
                      reads=[("ps", OB[0][dvc]), ("rinv", 0)], writes=[("of", dvc)])
                p.add("dve", lambda e, dvc=dvc: e.tensor_tensor(out=tmpf[:, :], in0=cx.bank(OB[1][dvc]),
                                                                 in1=rinv[1][:, :], op=ALU.mult),
                      reads=[("ps", OB[1][dvc]), ("rinv", 1)], writes=["tmpf"])
                p.add("dve", lambda e, dvc=dvc: e.tensor_tensor(out=of[dvc][:, :], in0=of[dvc][:, :],
                                                                 in1=tmpf[:, :], op=ALU.add),
                      reads=["tmpf", ("of", dvc)], writes=[("of", dvc)])
                cx.act(sqb[:, :], of[dvc][:, :], AF.Square, reads=[("of", dvc)], writes=["asq"])
                cx.mm(cx.bank(SB[0]), cx.ones[:, :], sqb[:, :], start=(dvc == 0), stop=(dvc == 1),
                      reads=["asq", "ones"], writes=[("ps", SB[0])])
            p.add("dve", lambda e: e.tensor_scalar(out=rst[:, :], in0=cx.bank(SB[0]), scalar1=1.0 / 256.0, scalar2=EPS,
                                                    op0=ALU.mult, op1=ALU.add),
                  reads=[("ps", SB[0])], writes=["arst"])
            cx.act(rst[:, :], rst[:, :], AF.Sqrt, reads=["arst"], writes=["arst"])
            p.add("dve", lambda e: e.reciprocal(out=rst[:, :], in_=rst[:, :]), reads=["arst"], writes=["arst"])
            for dvc in range(2):
                p.add("dve", lambda e, dvc=dvc: e.scalar_tensor_tensor(
                    out=ob[dvc][:, :], in0=of[dvc][:, :], scalar=cx.subg[:, dvc:dvc + 1], in1=rst[:, :],
                    op0=ALU.mult, op1=ALU.mult),
                    reads=[("of", dvc), "arst", "subg"], writes=[("aob", dvc)])
                j = 2 * h + dvc
                cx.store(cx.oattT[j * 128:(j + 1) * 128, qt * 512:(qt + 1) * 512], ob[dvc][:, :],
                         reads=[("aob", dvc)], writes=[("scr", "oatt", j)], key="aob%d" % dvc, eng="pool")


def bc(ap, shape, axis):
    return ap.unsqueeze(axis).to_broadcast(shape)


class ConvSide:
    def __init__(self, cx, ntok_ctx, ntok_own, T, NB=4):
        self.cx = cx
        NT = ntok_ctx + ntok_own
        self.T = T
        self.NB = NB
        self.ntok_ctx = ntok_ctx
        self.xin = [cx.alloc(T + 8, F32) for _ in range(NB)]
        self.acc = [cx.alloc(T, F32) for _ in range(NB)]
        self.outf = [cx.alloc(T, F32) for _ in range(2)]
        self.outb = [cx.alloc(T, BF16) for _ in range(2)]
        self.units = [(cc, t0) for t0 in range(0, NT, T) for cc in range(48)]
        self.avail = 0
        self.need = 0
        self.gen = self._gen()
        self.done = False

    def step(self):
        if self.done or self.need > self.avail:
            return
        try:
            next(self.gen)
        except StopIteration:
            self.done = True

    def drain(self):
        self.avail = len(self.units)
        while not self.done:
            self.step()

    def _gen(self):
        cx = self.cx
        p = cx.p
        T = self.T
        NB = self.NB
        xin, acc, outf, outb = self.xin, self.acc, self.outf, self.outb
        for u0 in range(0, len(self.units), NB):
            batch = self.units[u0:u0 + NB]
            self.need = u0 + len(batch)
            yield
            for b, (cc, t0) in enumerate(batch):
                xi = xin[b]
                cx.load(xi[:, 0:T + 3], cx.xbcT[cc * 128:(cc + 1) * 128, 4 + t0 - 3:4 + t0 + T],
                        reads=[("scr", "xbcT", cc), ("scr", "pad")], writes=[("cin", b)], key="cin%d" % b)
                if t0 == self.ntok_ctx and self.ntok_ctx > 0:
                    p.add("dve", lambda e, xi=xi: e.tensor_scalar(out=xi[:, 0:3], in0=xi[:, 0:3], scalar1=cx.flags[:, 1:2],
                                                                  scalar2=None, op0=ALU.mult),
                          reads=[("cin", b), "flags"], writes=[("cin", b)])
                a = acc[b]
                w = cx.convw
                p.add("dve", lambda e, xi=xi, a=a, cc=cc: e.tensor_scalar(
                    out=a[:, 0:T], in0=xi[:, 0:T], scalar1=w[:, cc * 4:cc * 4 + 1], scalar2=None, op0=ALU.mult),
                    reads=[("cin", b), "convw"], writes=[("cacc", b)])
                for j in range(1, 4):
                    p.add("dve", lambda e, xi=xi, a=a, cc=cc, j=j: e.scalar_tensor_tensor(
                        out=a[:, 0:T], in0=xi[:, j:j + T], scalar=w[:, cc * 4 + j:cc * 4 + j + 1], in1=a[:, 0:T],
                        op0=ALU.mult, op1=ALU.add),
                        reads=[("cin", b), ("cacc", b), "convw"], writes=[("cacc", b)])
                yield
            for b, (cc, t0) in enumerate(batch):
                a = acc[b]
                if cc < 32:
                    ob_ = rrn(cx, "cof", 2)
                    o = outf[ob_]
                    okey, dstT, row0, sk = ("cof", ob_), cx.xcT, cc * 128, "cof%d" % ob_
                else:
                    ob_ = rrn(cx, "cob", 2)
                    o = outb[ob_]
                    okey, dstT, row0, sk = ("cob", ob_), cx.bcT, (cc - 32) * 128, "cob%d" % ob_
                cx.act(o[:, 0:T], a[:, 0:T], AF.Silu, reads=[("cacc", b), "convw"], writes=[okey],
                       bias=cx.convb[:, cc:cc + 1])
                cx.store(dstT[row0:row0 + 128, t0:t0 + T], o[:, 0:T], reads=[okey],
                         writes=[("scr", "conv", cc)], key=sk, eng="act")
            yield


def ssd(cx, ntok_ctx, ntok_own):
    p = cx.p
    NT = ntok_ctx + ntok_own
    NCH = NT // 128
    NCC = ntok_ctx // 128
    A3 = lambda ap, a, b_: ap.rearrange("p (a b) -> p a b", a=a)
    xfm = [cx.alloc(4096, F32) for _ in range(2)]
    bct = [cx.alloc(2048, BF16) for _ in range(2)]
    dtr = [cx.alloc(128, F32) for _ in range(2)]
    zb = cx.alloc(4096, F32)
    xtok = cx.alloc(4096, F32)
    xw = cx.alloc(4096, BF16)
    xdt = cx.alloc(4096, BF16)
    S = cx.alloc(4096, F32)
    Sbf = cx.alloc(4096, BF16)
    y = cx.alloc(4096, F32)
    ngb = cx.alloc(4096, F32)
    yn = cx.alloc(4096, BF16)
    oTs = cx.alloc(4096, BF16)
    rhsL = [cx.alloc(1024, F32) for _ in range(2)]
    Eb = [cx.alloc(1024, BF16) for _ in range(2)]
    Mb = [cx.alloc(1024, BF16) for _ in range(2)]
    cbm = [cx.alloc(128, BF16) for _ in range(2)]
    Btok = cx.alloc(1024, BF16)
    e1 = cx.alloc(128, F32)
    dtf = cx.alloc(128, F32)
    aT = cx.alloc(128, F32)
    da = cx.alloc(128, F32)
    ct = cx.alloc(128, F32)
    dte = cx.alloc(64, F32)
    w1 = cx.alloc(64, F32)
    cdec = cx.alloc(64, F32)
    din = cx.alloc(64, F32)
    tmpg = [cx.alloc(512, F32) for _ in range(2)]
    junk = cx.alloc(512, BF16)
    ssq = cx.alloc(8, F32)
    rs8 = cx.alloc(8, F32)
    cx.load(ngb[:, :], cx.ngb_d, reads=[], writes=["ngb"], key="ngb")
    p.add("dve", lambda e: e.memset(S[:, :], 0.0), writes=["S"])

    def loads(c):
        b = c % 2
        t0 = c * 128
        cx.load(A3(xfm[b], 32, 128), cx.xcT[:, t0:t0 + 128].rearrange("(c p) t -> p c t", p=128), reads=[],
                writes=[("xfm", b)], key="xfm%d" % b)
        cx.load(A3(bct[b], 16, 128), cx.bcT[:, t0:t0 + 128].rearrange("(c p) t -> p c t", p=128), reads=[],
                writes=[("bct", b)], key="bct%d" % b)
        cx.load(dtr[b][0:64, :], cx.dtT[:, t0:t0 + 128], reads=[], writes=[("dtr", b)], key="dtr%d" % b)

    loads(0)
    for c in range(NCH):
        b = c % 2
        own = c >= NCC
        tl0 = (c - NCC) * 128
        if c + 1 < NCH:
            loads(c + 1)
        if own:
            cx.load(zb[:, :], cx.Z[tl0:tl0 + 128, :], reads=[], writes=["z"], key="z")
        xf3 = A3(xfm[b], 32, 128)
        bc3 = A3(bct[b], 16, 128)
        cx.act(e1[0:64, :], dtr[b][0:64, :], AF.Exp, reads=[("dtr", b), "ssmv"], writes=["e1"], bias=cx.ssmv[0:64, 0:1])
        cx.act(dtf[0:64, :], e1[0:64, :], AF.Ln, reads=["e1"], writes=["dtf"], bias=1.0)
        p.add("dve", lambda e: e.tensor_scalar(out=aT[0:64, :], in0=dtf[0:64, :], scalar1=cx.ssmv[0:64, 2:3], scalar2=None,
                                                op0=ALU.mult), reads=["dtf", "ssmv"], writes=["aT"])
        mb = 7
        p.add("pe", lambda e: e.transpose(cx.bank(mb)[:, 0:64], dtf[0:64, :], cx.ident_f[0:64, 0:64]),
              reads=["dtf", "consts"], writes=[("ps", mb)])
        p.add("pe", lambda e: e.transpose(cx.bank(mb)[:, 64:128], aT[0:64, :], cx.ident_f[0:64, 0:64]),
              reads=["aT", "consts"], writes=[("ps", mb)])
        p.add("dve", lambda e: e.tensor_copy(out=da[:, :], in_=cx.bank(mb)[:, 0:128]), reads=[("ps", mb)], writes=["da"])
        cx.mm(cx.bank(mb)[:, 128:192], cx.tri_f[:, :], da[:, 64:128], True, True, reads=["da", "consts"], writes=[("ps", mb)])
        cx.mm(cx.bank(mb)[:, 192:256], cx.ones_f[:, :], da[:, 64:128], True, True, reads=["da", "consts"], writes=[("ps", mb)])
        p.add("dve", lambda e: e.tensor_copy(out=ct[:, :], in_=cx.bank(mb)[:, 128:256]), reads=[("ps", mb)], writes=["ct"])
        p.add("dve", lambda e: e.tensor_tensor(out=dte[:, :], in0=ct[:, 64:128], in1=ct[:, 0:64], op=ALU.subtract),
              reads=["ct"], writes=["dte"])
        cx.act(dte[:, :], dte[:, :], AF.Exp, reads=["dte"], writes=["dte"])
        p.add("dve", lambda e: e.tensor_tensor(out=w1[:, :], in0=da[:, 0:64], in1=dte[:, :], op=ALU.mult),
              reads=["da", "dte"], writes=["w1"])
        cx.act(cdec[:, :], ct[:, 64:128], AF.Exp, reads=["ct"], writes=["cdec"])
        if own:
            cx.act(din[:, :], ct[:, 0:64], AF.Exp, reads=["ct"], writes=["din"])
        for g in range(8):
            xb_ = rrn(cx, "xtb", 4)
            for i in range(4):
                p.add("pe", lambda e, g=g, i=i, xb_=xb_, xf3=xf3: e.transpose(cx.bank(xb_)[:, i * 128:(i + 1) * 128],
                                                                      xf3[:, 4 * g + i, :], cx.ident_f[:, :]),
                      reads=[("xfm", b), "consts"], writes=[("ps", xb_)])
            if g % 2 == 0:
                cx.act(xtok[:, g * 512:(g + 1) * 512], cx.bank(xb_), AF.Copy, reads=[("ps", xb_)], writes=[("xtok", g)])
            else:
                p.add("dve", lambda e, g=g, xb_=xb_: e.tensor_copy(out=xtok[:, g * 512:(g + 1) * 512], in_=cx.bank(xb_)),
                      reads=[("ps", xb_)], writes=[("xtok", g)])
        pbf = cx.bank(4, 1).bitcast(BF16)
        for g in range(8):
            p.add("pe", lambda e, g=g, bc3=bc3: e.transpose(pbf[:, g * 128:(g + 1) * 128], bc3[:, g, :], cx.ident_bf[:, :]),
                  reads=[("bct", b), "consts"], writes=[("ps", 4)])
        p.add("dve", lambda e: e.tensor_copy(out=Btok[:, :], in_=pbf[:, 0:1024]), reads=[("ps", 4)], writes=["Btok"])
        xkeys = [("xtok", g) for g in range(8)]
        p.add("dve", lambda e: e.tensor_tensor(out=A3(xw, 64, 64), in0=A3(xtok, 64, 64), in1=bc(w1[:, :], [128, 64, 64], 2),
                                                op=ALU.mult), reads=xkeys + ["w1"], writes=["xw"])
        if own:
            p.add("dve", lambda e: e.tensor_tensor(out=A3(xdt, 64, 64), in0=A3(xtok, 64, 64),
                                                    in1=bc(da[:, 0:64], [128, 64, 64], 2), op=ALU.mult),
                  reads=xkeys + ["da"], writes=["xdt"])
            cx.act(Sbf[:, :], S[:, :], AF.Copy, reads=["S"], writes=["Sbf"])
            def s1(g):
                rb = g % 2
                rl = rhsL[rb]
                p.add("dve", lambda e, g=g, rl=rl: e.tensor_tensor(
                    out=A3(rl, 8, 128), in0=bc(da[:, 64 + 8 * g:72 + 8 * g], [128, 8, 128], 2),
                    in1=bc(cx.tri_f[:, :], [128, 8, 128], 1), op=ALU.mult),
                    reads=["da", "consts"], writes=[("rhsL", rb)])
                cbs = cx.bank(5)[:, (g % 4) * 128:(g % 4 + 1) * 128]
                cx.mm(cbs, bc3[:, g, :], bc3[:, 8 + g, :], True, True, reads=[("bct", b)], writes=[("ps", 5)])
                cb_ = cbm[rb]
                p.add("dve", lambda e, cbs=cbs, cb_=cb_: e.tensor_tensor(out=cb_[:, :], in0=cbs, in1=cx.tri_f[:, :], op=ALU.mult),
                      reads=[("ps", 5), "consts"], writes=[("cbm", rb)])

            def s1b(g):
                lb = (g % 2) * 2
                rb = g % 2
                rl = rhsL[rb]
                for hf in range(2):
                    cx.mm(cx.bank(lb + hf), cx.stri_f[:, :], rl[:, hf * 512:(hf + 1) * 512], True, True,
                          reads=[("rhsL", rb), "consts"], writes=[("ps", lb + hf)])
                E_ = Eb[rb]
                cx.act(E_[:, :], cx.bank(lb, 2), AF.Exp, reads=[("ps", lb), ("ps", lb + 1)], writes=[("E", rb)])

            def s2(g):
                rb = g % 2
                E_, M_, cb_ = Eb[rb], Mb[rb], cbm[rb]
                p.add("dve", lambda e, E_=E_, M_=M_, cb_=cb_: e.tensor_tensor(
                    out=A3(M_, 8, 128), in0=A3(E_, 8, 128), in1=bc(cb_[:, :], [128, 8, 128], 1), op=ALU.mult),
                    reads=[("E", rb), ("cbm", rb)], writes=[("M", rb)])
                M3 = A3(M_, 8, 128)
                for e_ in range(8):
                    hh = 8 * g + e_
                    cx.mm(cx.bank(6)[:, e_ * 64:(e_ + 1) * 64], M3[:, e_, :], xdt[:, hh * 64:(hh + 1) * 64], True, True,
                          reads=[("M", rb), "xdt"], writes=[("ps", 6)])
                cx.mm(cx.bank(7), bc3[:, 8 + g, :], Sbf[:, g * 512:(g + 1) * 512], True, True,
                      reads=[("bct", b), "Sbf"], writes=[("ps", 7)])
                tg = tmpg[rb]
                p.add("dve", lambda e, g=g, tg=tg: e.tensor_tensor(
                    out=A3(tg, 8, 64), in0=A3(xtok[:, g * 512:(g + 1) * 512], 8, 64),
                    in1=bc(cx.dskb[:, 8 * g:8 * g + 8], [128, 8, 64], 2), op=ALU.mult),
                    reads=[("xtok", g), "dskb"], writes=[("tmpg", rb)])
                yg = y[:, g * 512:(g + 1) * 512]
                p.add("dve", lambda e, g=g, yg=yg: e.tensor_tensor(
                    out=A3(yg, 8, 64), in0=A3(cx.bank(7), 8, 64), in1=bc(din[:, 8 * g:8 * g + 8], [128, 8, 64], 2),
                    op=ALU.mult), reads=[("ps", 7), "din"], writes=[("y", g)])
                p.add("dve", lambda e, yg=yg: e.tensor_tensor(out=yg, in0=yg, in1=cx.bank(6), op=ALU.add),
                      reads=[("ps", 6), ("y", g)], writes=[("y", g)])
                p.add("dve", lambda e, yg=yg, tg=tg: e.tensor_tensor(out=yg, in0=yg, in1=tg[:, :], op=ALU.add),
                      reads=[("tmpg", rb), ("y", g)], writes=[("y", g)])

            s1(0)
            s1b(0)
            for g in range(8):
                if g + 1 < 8:
                    s1(g + 1)
                s2(g)
                if g + 1 < 8:
                    s1b(g + 1)
        if c < NCH - 1:
            for g in range(8):
                sb_ = 4 if False else (2 + g % 2) if not own else 7
                sb_ = rrn(cx, "stb", 2) + 2 if not own else 7
                cx.mm(cx.bank(sb_), Btok[:, g * 128:(g + 1) * 128], xw[:, g * 512:(g + 1) * 512], True, True,
                      reads=["Btok", "xw"], writes=[("ps", sb_)])
                Sg = S[:, g * 512:(g + 1) * 512]
                p.add("dve", lambda e, g=g, Sg=Sg: e.tensor_tensor(
                    out=A3(Sg, 8, 64), in0=A3(Sg, 8, 64), in1=bc(cdec[:, 8 * g:8 * g + 8], [128, 8, 64], 2), op=ALU.mult),
                    reads=["S", "cdec", "Sbf"], writes=["S"])
                p.add("dve", lambda e, Sg=Sg, sb_=sb_: e.tensor_tensor(out=Sg, in0=Sg, in1=cx.bank(sb_), op=ALU.add),
                      reads=["S", ("ps", sb_)], writes=["S"])
            if c == NCC - 1:
                p.add("dve", lambda e: e.tensor_scalar(out=S[:, :], in0=S[:, :], scalar1=cx.flags[:, 1:2], scalar2=None,
                                                        op0=ALU.mult), reads=["S", "flags"], writes=["S"])
        if c in cx.dbg_chunks:
            cx.dbg("da%d" % c, da[:, :], ["da"])
            cx.dbg("xtok%d" % c, xtok[:, :], [("xtok", g) for g in range(8)])
            cx.dbg("y%d" % c, y[:, :], [("y", g) for g in range(8)])
        if own:
            cx.act(zb[:, :], zb[:, :], AF.Silu, reads=["z"], writes=["z"])
            ykeys = [("y", g) for g in range(8)]
            p.add("dve", lambda e: e.tensor_tensor(out=y[:, :], in0=y[:, :], in1=zb[:, :], op=ALU.mult),
                  reads=ykeys + ["z"], writes=ykeys)
            p.add("dve", lambda e: e.memset(ssq[:, :], 0.0), writes=["ssq"])
            for g in range(8):
                cx.act(junk[:, :], y[:, g * 512:(g + 1) * 512], AF.Square, reads=[("y", g), "ssq"], writes=["junk", "ssq"],
                       accum_out=ssq[:, g:g + 1])
            p.add("dve", lambda e: e.tensor_scalar(out=rs8[:, :], in0=ssq[:, :], scalar1=1.0 / 512.0, scalar2=EPS,
                                                    op0=ALU.mult, op1=ALU.add), reads=["ssq"], writes=["rs8"])
            cx.act(rs8[:, :], rs8[:, :], AF.Sqrt, reads=["rs8"], writes=["rs8"])
            p.add("dve", lambda e: e.reciprocal(out=rs8[:, :], in_=rs8[:, :]), reads=["rs8"], writes=["rs8"])
            for g in range(8):
                p.add("dve", lambda e, g=g: e.scalar_tensor_tensor(
                    out=yn[:, g * 512:(g + 1) * 512], in0=y[:, g * 512:(g + 1) * 512], scalar=rs8[:, g:g + 1],
                    in1=ngb[:, g * 512:(g + 1) * 512], op0=ALU.mult, op1=ALU.mult),
                    reads=[("y", g), "rs8", "ngb"], writes=["yn"])
            for q4 in range(4):
                tb_ = rrn(cx, "xtb", 4)
                pv = cx.bank(tb_).bitcast(BF16)
                for i in range(8):
                    cc = q4 * 8 + i
                    p.add("pe", lambda e, pv=pv, i=i, cc=cc: e.transpose(pv[:, i * 128:(i + 1) * 128],
                                                                         yn[:, cc * 128:(cc + 1) * 128], cx.ident_bf[:, :]),
                          reads=["yn", "consts"], writes=[("ps", tb_)])
                if q4 % 2 == 0:
                    cx.act(oTs[:, q4 * 1024:(q4 + 1) * 1024], pv[:, 0:1024], AF.Copy, reads=[("ps", tb_)], writes=["oTs"])
                else:
                    p.add("dve", lambda e, pv=pv, q4=q4: e.tensor_copy(out=oTs[:, q4 * 1024:(q4 + 1) * 1024], in_=pv[:, 0:1024]),
                          reads=[("ps", tb_)], writes=["oTs"])
            cx.store(cx.ossmT[:, tl0:tl0 + 128].rearrange("(c p) t -> p c t", p=128), A3(oTs, 32, 128), reads=["oTs"],
                     writes=[("scr", "ossm")], key="oTs", eng="pool")


def merge(cx, W, ntok_ctx, ntok_own, T=1024):
    p = cx.p
    NH = T // 512
    rT = cx.alloc(48 * T, BF16).rearrange("p (k t) -> p k t", k=48)
    mT = cx.alloc(KC * T, BF16).rearrange("p (k t) -> p k t", k=KC)
    cx.wbuf = [cx.alloc(48 * 128, BF16) for _ in range(2)]
    cx.xbuf = [cx.alloc(T, F32) for _ in range(3)]
    cx.sqbuf = [cx.alloc(T, BF16) for _ in range(2)]
    gb = [cx.alloc(2 * T, F32) for _ in range(2)]
    cx.ssbank = 8 - NH
    for t0 in range(0, ntok_own, T):
        cx.load(rT[:, 0:16, :], cx.oattT[:, t0:t0 + T].rearrange("(c p) t -> p c t", p=128), reads=[],
                writes=["rT"], key="rTa")
        cx.load(rT[:, 16:48, :], cx.ossmT[:, t0:t0 + T].rearrange("(c p) t -> p c t", p=128), reads=[],
                writes=["rT"], key="rTb")
        for dc in range(KC):
            wb = rrn(cx, "wb", 2)
            w = cx.wbuf[wb]
            if t0 > 0:
                cx.load(w[:, 0:2048], cx.msc[0][dc], reads=[("msc", "a", dc)], writes=[("wb", wb, 0)], key="wb%d" % wb, eng="pool")
                cx.load(w[:, 2048:6144], cx.msc[1][dc], reads=[("msc", "s", dc)], writes=[("wb", wb, 1)], key="wb%d" % wb, eng="pool")
            else:
                wload(cx, w[:, 0:2048], W["wba"][dc], [], [("wb", wb, 0)], "wb%d" % wb)
                wload(cx, w[:, 2048:6144], W["wbs"][dc], [], [("wb", wb, 1)], "wb%d" % wb)
                if ntok_own > T:
                    cx.store(cx.msc[0][dc], w[:, 0:2048], reads=[("wb", wb, 0)], writes=[("msc", "a", dc)], key="wbs%d" % wb, eng="act")
                    cx.store(cx.msc[1][dc], w[:, 2048:6144], reads=[("wb", wb, 1)], writes=[("msc", "s", dc)], key="wbs%d" % wb, eng="act")
            g_ = rrn(cx, "gb", 2)
            gt = gb[g_]
            cx.load(gt[:, 0:T], cx.gT[dc * 128:(dc + 1) * 128, t0:t0 + T], reads=[], writes=[("gb", g_)], key="gb%d" % g_)
            cx.load(gt[:, T:2 * T], cx.gT[D + dc * 128:D + (dc + 1) * 128, t0:t0 + T], reads=[], writes=[("gb", g_)],
                    key="gb%d" % g_)
            pa = rrn(cx, "mpb", 2) * 2 * NH
            psa = [("ps", pa + h) for h in range(NH)]
            pss = [("ps", pa + NH + h) for h in range(NH)]
            for kc in range(16):
                for h in range(NH):
                    cx.mm(cx.bank(pa + h), w[:, kc * 128:(kc + 1) * 128], rT[:, kc, h * 512:(h + 1) * 512], kc == 0, kc == 15,
                          reads=[("wb", wb, 0), "rT"], writes=[("ps", pa + h)])
            for kc in range(32):
                for h in range(NH):
                    cx.mm(cx.bank(pa + NH + h), w[:, 2048 + kc * 128:2048 + (kc + 1) * 128], rT[:, 16 + kc, h * 512:(h + 1) * 512],
                          kc == 0, kc == 31, reads=[("wb", wb, 1), "rT"], writes=[("ps", pa + NH + h)])
            p.add("dve", lambda e, gt=gt, pa=pa: e.tensor_tensor(out=gt[:, 0:T], in0=gt[:, 0:T], in1=cx.bank(pa, NH), op=ALU.mult),
                  reads=[("gb", g_)] + psa, writes=[("gb", g_)])
            p.add("dve", lambda e, gt=gt, pa=pa: e.tensor_tensor(out=gt[:, T:2 * T], in0=gt[:, T:2 * T],
                                                                  in1=cx.bank(pa + NH, NH), op=ALU.mult),
                  reads=[("gb", g_)] + pss, writes=[("gb", g_)])
            p.add("dve", lambda e, gt=gt, dc=dc: e.tensor_tensor(out=mT[:, dc, :], in0=gt[:, 0:T], in1=gt[:, T:2 * T], op=ALU.add),
                  reads=[("gb", g_)], writes=[("mT", dc)])
        for dc in range(KC):
            wb = rrn(cx, "wb", 2)
            w = cx.wbuf[wb]
            if t0 > 0:
                cx.load(w[:, 0:2048], cx.msc[2][dc], reads=[("msc", "o", dc)], writes=[("wb", wb, 0), ("wb", wb, 1)], key="wb%d" % wb, eng="pool")
            else:
                wload(cx, w[:, 0:2048], W["wo"][dc], [], [("wb", wb, 0), ("wb", wb, 1)], "wb%d" % wb)
                if ntok_own > T:
                    cx.store(cx.msc[2][dc], w[:, 0:2048], reads=[("wb", wb, 0), ("wb", wb, 1)], writes=[("msc", "o", dc)], key="wbs%d" % wb, eng="act")
            pa = rrn(cx, "mob", 2) * NH
            pk = [("ps", pa + h) for h in range(NH)]
            for kc in range(16):
                for h in range(NH):
                    cx.mm(cx.bank(pa + h), w[:, kc * 128:(kc + 1) * 128], mT[:, kc, h * 512:(h + 1) * 512], kc == 0, kc == 15,
                          reads=[("wb", wb, 0), ("wb", wb, 1), ("mT", kc)], writes=[("ps", pa + h)])
            b = rrn(cx, "xb", 3)
            ft = cx.xbuf[b]
            cx.act(ft[:, 0:T], cx.bank(pa, NH), AF.Copy, reads=pk, writes=[("xb", b)])
            sb_ = rrn(cx, "sq", 2)
            sq = cx.sqbuf[sb_]
            cx.act(sq[:, 0:T], cx.bank(pa, NH), AF.Square, reads=pk, writes=[("sq", sb_)])
            cx.store(cx.fscr[dc * 128:(dc + 1) * 128, 0:T], ft[:, 0:T], reads=[("xb", b)], writes=[("fs", "f", dc)],
                     key="xb%d" % b)
            for h in range(NH):
                cx.mm(cx.bank(cx.ssbank + h), cx.ones[:, :], sq[:, h * 512:(h + 1) * 512], dc == 0, dc == KC - 1,
                      reads=[("sq", sb_), "ones"], writes=[("ps", cx.ssbank + h)])
        rstd_from_ss(cx, T, D)
        combine_residual(cx, cx.fscr, cx.h1, cx.h2, t0, T, cx.gv("mix_post_g"), "fs", "h1o", "h2", half=False,
                         xoff=ntok_ctx)


VEC_NAMES = ["ffn1_pre_g", "ffn1_post_g", "mix_pre_g", "mix_post_g", "ffn2_pre_g", "ffn2_post_g"]
NVEC = len(VEC_NAMES) * KC
W_SHAPES = {
    "w1g": [FC, 128, KC * 128], "w1u": [FC, 128, KC * 128], "w1d": [KC, 128, FC * 128],
    "w2g": [FC, 128, KC * 128], "w2u": [FC, 128, KC * 128], "w2d": [KC, 128, FC * 128],
    "wq": [16, 128, 2048], "wk": [16, 128, 2048], "wxbc": [48, 128, 2048], "wdt": [1, 128, KC * 64],
    "wg": [32, 128, 2048], "wv": [4, 128, KC * 512], "wz": [8, 128, KC * 512],
    "wba": [16, 128, 2048], "wbs": [16, 128, 4096], "wo": [16, 128, 2048],
}
SM_CONVW = 0
SM_CONVB = 192
SM_SSMV = 240
SM_LAM = 243
SM_SUBG = 247
SM_FLAGS = 249
SM_DSK = 251
NSM = 315
ALL_STAGES = ("ffn1", "inproj", "attn", "conv", "ssd", "merge", "ffn2")


def build(T=1024, stages=ALL_STAGES, ntok_ctx=HALF, ntok_own=HALF, debug=(), ext_in=(), dbg_chunks=()):
    nc = bass.Bass("TRN2", target_bir_lowering=False)
    NT = ntok_ctx + ntok_own

    def din(name, shape, dt=F32):
        return nc.dram_tensor(name, shape, dt, kind="ExternalInput").ap()

    def scr(name, shape, dt=F32):
        kind = "ExternalOutput" if name in debug else ("ExternalInput" if name in ext_in else "Internal")
        return nc.dram_tensor(name, shape, dt, kind=kind).ap()

    xT = din("xT", [D, NT])
    vecs_d = din("vecs", [128, NVEC])
    sm_d = din("smalls", [128, NSM])
    consts_d = din("consts", [128, 4 * 128])
    ngb_d = din("ngb", [128, DIN])
    W = {k: din(k, v) for k, v in W_SHAPES.items()}
    outT = nc.dram_tensor("outT", [D, ntok_own], F32, kind="ExternalOutput").ap()
    with ExitStack() as stack:
        cx = Ctx(nc, stack)
        p = cx.p
        cx.ngb_d = ngb_d
        cx.dbg_chunks = dbg_chunks

        def dbg(name, ap, keys):
            t = nc.dram_tensor("dbg_" + name, list(ap.shape), ap.dtype, kind="ExternalOutput").ap()
            cx.store(t, ap, reads=keys, writes=[("dbg", name)], key="dbg_" + name)
        cx.dbg = dbg
        cx.h1 = scr("h1", [D, NT])
        cx.h2 = scr("h2", [D, ntok_own])
        cx.fscr = scr("fscr", [D, T])
        cx.qT = scr("qT", [NQK, ntok_own], BF16)
        cx.kT = scr("kT", [NQK, NT], BF16)
        cx.V = scr("V", [NT, NV], BF16)
        cx.Z = scr("Z", [ntok_own, DIN])
        cx.xbcT = scr("xbcT", [6144, 4 + NT])
        cx.dtT = scr("dtT", [64, NT])
        cx.gT = scr("gT", [2 * D, ntok_own])
        cx.xcT = scr("xcT", [DIN, NT])
        cx.bcT = scr("bcT", [2 * NBC, NT], BF16)
        cx.oattT = scr("oattT", [NV, ntok_own], BF16)
        cx.ossmT = scr("ossmT", [DIN, ntok_own], BF16)
        cx.msc = (scr("msc_a", [16, 128, 2048], BF16), scr("msc_s", [16, 128, 4096], BF16), scr("msc_o", [16, 128, 2048], BF16))
        wsc = (scr("wsc_g", [FC, 128, KC * 128], BF16), scr("wsc_u", [FC, 128, KC * 128], BF16),
               scr("wsc_d", [KC, 128, FC * 128], BF16))
        cx.vecs = cx.sb("vecsb", [128, NVEC], F32)
        sm = cx.sb("smalls_sb", [128, NSM], F32)
        cf = cx.sb("consts_sb", [128, 4 * 128], F32)
        cb16 = cx.sb("consts_bf", [128, 3 * 128], BF16)
        cx.rstd = cx.sb("rstd", [128, 1024], F32)
        cx.neglam = cx.sb("neglam", [128, 4], F32)
        cx.subg = cx.sb("subg", [128, 2], F32)
        zero = cx.sb("zero", [128, 48 * 4], F32)

        cx.arena = cx.sb("arena", [128, 98 * 1024], BF16)
        cx.ident_f, cx.ones_f, cx.tri_f, cx.stri_f = (cf[:, i * 128:(i + 1) * 128] for i in range(4))
        cx.ident_bf, cx.ones, cx.tri_bf = (cb16[:, i * 128:(i + 1) * 128] for i in range(3))
        cx.convw = sm[:, SM_CONVW:SM_CONVW + 192]
        cx.convb = sm[:, SM_CONVB:SM_CONVB + 48]
        cx.ssmv = sm[:, SM_SSMV:SM_SSMV + 3]
        cx.flags = sm[:, SM_FLAGS:SM_FLAGS + 2]
        cx.dskb = sm[:, SM_DSK:SM_DSK + 64]
        cx.negb = cx.sb("negb", [128, 48], F32)
        p.add("dve", lambda e: e.tensor_scalar(out=cx.negb[:, :], in0=sm[:, SM_CONVB:SM_CONVB + 48], scalar1=-1.0,
                                                scalar2=None, op0=ALU.mult), reads=["convw"], writes=["negb"])
        cx.gv = lambda name: cx.vecs[:, VEC_NAMES.index(name) * KC:(VEC_NAMES.index(name) + 1) * KC]
        cx.load(cx.vecs[:, :], vecs_d, reads=[], writes=["vecs"], key="vecs")
        cx.load(sm[:, :], sm_d, reads=[], writes=["convw", "ssmv", "flags", "dskb", "smraw"], key="sm")
        cx.load(cf[:, :], consts_d, reads=[], writes=["constsf"], key="cf")
        p.add("dve", lambda e: e.tensor_copy(out=cb16[:, :], in_=cf[:, 0:384]), reads=["constsf"], writes=["consts", "ones"])
        p.add("dve", lambda e: e.memset(zero[:, :], 0.0), writes=["zero"])
        cx.store(cx.xbcT[:, 0:4].rearrange("(c p) t -> p c t", p=128), zero[:, :].rearrange("p (c t) -> p c t", c=48),
                 reads=["zero"], writes=[("scr", "pad")], key="zero")
        cx.act(sm[0:64, SM_SSMV + 2:SM_SSMV + 3], sm[0:64, SM_SSMV + 1:SM_SSMV + 2], AF.Exp, reads=["ssmv"], writes=["ssmv"])
        p.add("dve", lambda e: e.tensor_scalar(out=sm[0:64, SM_SSMV + 2:SM_SSMV + 3], in0=sm[0:64, SM_SSMV + 2:SM_SSMV + 3],
                                                scalar1=-1.0, scalar2=None, op0=ALU.mult), reads=["ssmv"], writes=["ssmv"])
        lam = sm[:, SM_LAM:SM_LAM + 4]
        nl = cx.neglam
        p.add("dve", lambda e: e.tensor_tensor(out=nl[:, 0:1], in0=lam[:, 0:1], in1=lam[:, 1:2], op=ALU.mult),
              reads=["smraw"], writes=["nl"])
        p.add("dve", lambda e: e.tensor_tensor(out=nl[:, 1:2], in0=lam[:, 2:3], in1=lam[:, 3:4], op=ALU.mult),
              reads=["smraw", "nl"], writes=["nl"])
        cx.mm(cx.bank(0)[:, 0:2], cx.ones_f, nl[:, 0:2], True, True, reads=["nl", "constsf"], writes=[("ps", 0)])
        cx.act(nl[:, 2:4], cx.bank(0)[:, 0:2], AF.Exp, reads=[("ps", 0)], writes=["nl2"])
        p.add("dve", lambda e: e.scalar_tensor_tensor(out=nl[:, 0:1], in0=nl[:, 3:4], scalar=-0.2, in1=nl[:, 2:3],
                                                       op0=ALU.add, op1=ALU.subtract), reads=["nl2", "nl"], writes=["neglam"])
        p.add("dve", lambda e: e.tensor_scalar(out=cx.subg[:, :], in0=sm[:, SM_SUBG:SM_SUBG + 2], scalar1=0.8, scalar2=None,
                                                op0=ALU.mult), reads=["smraw"], writes=["subg"])
        cx.amark = 0

        if "ffn1" in stages:
            cx.phase()
            layout_ffn(cx, T)
            ffn(cx, xT, cx.h1, NT, T, W["w1g"], W["w1u"], W["w1d"], cx.gv("ffn1_pre_g"), cx.gv("ffn1_post_g"), "x", "h1",
                wsc=wsc)
        if "inproj" in stages or "conv" in stages:
            cx.phase()
            side = ConvSide(cx, ntok_ctx, ntok_own, T) if "conv" in stages else None
            if "inproj" in stages:
                inproj(cx, W, ntok_ctx, ntok_own, T, side=side)
            if side is not None:
                side.drain()
        if "attn" in stages:
            cx.phase()
            attention(cx, ntok_ctx, ntok_own)
        if "ssd" in stages:
            cx.phase()
            ssd(cx, ntok_ctx, ntok_own)
        if "merge" in stages:
            cx.phase()
            merge(cx, W, ntok_ctx, ntok_own)
        if "ffn2" in stages:
            cx.phase()
            layout_ffn(cx, T)
            ffn(cx, cx.h2, outT, ntok_own, T, W["w2g"], W["w2u"], W["w2d"], cx.gv("ffn2_pre_g"), cx.gv("ffn2_post_g"),
                "h2", "out", wsc=wsc)
        cx.phase()
        p.finalize(stack)
    return nc


def tile_w(W):
    K, N = W.shape
    return np.ascontiguousarray(
        W.reshape(K // 128, 128, N // 128, 128).transpose(2, 1, 0, 3).reshape(N // 128, 128, K))


def tile_w_rows(W, rows):
    K, N = W.shape
    return np.ascontiguousarray(
        W.reshape(K // 128, 128, N // rows, rows).transpose(2, 1, 0, 3).reshape(N // rows, 128, (K // 128) * rows))


def col_vec(v):
    return np.ascontiguousarray(v.reshape(-1, 128).T)


def host_weights(inp):
    w_in = inp["w_in"][0]
    o = 0
    sl = {}
    for name, n in (("q", 2048), ("k", 2048), ("v", 2048), ("z", 4096), ("xbc", 6144), ("dt", 64), ("g", 4096)):
        sl[name] = w_in[:, o:o + n]
        o += n
    Wd = {
        "w1g": tile_w(inp["ffn1_w_gate"][0]), "w1u": tile_w(inp["ffn1_w_up"][0]), "w1d": tile_w(inp["ffn1_w_down"][0]),
        "w2g": tile_w(inp["ffn2_w_gate"][0]), "w2u": tile_w(inp["ffn2_w_up"][0]), "w2d": tile_w(inp["ffn2_w_down"][0]),
        "wq": tile_w(sl["q"]), "wk": tile_w(sl["k"]), "wxbc": tile_w(sl["xbc"]), "wdt": tile_w_rows(sl["dt"], 64),
        "wg": tile_w(sl["g"]), "wv": tile_w_rows(sl["v"], 512), "wz": tile_w_rows(sl["z"], 512),
        "wba": tile_w(inp["w_branch_att"][0]), "wbs": tile_w(inp["w_branch_ssm"][0]), "wo": tile_w(inp["w_out"][0]),
    }
    vecs = np.concatenate([col_vec(inp[n][0]) for n in VEC_NAMES], axis=1).astype(np.float32)
    sm = np.zeros((128, NSM), np.float32)
    cw = inp["ssm_conv_w"][0]
    sm[:, SM_CONVW:SM_CONVW + 192] = cw.reshape(4, 48, 128).transpose(2, 1, 0).reshape(128, 192)
    sm[:, SM_CONVB:SM_CONVB + 48] = col_vec(inp["ssm_conv_b"][0])
    sm[0:64, SM_SSMV] = inp["ssm_dt_bias"][0]
    sm[0:64, SM_SSMV + 1] = inp["ssm_a_log"][0]
    for i, n in enumerate(("att_lambda_q1", "att_lambda_k1", "att_lambda_q2", "att_lambda_k2")):
        sm[:, SM_LAM + i] = inp[n][0]
    sm[:, SM_SUBG:SM_SUBG + 2] = col_vec(inp["att_subln_g"][0])
    sm[:, SM_DSK:SM_DSK + 64] = inp["ssm_d"][0][None, :]
    ngb = np.ascontiguousarray(np.broadcast_to(inp["ssm_norm_g"][0][None, :], (128, DIN))).astype(np.float32)
    idx = np.arange(128)
    consts = np.concatenate([
        np.eye(128), np.ones((128, 128)),
        (idx[:, None] <= idx[None, :]).astype(np.float64),
        (idx[:, None] > idx[None, :]).astype(np.float64),
    ], axis=1).astype(np.float32)
    return Wd, vecs, sm, ngb, consts


def core_maps(inp, ntok_ctx=HALF, ntok_own=HALF, cores=range(8)):
    Wd, vecs, sm, ngb, consts = host_weights(inp)
    maps = []
    for c in cores:
        b, r = c // 2, c % 2
        x = inp["x"][b]
        xc = x[0:ntok_ctx]
        xo = x[r * ntok_own + (ntok_ctx if False else 0):][:ntok_own] if r == 0 else x[ntok_ctx:ntok_ctx + ntok_own]
        xT = np.ascontiguousarray(np.concatenate([xc, xo], axis=0).T)
        smc = sm.copy()
        smc[:, SM_FLAGS] = 0.0 if r == 1 else -30000.0
        smc[:, SM_FLAGS + 1] = 1.0 if r == 1 else 0.0
        m = {"xT": xT, "vecs": vecs, "smalls": smc, "consts": consts, "ngb": ngb}
        m.update(Wd)
        maps.append(m)
    return maps


_NC_CACHE = {}


def kernel(**inputs):
    inp = {k: np.asarray(v) for k, v in inputs.items()}
    if "nc" not in _NC_CACHE:
        _NC_CACHE["nc"] = build()
    nc = _NC_CACHE["nc"]
    maps = core_maps(inp)
    res = run_bass_kernel_spmd(nc, maps, core_ids=list(range(8)))
    out = np.empty((4, SEQ, D), np.float32)
    for c in range(8):
        b, r = c // 2, c % 2
        out[b, r * HALF:(r + 1) * HALF, :] = res.results[c]["outT"].T
    return out
```
